# Optimizing a Trainium2 kernel written in Bass

```python
import jax, jax.numpy as jnp
from jax import lax
import numpy as np

D_MODEL = 2048
BATCH = 4
SEQ = 4096
DEPTH = 2
DEC_BATCH = 8
DEC_SEQ = 32
PAST_LEN = 2048

CHUNK = 64
N_EVEN = (DEPTH + 1) // 2
N_ODD = DEPTH // 2
H_A = 4
DK_A = D_MODEL // 16
DV_A = D_MODEL // 8
GATE_RANK = 16
GATE_TEMP = 16.0
H_B = 8
DH_B = D_MODEL // 16
N_PREV_CHUNKS = 8
BAND_PAST = N_PREV_CHUNKS * CHUNK
BAND = BAND_PAST + CHUNK
REL_CLIP = 128
CHUNK_C = 128
DC = D_MODEL
G_C = 8
D_FF = ((8 * D_MODEL // 3 + 127) // 128) * 128
CONV_W = 3
ALPHA = (2 * DEPTH) ** 0.25
BETA = (8 * DEPTH) ** -0.25
LN_EPS = 1e-5
NEG_INF = -1e30
IN_EVEN = 2 * H_A * DK_A + 2 * H_A * DV_A + GATE_RANK + 3 * H_B * DH_B
MIX_EVEN = H_A * DV_A + H_B * DH_B

kernel_name = 'hybrid_streaming_gla_band_gmlp_step'


def _layer_norm(x, g, b):
    xf = x.astype(jnp.float32)
    mu = jnp.mean(xf, -1, keepdims=True)
    xc = xf - mu
    var = jnp.mean(xc * xc, -1, keepdims=True)
    y = xc * lax.rsqrt(var + LN_EPS)
    return (y * g.astype(jnp.float32) + b.astype(jnp.float32)).astype(x.dtype)


def _gla_chunked(q, k, v, logg, s0):
    B, T, H, K = q.shape
    V = v.shape[-1]
    L = min(T, CHUNK)
    n = T // L

    def to_blocks(a):
        return jnp.moveaxis(a.reshape(B, n, L, H, a.shape[-1]), 1, 0)

    causal = jnp.tril(jnp.ones((L, L), dtype=bool))

    def step(S, xs):
        qc, kc, vc, gc = xs
        b = jnp.cumsum(gc, axis=1)
        q_dec = qc * jnp.exp(b)
        k_dec = kc * jnp.exp(-b)
        a = jnp.where(causal, jnp.einsum('bthk,bshk->bhts', q_dec, k_dec), 0.0)
        o = jnp.einsum('bthk,bhkv->bthv', q_dec, S) + jnp.einsum('bhts,bshv->bthv', a, vc)
        b_last = b[:, -1]
        k_upd = kc * jnp.exp(b_last[:, None] - b)
        S = S * jnp.exp(b_last)[..., None] + jnp.einsum('bshk,bshv->bhkv', k_upd, vc)
        return S, o

    s_fin, o = lax.scan(step, s0, (to_blocks(q), to_blocks(k), to_blocks(v), to_blocks(logg)))
    return jnp.moveaxis(o, 0, 1).reshape(B, T, H, V), s_fin


def _rel_bias(table, rel):
    return table[:, jnp.clip(rel, -REL_CLIP, REL_CLIP) + REL_CLIP].astype(jnp.float32)


def _band_attn_prompt(q, k, v, table):
    B, T, H, Dh = q.shape
    n = T // CHUNK
    pad = ((0, 0), (BAND_PAST, 0), (0, 0), (0, 0))
    kp = jnp.pad(k, pad)
    vp = jnp.pad(v, pad)
    r = jnp.arange(CHUNK)
    j = jnp.arange(BAND)
    bias = _rel_bias(table, r[:, None] + BAND_PAST - j[None, :])
    scale = DH_B ** -0.5

    def one_chunk(c):
        start = c * CHUNK
        qb = lax.dynamic_slice_in_dim(q, start, CHUNK, axis=1)
        kb = lax.dynamic_slice_in_dim(kp, start, BAND, axis=1)
        vb = lax.dynamic_slice_in_dim(vp, start, BAND, axis=1)
        s = jnp.einsum('bqhd,bkhd->bhqk', qb, kb).astype(jnp.float32) * scale + bias
        valid = (start - BAND_PAST + j) >= 0
        s = jnp.where(valid[None, None, None, :], s, NEG_INF)
        p = jax.nn.softmax(s, axis=-1)
        return jnp.einsum('bhqk,bkhd->bqhd', p.astype(v.dtype), vb)

    o = lax.map(one_chunk, jnp.arange(n))
    return jnp.moveaxis(o, 0, 1).reshape(B, T, H, Dh)


def _band_attn_step(q, k, v, ck, cv, table):
    T = q.shape[1]
    L = ck.shape[1]
    kk = jnp.concatenate([ck.astype(k.dtype), k], axis=1)
    vv = jnp.concatenate([cv.astype(v.dtype), v], axis=1)
    qpos = PAST_LEN + jnp.arange(T)
    kpos = jnp.concatenate([PAST_LEN - L + jnp.arange(L), qpos])
    bias = _rel_bias(table, qpos[:, None] - kpos[None, :])
    s = jnp.einsum('bqhd,bkhd->bhqk', q, kk).astype(jnp.float32) * (DH_B ** -0.5) + bias
    p = jax.nn.softmax(s, axis=-1)
    return jnp.einsum('bhqk,bkhd->bqhd', p.astype(v.dtype), vv)


def _even_mixer(x, P, e, state):
    B, T, _ = x.shape
    f32 = jnp.float32
    sizes = (H_A * DK_A, H_A * DK_A, H_A * DV_A, GATE_RANK, H_A * DV_A,
             H_B * DH_B, H_B * DH_B, H_B * DH_B)
    split_at = [int(s) for s in np.cumsum(sizes)[:-1]]
    proj = x @ P['w_in_even'][e]
    q_a, k_a, v_a, g_low, og, q_b, k_b, v_b = jnp.split(proj, split_at, axis=-1)
    q_a = (q_a.reshape(B, T, H_A, DK_A) * (DK_A ** -0.5)).astype(f32)
    k_a = k_a.reshape(B, T, H_A, DK_A).astype(f32)
    v_a = v_a.reshape(B, T, H_A, DV_A).astype(f32)
    logg = jax.nn.log_sigmoid((g_low @ P['w_gate_up'][e] + P['b_gate'][e]).astype(f32)) / GATE_TEMP
    logg = logg.reshape(B, T, H_A, DK_A)
    s0 = jnp.zeros((B, H_A, DK_A, DV_A), f32) if state is None else state[0].astype(f32)
    o_a, s_new = _gla_chunked(q_a, k_a, v_a, logg, s0)
    o_a = o_a * lax.rsqrt(jnp.mean(o_a * o_a, -1, keepdims=True) + LN_EPS) * P['gla_norm_g'][e].astype(f32)
    o_a = (o_a.reshape(B, T, H_A * DV_A) * jax.nn.silu(og.astype(f32))).astype(x.dtype)
    q_b = q_b.reshape(B, T, H_B, DH_B)
    k_b = k_b.reshape(B, T, H_B, DH_B)
    v_b = v_b.reshape(B, T, H_B, DH_B)
    if state is None:
        o_b = _band_attn_prompt(q_b, k_b, v_b, P['rel_bias'][e])
        keep = min(BAND_PAST, T)
        k_rows, v_rows = k_b[:, T - keep:], v_b[:, T - keep:]
    else:
        o_b = _band_attn_step(q_b, k_b, v_b, state[1], state[2], P['rel_bias'][e])
        k_rows, v_rows = k_b, v_b
    mixed = jnp.concatenate([o_a, o_b.reshape(B, T, H_B * DH_B)], axis=-1)
    return mixed @ P['w_out_even'][e], k_rows, v_rows, s_new.astype(x.dtype)


def _odd_mixer(x, P, o):
    B, T, _ = x.shape
    h = jax.nn.gelu(x @ P['w_in_odd'][o])
    u, v = jnp.split(h, 2, axis=-1)
    v = _layer_norm(v, P['ln_v_g'][o], P['ln_v_b'][o])
    L = min(T, CHUNK_C)
    n = T // L
    causal = jnp.tril(jnp.ones((L, L), dtype=bool))
    wm = jnp.where(causal, P['w_spatial'][o][:, :L, :L], 0.0)
    vc = v.reshape(B, n, L, G_C, DC // G_C)
    bias = jnp.transpose(P['b_spatial'][o][:, :L])[None, None, :, :, None]
    sv = jnp.einsum('gts,bnsgc->bntgc', wm.astype(v.dtype), vc) + bias
    y = (u * sv.reshape(B, T, DC)) @ P['w_out_odd'][o]
    return y, v


def _conv_ffn(x, P, i, conv_state):
    B, T, _ = x.shape
    u = x @ P['ffn_w1'][i]
    z = x @ P['ffn_w2'][i]
    hist = jnp.zeros((B, CONV_W - 1, D_FF), u.dtype) if conv_state is None else conv_state.astype(u.dtype)
    upad = jnp.concatenate([hist, u], axis=1)
    cw = P['ffn_conv_w'][i]
    c = upad[:, 0:T] * cw[0]
    for tap in range(1, CONV_W):
        c = c + upad[:, tap:tap + T] * cw[tap]
    hdn = jax.nn.gelu(c + P['ffn_conv_b'][i]) * z
    return hdn @ P['ffn_w3'][i], upad[:, -(CONV_W - 1):]


def _trunk(x, P, state):
    ks, vs, gs, cs, ms = [], [], [], [], []
    for i in range(DEPTH):
        if i % 2 == 0:
            e = i // 2
            st = None if state is None else (state[2][e], state[0][e], state[1][e])
            h, kr, vr, s_new = _even_mixer(x, P, e, st)
            ks.append(kr)
            vs.append(vr)
            gs.append(s_new)
        else:
            h, v_rows = _odd_mixer(x, P, i // 2)
            ms.append(v_rows)
        x = _layer_norm(ALPHA * x + h, P['ln1_g'][i], P['ln1_b'][i])
        f, c_new = _conv_ffn(x, P, i, None if state is None else state[3][i])
        cs.append(c_new)
        x = _layer_norm(ALPHA * x + f, P['ln2_g'][i], P['ln2_b'][i])
    return x, jnp.stack(ks), jnp.stack(vs), jnp.stack(gs), jnp.stack(cs), jnp.stack(ms)


def setup_inputs(seed: int = 0) -> dict:
    key = jax.random.key(seed)
    k = jax.random.split(key, 32)

    def nrm(kk, shape, s):
        return jax.random.normal(kk, shape, jnp.float32) * s

    lb = min(BAND_PAST, PAST_LEN)
    return {
        'x_prompt': nrm(k[0], (BATCH, SEQ, D_MODEL), 1.0),
        'x_sample': nrm(k[1], (DEC_BATCH, DEC_SEQ, D_MODEL), 1.0),
        'cache_attn_k': nrm(k[2], (N_EVEN, DEC_BATCH, lb, H_B, DH_B), 1.0),
        'cache_attn_v': nrm(k[3], (N_EVEN, DEC_BATCH, lb, H_B, DH_B), 1.0),
        'state_gla': nrm(k[4], (N_EVEN, DEC_BATCH, H_A, DK_A, DV_A), 1.0),
        'state_ffn_conv': nrm(k[5], (DEPTH, DEC_BATCH, CONV_W - 1, D_FF), 1.0),
        'w_in_even': nrm(k[6], (N_EVEN, D_MODEL, IN_EVEN), D_MODEL ** -0.5),
        'w_gate_up': nrm(k[7], (N_EVEN, GATE_RANK, H_A * DK_A), GATE_RANK ** -0.5),
        'b_gate': nrm(k[8], (N_EVEN, H_A * DK_A), 0.1),
        'gla_norm_g': 1.0 + nrm(k[9], (N_EVEN, H_A, DV_A), 0.02),
        'rel_bias': nrm(k[10], (N_EVEN, H_B, 2 * REL_CLIP + 1), 0.1),
        'w_out_even': nrm(k[11], (N_EVEN, MIX_EVEN, D_MODEL), BETA * MIX_EVEN ** -0.5),
        'w_in_odd': nrm(k[12], (N_ODD, D_MODEL, 2 * DC), D_MODEL ** -0.5),
        'ln_v_g': 1.0 + nrm(k[13], (N_ODD, DC), 0.02),
        'ln_v_b': nrm(k[14], (N_ODD, DC), 0.02),
        'w_spatial': nrm(k[15], (N_ODD, G_C, CHUNK_C, CHUNK_C), CHUNK_C ** -0.5),
        'b_spatial': 1.0 + nrm(k[16], (N_ODD, G_C, CHUNK_C), 0.1),
        'w_out_odd': nrm(k[17], (N_ODD, DC, D_MODEL), BETA * DC ** -0.5),
        'ffn_w1': nrm(k[18], (DEPTH, D_MODEL, D_FF), D_MODEL ** -0.5),
        'ffn_w2': nrm(k[19], (DEPTH, D_MODEL, D_FF), D_MODEL ** -0.5),
        'ffn_conv_w': nrm(k[20], (DEPTH, CONV_W, D_FF), CONV_W ** -0.5),
        'ffn_conv_b': nrm(k[21], (DEPTH, D_FF), 0.02),
        'ffn_w3': nrm(k[22], (DEPTH, D_FF, D_MODEL), BETA * D_FF ** -0.5),
        'ln1_g': 1.0 + nrm(k[23], (DEPTH, D_MODEL), 0.02),
        'ln1_b': nrm(k[24], (DEPTH, D_MODEL), 0.02),
        'ln2_g': 1.0 + nrm(k[25], (DEPTH, D_MODEL), 0.02),
        'ln2_b': nrm(k[26], (DEPTH, D_MODEL), 0.02),
    }


def reference(x_prompt, x_sample, cache_attn_k, cache_attn_v, state_gla, state_ffn_conv,
              w_in_even, w_gate_up, b_gate, gla_norm_g, rel_bias, w_out_even,
              w_in_odd, ln_v_g, ln_v_b, w_spatial, b_spatial, w_out_odd,
              ffn_w1, ffn_w2, ffn_conv_w, ffn_conv_b, ffn_w3,
              ln1_g, ln1_b, ln2_g, ln2_b):
    P = dict(w_in_even=w_in_even, w_gate_up=w_gate_up, b_gate=b_gate, gla_norm_g=gla_norm_g,
             rel_bias=rel_bias, w_out_even=w_out_even, w_in_odd=w_in_odd, ln_v_g=ln_v_g,
             ln_v_b=ln_v_b, w_spatial=w_spatial, b_spatial=b_spatial, w_out_odd=w_out_odd,
             ffn_w1=ffn_w1, ffn_w2=ffn_w2, ffn_conv_w=ffn_conv_w, ffn_conv_b=ffn_conv_b,
             ffn_w3=ffn_w3, ln1_g=ln1_g, ln1_b=ln1_b, ln2_g=ln2_g, ln2_b=ln2_b)
    y_prompt, k_p, v_p, gla_p, conv_p, _ = _trunk(x_prompt, P, None)
    y_sample, k_s, v_s, gla_s, conv_s, mlp_v_s = _trunk(
        x_sample, P, (cache_attn_k, cache_attn_v, state_gla, state_ffn_conv))
    return (y_prompt, y_sample, k_p, v_p, gla_p, conv_p, k_s, v_s, gla_s, conv_s, mlp_v_s)
```

```python
import numpy as np
import concourse.bass as bass
import concourse.mybir as mybir
from concourse.bass_utils import run_bass_kernel_spmd
from contextlib import ExitStack

F32 = mybir.dt.float32
BF16 = mybir.dt.bfloat16
AF = mybir.ActivationFunctionType
ALU = mybir.AluOpType
AX = mybir.AxisListType

D = 2048
KC = 16
DFF = 5504
FC = 43
NMAIN = 17
NPRE = 15
NTM = 2
NCMAX = NTM * 128
ALPHA = 4.0 ** 0.25
EPS = 1e-5
WCOL = 256
NSLOT = 4
SAFE_SAME_ENGINE = True


class Buf:
    __slots__ = ("name", "w", "r", "dsem", "dcnt")

    def __init__(self, name):
        self.name = name
        self.w = None
        self.r = {}
        self.dsem = None
        self.dcnt = 0


class Ker:
    ENG = {"pe": "tensor", "act": "scalar", "dve": "vector", "pool": "gpsimd", "sp": "sync"}

    def __init__(self, nc, es):
        self.nc = nc
        self.es = es
        self.plan = False
        self.sem = {e: es.enter_context(nc.semaphore("sem_" + e)) for e in self.ENG}
        self.dsems = []
        self.reset()

    def reset(self):
        self.q = {e: [] for e in self.ENG}
        self.cnt = {e: 0 for e in self.ENG}
        self.seen = {e: {} for e in self.ENG}
        self.dbufs = []

    def _wait(self, eng, deps):
        own = self.sem[eng]
        for sem, val in deps:
            if sem is own and (eng == "pe" or not SAFE_SAME_ENGINE):
                continue
            if self.seen[eng].get(sem, 0) < val:
                self.seen[eng][sem] = val
                self.q[eng].append(lambda e, s=sem, v=val: e.wait_ge(s, v))

    @staticmethod
    def _deps(R, W):
        deps = []
        for b in R:
            if b.w is not None:
                deps.append(b.w)
        for b in W:
            if b.w is not None:
                deps.append(b.w)
            deps.extend(b.r.items())
        return deps

    def op(self, eng, fn, R=(), W=()):
        if self.plan:
            return
        self._wait(eng, self._deps(R, W))
        self.cnt[eng] += 1
        c = self.cnt[eng]
        sem = self.sem[eng]
        self.q[eng].append(lambda e, f=fn, s=sem: f(e).then_inc(s, 1))
        for b in W:
            b.w = (sem, c)
            b.r = {}
        for b in R:
            if b not in W:
                b.r[sem] = c

    def dma(self, eng, pairs, buf, write, extraR=(), deps=()):
        if self.plan:
            return
        key = "w" if write else "r"
        if buf.dsem is None:
            buf.dsem = {}
            buf.dcnt = {}
        if key not in buf.dsem:
            buf.dsem[key] = self.es.enter_context(self.nc.semaphore("d%s_%s" % (key, buf.name)))
            buf.dcnt[key] = 0
        if buf not in self.dbufs:
            self.dbufs.append(buf)
        R = list(extraR) + ([] if write else [buf])
        W = [buf] if write else []
        self._wait(eng, self._deps(R, W) + list(deps))
        sem = buf.dsem[key]
        for (o, i) in pairs:
            self.q[eng].append(lambda e, o=o, i=i, s=sem: e.dma_start(out=o, in_=i).then_inc(s, 16))
        buf.dcnt[key] += 16 * len(pairs)
        if write:
            buf.w = (sem, buf.dcnt[key])
            buf.r = {}
        else:
            buf.r[sem] = buf.dcnt[key]
        for b in extraR:
            b.r[sem] = buf.dcnt[key]
        return (sem, buf.dcnt[key])

    def barrier(self, engs=("pe", "act", "dve")):
        if self.plan:
            return
        for e in engs:
            deps = [(self.sem[f], self.cnt[f]) for f in engs if f != e and self.cnt[f] > 0]
            self._wait(e, deps)

    def finish(self):
        deps = []
        for b in self.dbufs:
            for k_ in b.dsem:
                deps.append((b.dsem[k_], b.dcnt[k_]))
        for e in ("pe", "act", "dve"):
            if self.cnt[e] > 0:
                deps.append((self.sem[e], self.cnt[e]))
        self._wait("sp", deps)

    def emit(self):
        block = self.es.enter_context(self.nc.Block())
        for eng, attr in self.ENG.items():
            lst = self.q[eng]

            def body(e, lst=lst):
                for f in lst:
                    f(e)
            getattr(block, attr)(body)


def build():
    nc = bass.Bass("TRN2", target_bir_lowering=False)

    def din(name, shape):
        return nc.dram_tensor(name, list(shape), F32, kind="ExternalInput").ap()

    def dout(name, shape):
        return nc.dram_tensor(name, list(shape), F32, kind="ExternalOutput").ap()

    xm = din("xm", [NMAIN * 128, D])
    xpre = din("xpre", [NPRE * 128, D])
    xs = din("xs", [32, D])
    ck = din("ck", [512, 1024])
    cv = din("cv", [512, 1024])
    sg = din("sg", [128, 4, 256])
    chs_in = din("chs", [128, 2 * FC * 2])
    w_in_even = din("w_in_even", [D, 6160])
    w_out_even = din("w_out_even", [D, D])
    w_in_odd = din("w_in_odd", [D, 2 * D])
    w_out_odd = din("w_out_odd", [D, D])
    ffn_w1 = din("ffn_w1", [2, D, DFF])
    ffn_w2 = din("ffn_w2", [2, D, DFF])
    ffn_w3 = din("ffn_w3", [2, DFF, D])
    lnp = din("lnp", [10, 128, D])
    c_ident = din("c_ident", [128, 128])
    c_tri2 = din("c_tri2", [128, 128])
    c_sel = din("c_sel", [128, 2])
    c_maskt = din("c_maskt", [128, 128])
    c_gn = din("c_gn", [128, 1024])
    c_wgu = din("c_wgu", [16, 512])
    c_bg = din("c_bg", [1, 512])
    c_ones = din("c_ones", [1, 128])
    c_cw = din("c_cw", [128, 2 * FC * 3])
    c_cb = din("c_cb", [128, 2 * FC])
    c_wst = din("c_wst", [128, 8 * 128])
    c_bsp = din("c_bsp", [128, 8])
    c_pv = din("c_pv", [128, 1])
    c_bt = din("c_bt", [128, 3 * 8 * 128])
    c_bts3 = din("c_bts3", [128, 8 * 32])
    c_bts4 = din("c_bts4", [32, 8 * 32])
    c_cbias = din("c_cbias", [128, 8])

    y_main = dout("y_main", [NMAIN * 128, D])
    y_s = dout("y_s", [32, D])
    kout = dout("kout", [512, 1024])
    vout = dout("vout", [512, 1024])
    gla_out = dout("gla_out", [128, 1024])
    conv_out = dout("conv_out", [128, 2 * FC * 2])
    ks_out = dout("ks_out", [32, 1024])
    vs_out = dout("vs_out", [32, 1024])
    gla_s = dout("gla_s", [128, 1024])
    conv_s = dout("conv_s", [128, 2 * FC * 2])
    mlpv_s = dout("mlpv_s", [32, D])

    es = ExitStack()
    with es:
        def sb(name, shape, dt):
            return es.enter_context(nc.sbuf_tensor(name, list(shape), dt))

        X = [sb("X%d" % j, [128, D], F32) for j in range(NTM)]
        Xb = [Buf("X%d" % j) for j in range(NTM)]
        XB = sb("XB", [128, D], BF16); XBb = Buf("XB")
        XT = sb("XT", [128, KC, NCMAX], BF16); XTb = [Buf("XT%d" % j) for j in range(NTM)]
        NRING = 4 + NTM
        KT = sb("KT", [128, 8, NRING * 128], BF16); KTb = [Buf("KT%d" % s) for s in range(NRING)]
        VB = sb("VB", [128, NRING, 8, 130], BF16); VBb = [Buf("VB%d" % s) for s in range(NRING)]
        S = sb("S", [128, 4, 256], F32); Sb = Buf("S")
        SBf = sb("SBf", [128, 4, 256], BF16); SBfb = Buf("SBf")
        SBf2 = sb("SBf2", [128, 4, 256], BF16); SBf2b = Buf("SBf2")
        SS = sb("SS", [128, 4, 256], F32); SSb = Buf("SS")
        SSBf = sb("SSBf", [128, 4, 256], BF16); SSBfb = Buf("SSBf")
        STMP = sb("STMP", [128, 256], F32); STMPb = Buf("STMP")
        WS = [sb("WS%d" % i, [128, KC * WCOL], BF16) for i in range(NSLOT)]
        WSb = [Buf("WS%d" % i) for i in range(NSLOT)]
        LNG = sb("LNG", [128, D], F32); LNGb = Buf("LNG")
        LNB = sb("LNB", [128, D], F32); LNBb = Buf("LNB")
        BT = sb("BT", [128, 3, 8, 128], F32)
        BTS3 = sb("BTS3", [128, 8, 32], F32)
        BTS4 = sb("BTS4", [32, 8, 32], F32)
        CBIAS = sb("CBIAS", [128, 8], F32)
        IDENT = sb("IDENT", [128, 128], BF16)
        TRI2 = sb("TRI2", [128, 128], F32)
        SEL = sb("SEL", [128, 2], F32)
        MASKT = sb("MASKT", [128, 128], F32)
        GN = sb("GN", [128, 1024], F32)
        WGU = sb("WGU", [16, 512], BF16)
        BG = sb("BG", [1, 512], BF16)
        ONES = sb("ONES", [1, 128], BF16)
        CW = sb("CW", [128, 2, FC, 3], F32)
        CBc = sb("CBc", [128, 2, FC], F32)
        WST = sb("WST", [128, 8, 128], BF16); WSTb = Buf("WST")
        BSP = sb("BSP", [128, 8], F32)
        PV = sb("PV", [128, 1], F32)
        CH = sb("CH", [128, 2, FC, 2], F32); CHb = Buf("CH")
        CHS = sb("CHS", [128, 2, FC, 2], F32); CHSb = Buf("CHS")
        STG = [sb("STG%d" % i, [128, 256], F32) for i in range(4)]
        STGb = [Buf("STG%d" % i) for i in range(4)]
        SMALL = sb("SMALL", [128, 64], F32); SMALLb = Buf("SMALL")
        CONSTb = Buf("CONST")
        CONSTPb = Buf("CONSTP")

        NB = 21900
        NF = 4096
        SCRB = sb("SCRB", [128, NB], BF16)
        SCRF = sb("SCRF", [128, NF], F32)

        psS = [es.enter_context(nc.psum_tensor("psS%d" % i, [128, 512], F32)) for i in range(4)]
        psSb = [Buf("psS%d" % i) for i in range(4)]
        psL = [es.enter_context(nc.psum_tensor("psL%d" % i, [128, 512], F32)) for i in range(2)]
        psLb = [Buf("psL%d" % i) for i in range(2)]
        psT = [es.enter_context(nc.psum_tensor("psT%d" % i, [128, 1024], BF16)) for i in range(2)]
        psTb = [Buf("psT%d" % i) for i in range(2)]

        K = Ker(nc, es)
        st = {}

        def getS():
            i = st["iS"] % 4
            st["iS"] += 1
            return psS[i], psSb[i]

        def getT():
            i = st["iT"] % 2
            st["iT"] += 1
            return psT[i], psTb[i]

        def getSTG():
            i = st["iG"] % 4
            st["iG"] += 1
            return STG[i], STGb[i]

        def wnext(wap, r0, nk, c0, n):
            if K.plan:
                st["pieces"].append((wap, r0, nk, c0, n, st["mp"], st["mpk"]))
                if st["mp"] == 0:
                    st["scr_off"].append(st["scr_total"])
                    st["scr_total"] += nk * n
                st["mpk"] += 1
                return None, None
            i = st["wi"]
            exp = st["pieces"][i]
            assert exp[1:5] == (r0, nk, c0, n), (exp[1:5], (r0, nk, c0, n))
            while st["wissued"] < min(len(st["pieces"]), i + NSLOT - 1):
                pi = st["wissued"]
                (pw, pr0, pnk, pc0, pn, pmp, pk) = st["pieces"][pi]
                slot = pi % NSLOT
                if pmp >= 1:
                    off = st["scr_off"][pk]
                    sz = pnk * pn
                    K.dma("pool", [(WS[slot][:, 0:sz], st["scratch"][:, off:off + sz])], WSb[slot], write=True, deps=[st["store_dep"][pk]])
                else:
                    pairs = []
                    kstep = 4
                    for k0 in range(0, pnk, kstep):
                        k1 = min(pnk, k0 + kstep)
                        o = WS[slot][:, k0 * pn:k1 * pn].rearrange("p (k n) -> p k n", n=pn)
                        src = pw[pr0 + k0 * 128: pr0 + k1 * 128, pc0:pc0 + pn].rearrange("(k p) n -> p k n", p=128)
                        pairs.append((o, src))
                    K.dma("pool", pairs, WSb[slot], write=True)
                st["wissued"] += 1
            st["wi"] += 1
            slot = i % NSLOT
            if exp[5] == 0:
                off = st["scr_off"][exp[6]]
                sz = nk * n
                st["store_dep"][exp[6]] = K.dma("sp", [(st["scratch"][:, off:off + sz], WS[slot][:, 0:sz])], WSb[slot], write=False)
            view = WS[slot][:, 0:nk * n].rearrange("p (k n) -> p k n", n=n)
            return view, WSb[slot]

        def to_T(src, srcb, ntok, col0, dstbufs):
            for half in range(2):
                pt, ptb = getT()

                def f(e, pt=pt, half=half):
                    ins = None
                    for i in range(8):
                        kc = half * 8 + i
                        ins = e.transpose(out=pt[:, i * 128:(i + 1) * 128], in_=src[:, kc * 128:(kc + 1) * 128], identity=IDENT[:, :])
                    return ins
                K.op("pe", f, R=[srcb, CONSTPb], W=[ptb])
                pv = pt[:, :].rearrange("p (i t) -> p i t", t=128)[:, :, 0:ntok]
                dst = XT[:, half * 8:(half + 1) * 8, col0:col0 + ntok]
                K.op("dve", lambda e, pv=pv, dst=dst: e.tensor_copy(out=dst, in_=pv), R=[ptb], W=dstbufs)

        def cast_T(j, ntok, col0):
            K.op("dve", lambda e: e.tensor_copy(out=XB[0:ntok, :], in_=X[j][0:ntok, :]), R=[Xb[j]], W=[XBb])
            to_T(XB, XBb, ntok, col0, [XTb[j]])

        def load_ln(gi):
            K.dma("sp", [(LNG[:, :], lnp[gi])], LNGb, write=True)
            K.dma("sp", [(LNB[:, :], lnp[gi + 1])], LNBb, write=True)

        def layer_norm(xt, xtb, ntok):
            st6 = SMALL[0:ntok, 0:24]
            mv = SMALL[0:ntok, 24:26]
            sd = SMALL[0:ntok, 26:27]
            rs = SMALL[0:ntok, 27:28]

            def f1(e):
                ins = None
                for i in range(4):
                    ins = e.bn_stats(out=SMALL[0:ntok, i * 6:(i + 1) * 6], in_=xt[0:ntok, i * 512:(i + 1) * 512])
                return ins
            K.op("dve", f1, R=[xtb], W=[SMALLb])
            K.op("dve", lambda e: e.bn_aggr(out=mv, in_=st6), R=[], W=[SMALLb])
            K.op("dve", lambda e: e.tensor_scalar(out=sd, in0=SMALL[0:ntok, 25:26], scalar1=EPS, scalar2=None, op0=ALU.add),
                 R=[], W=[SMALLb])
            K.op("act", lambda e: e.activation(out=sd, in_=sd, func=AF.Sqrt), R=[], W=[SMALLb])
            K.op("dve", lambda e: e.reciprocal(out=rs, in_=sd), R=[], W=[SMALLb])
            K.op("dve", lambda e: e.tensor_scalar(out=xt[0:ntok, :], in0=xt[0:ntok, :], scalar1=SMALL[0:ntok, 24:25],
                                                  scalar2=rs, op0=ALU.subtract, op1=ALU.mult), R=[SMALLb], W=[xtb])
            K.op("dve", lambda e: e.tensor_tensor(out=xt[0:ntok, :], in0=xt[0:ntok, :], in1=LNG[0:ntok, :], op=ALU.mult),
                 R=[LNGb], W=[xtb])
            K.op("dve", lambda e: e.tensor_tensor(out=xt[0:ntok, :], in0=xt[0:ntok, :], in1=LNB[0:ntok, :], op=ALU.add),
                 R=[LNBb], W=[xtb])

        def proj_tok(wv, wb, nk, xt_cols, ntok, xtbufs, ncols):
            ps, psb = getS()

            def f(e):
                ins = None
                for k in range(nk):
                    ins = e.matmul(ps[0:ntok, 0:ncols], lhsT=XT[:, k, xt_cols:xt_cols + ntok], rhs=wv[:, k, 0:ncols],
                                   start=(k == 0), stop=(k == nk - 1))
                return ins
            K.op("pe", f, R=[wb] + xtbufs, W=[psb])
            return ps, psb

        def out_rows(ps, psb, ntok, ncols, dst_ap):
            sg_, sgb = getSTG()
            K.op("dve", lambda e: e.tensor_copy(out=sg_[0:ntok, 0:ncols], in_=ps[0:ntok, 0:ncols]), R=[psb], W=[sgb])
            K.dma("sp", [(dst_ap, sg_[0:ntok, 0:ncols])], sgb, write=False)

        def gla_block(ntok, nch, QKAj, QKAb, VAj, VAb, LGj, LGb, Sx, Sxb, SBl, SBlb, need_out, MIXj, MIXb, GOGj, GOGb, scr):
            csz = ntok // nch
            ENB, EB, KD, QD, QDTf, QDT0, QDT1, KDT, ATm, EBLt, SQ, scrb = scr
            bps, bpsb = getS()
            K.op("pe", lambda e: e.matmul(bps[0:ntok, 0:512], lhsT=TRI2[0:ntok, 0:ntok], rhs=LGj[0:ntok, :], start=True, stop=True),
                 R=[LGb, CONSTb], W=[bpsb])
            blps, blpsb = getS()

            def fbl(e):
                ins = None
                for h in range(4):
                    for c_ in range(nch):
                        ins = e.matmul(blps[:, c_ * 4 + h:c_ * 4 + h + 1], lhsT=LGj[0:ntok, h * 128:(h + 1) * 128], rhs=SEL[0:ntok, c_:c_ + 1],
                                       start=True, stop=True)
                return ins
            K.op("pe", fbl, R=[LGb, CONSTb], W=[blpsb])
            K.op("act", lambda e: e.activation(out=ENB[0:ntok, :], in_=bps[0:ntok, 0:512], func=AF.Exp, scale=-1.0), R=[bpsb], W=[scrb["ENB"]])
            if need_out:
                K.op("act", lambda e: e.activation(out=EB[0:ntok, :], in_=bps[0:ntok, 0:512], func=AF.Exp), R=[bpsb], W=[scrb["EB"]])
            K.op("act", lambda e: e.activation(out=EBLt[:, 0:4 * nch], in_=blps[:, 0:4 * nch], func=AF.Exp), R=[blpsb], W=[scrb["EBL"]])
            dbg0("g_b")
            yield
            K.op("dve", lambda e: e.tensor_tensor(out=KD[0:ntok, :], in0=QKAj[0:ntok, 512:1024], in1=ENB[0:ntok, :], op=ALU.mult),
                 R=[QKAb, scrb["ENB"]], W=[scrb["KD"]])
            if need_out:
                K.op("dve", lambda e: e.tensor_tensor(out=QD[0:ntok, :], in0=QKAj[0:ntok, 0:512], in1=EB[0:ntok, :], op=ALU.mult),
                     R=[QKAb, scrb["EB"]], W=[scrb["QD"]])
                dbg0("g_qd")
                pt, ptb = getT()

                def ftr(e):
                    ins = None
                    for h in range(4):
                        ins = e.transpose(out=pt[:, h * 128:(h + 1) * 128], in_=QD[:, h * 128:(h + 1) * 128], identity=IDENT[:, :])
                    for h in range(4):
                        ins = e.transpose(out=pt[:, 512 + h * 128:512 + (h + 1) * 128], in_=KD[:, h * 128:(h + 1) * 128], identity=IDENT[:, :])
                    return ins
                K.op("pe", ftr, R=[scrb["QD"], scrb["KD"], CONSTPb], W=[ptb])
                dbg0("g_trp")
                pq = pt[:, 0:512].rearrange("p (h t) -> p h t", t=128)
                pk = pt[:, 512:1024].rearrange("p (h t) -> p h t", t=128)
                K.op("dve", lambda e: e.tensor_copy(out=QDTf[:, :, 0:ntok], in_=pq[:, :, 0:ntok]), R=[ptb], W=[scrb["QDTf"]])
                K.op("dve", lambda e: e.tensor_copy(out=KDT[:, :, 0:ntok], in_=pk[:, :, 0:ntok]), R=[ptb], W=[scrb["KDT"]])
                dbg0("g_ev")
                if nch == 2:
                    K.op("dve", lambda e: e.memset(QDT0[:, :, :], 0.0), R=[], W=[scrb["QDT0"]])
                    K.op("dve", lambda e: e.memset(QDT1[:, :, :], 0.0), R=[], W=[scrb["QDT1"]])
                    K.op("dve", lambda e: e.tensor_copy(out=QDT0[:, :, 0:64], in_=pq[:, :, 0:64]), R=[ptb], W=[scrb["QDT0"]])
                    K.op("dve", lambda e: e.tensor_copy(out=QDT1[:, :, 64:128], in_=pq[:, :, 64:128]), R=[ptb], W=[scrb["QDT1"]])
                dbg0("g_tr")
                yield
                aps, apsb = getS()

                def fat(e):
                    ins = None
                    for h in range(4):
                        ins = e.matmul(aps[0:ntok, h * 128:h * 128 + ntok], lhsT=KDT[:, h, 0:ntok], rhs=QDTf[:, h, 0:ntok], start=True, stop=True)
                    return ins
                K.op("pe", fat, R=[scrb["KDT"], scrb["QDTf"]], W=[apsb])

                def fmask(e):
                    ins = None
                    for h in range(4):
                        ins = e.tensor_tensor(out=ATm[0:ntok, h, 0:ntok], in0=aps[0:ntok, h * 128:h * 128 + ntok], in1=MASKT[0:ntok, 0:ntok], op=ALU.mult)
                    return ins
                K.op("dve", fmask, R=[apsb, CONSTb], W=[scrb["ATm"]])
                ops_ = [(psL[0], psLb[0]), (psL[1], psLb[1])]
            dbg0("g_at")
            yield

            def o_groups():
                for hp in range(2):
                    for hh in range(2):
                        h = hp * 2 + hh

                        def fo(e, hp=hp, hh=hh, h=h):
                            dst = ops_[hp][0][0:ntok, hh * 256:(hh + 1) * 256]
                            if nch == 1:
                                e.matmul(dst, lhsT=QDTf[:, h, 0:ntok], rhs=SBl[0][:, h, :], start=True, stop=False)
                            else:
                                e.matmul(dst, lhsT=QDT0[:, h, 0:ntok], rhs=SBl[0][:, h, :], start=True, stop=False)
                                e.matmul(dst, lhsT=QDT1[:, h, 0:ntok], rhs=SBl[1][:, h, :], start=False, stop=False)
                            return e.matmul(dst, lhsT=ATm[0:ntok, h, 0:ntok], rhs=VAj[0:ntok, h * 256:(h + 1) * 256], start=False, stop=True)
                        K.op("pe", fo, R=[scrb["QDTf"], scrb["QDT0"], scrb["QDT1"], scrb["ATm"], VAb] + SBlb, W=[ops_[hp][1]])
            if need_out and nch == 1:
                o_groups()
            for c in range(nch):
                r0 = c * csz
                if need_out and nch == 2 and c == 1:
                    o_groups()
                for hp in range(2):
                    dps, dpsb = getS()

                    def fds(e, hp=hp, dps=dps, r0=r0):
                        ins = None
                        for hh in range(2):
                            h = hp * 2 + hh
                            ins = e.matmul(dps[:, hh * 256:(hh + 1) * 256], lhsT=KD[r0:r0 + csz, h * 128:(h + 1) * 128],
                                           rhs=VAj[r0:r0 + csz, h * 256:(h + 1) * 256], start=True, stop=True)
                        return ins
                    K.op("pe", fds, R=[scrb["KD"], VAb], W=[dpsb])
                    for hh in range(2):
                        h = hp * 2 + hh
                        ebl = EBLt[:, c * 4 + h:c * 4 + h + 1]
                        K.op("dve", lambda e, h=h, ebl=ebl: e.tensor_scalar(out=STMP[:, :], in0=Sx[:, h, :], scalar1=ebl, scalar2=None, op0=ALU.mult),
                             R=[Sxb, scrb["EBL"]], W=[STMPb])
                        K.op("dve", lambda e, h=h, hh=hh, ebl=ebl, dps=dps: e.scalar_tensor_tensor(
                            out=Sx[:, h, :], in0=dps[:, hh * 256:(hh + 1) * 256], scalar=ebl, in1=STMP[:, :], op0=ALU.mult, op1=ALU.add),
                            R=[dpsb, STMPb, scrb["EBL"]], W=[Sxb])
                    yield
                if need_out and nch == 2:
                    nb_ = (c + 1) % 2
                    K.op("act", lambda e, nb_=nb_: e.activation(out=SBl[nb_][:, :, :].rearrange("p h v -> p (h v)"), in_=Sx[:, :, :].rearrange("p h v -> p (h v)"), func=AF.Copy),
                         R=[Sxb], W=[SBlb[nb_]])
            dbg0("g_o")
            yield
            if need_out:
                for h in range(4):
                    hp, hh = h // 2, h % 2
                    osrc = ops_[hp][0][0:ntok, hh * 256:(hh + 1) * 256]
                    K.op("act", lambda e, osrc=osrc: e.activation(out=SQ[0:ntok, :], in_=osrc, func=AF.Square), R=[ops_[hp][1]], W=[scrb["SQ"]])
                    K.op("dve", lambda e, h=h: e.reduce_sum(out=SMALL[0:ntok, 32 + h:33 + h], in_=SQ[0:ntok, :], axis=AX.X), R=[scrb["SQ"]], W=[SMALLb])
                    if h % 2 == 1:
                        yield
                K.op("dve", lambda e: e.tensor_scalar(out=SMALL[0:ntok, 36:40], in0=SMALL[0:ntok, 32:36], scalar1=1.0 / 256.0, scalar2=EPS,
                                                      op0=ALU.mult, op1=ALU.add), R=[], W=[SMALLb])
                K.op("act", lambda e: e.activation(out=SMALL[0:ntok, 36:40], in_=SMALL[0:ntok, 36:40], func=AF.Sqrt), R=[], W=[SMALLb])
                K.op("dve", lambda e: e.reciprocal(out=SMALL[0:ntok, 40:44], in_=SMALL[0:ntok, 36:40]), R=[], W=[SMALLb])
                for h in range(4):
                    hp, hh = h // 2, h % 2
                    osrc = ops_[hp][0][0:ntok, hh * 256:(hh + 1) * 256]
                    K.op("dve", lambda e, h=h, osrc=osrc: e.scalar_tensor_tensor(
                        out=MIXj[0:ntok, h * 256:(h + 1) * 256], in0=osrc, scalar=SMALL[0:ntok, 40 + h:41 + h],
                        in1=GOGj[0:ntok, h * 256:(h + 1) * 256], op0=ALU.mult, op1=ALU.mult),
                        R=[ops_[hp][1], SMALLb, GOGb], W=[MIXb])

        def attn_head(h, nq, qap, qb, blocks, PTt, PTb, TMPt, TMPb, dst, dstb, dflag=False):
            sc = [getS(), getS()]

            def fsc(e):
                ins = None
                for bi, (kT, kb, v, vb, nk, bias) in enumerate(blocks):
                    ps = sc[bi // 4][0]
                    ins = e.matmul(ps[0:nk, (bi % 4) * 128:(bi % 4) * 128 + nq], lhsT=kT, rhs=qap, start=True, stop=True)
                return ins
            K.op("pe", fsc, R=[qb] + [b[1] for b in blocks], W=[sc[0][1], sc[1][1]])
            if dflag:
                dbg0("a_sc")
            for bi, (kT, kb, v, vb, nk, bias) in enumerate(blocks):
                ps, psb = sc[bi // 4]
                src = ps[0:nk, (bi % 4) * 128:(bi % 4) * 128 + nq]
                dstp = PTt[0:nk, bi, 0:nq]
                import os as _os6
                _sk = _os6.environ.get("KDBG_SKIP", "")
                if dflag and (("b%d" % bi) in _sk.split(",")):
                    continue
                if bias is None:
                    K.op("act", lambda e, src=src, dstp=dstp, nk=nk: e.activation(out=dstp, in_=src, func=AF.Exp, bias=CBIAS[0:nk, h:h + 1]),
                         R=[psb, CONSTb], W=[PTb])
                else:
                    K.op("dve", lambda e, src=src, bias=bias: e.tensor_tensor(out=src, in0=src, in1=bias, op=ALU.add), R=[CONSTb], W=[psb])
                    K.op("act", lambda e, src=src, dstp=dstp: e.activation(out=dstp, in_=src, func=AF.Exp), R=[psb], W=[PTb])
            if dflag:
                dbg0("a_exp")
            ob, obb = getS()

            def fpv(e):
                ins = None
                nb = len(blocks)
                for bi, (kT, kb, v, vb, nk, bias) in enumerate(blocks):
                    ins = e.matmul(ob[0:nq, 0:129], lhsT=PTt[0:nk, bi, 0:nq], rhs=v, start=(bi == 0), stop=(bi == nb - 1))
                return ins
            K.op("pe", fpv, R=[PTb] + [b[3] for b in blocks], W=[obb])
            if dflag:
                dbg0("a_pv")
            K.op("dve", lambda e: e.reciprocal(out=SMALL[0:nq, 48:49], in_=ob[0:nq, 128:129]), R=[obb], W=[SMALLb])
            K.op("dve", lambda e: e.tensor_scalar(out=dst, in0=ob[0:nq, 0:128], scalar1=SMALL[0:nq, 48:49], scalar2=None, op0=ALU.mult),
                 R=[obb, SMALLb], W=[dstb])

        def ffn(l, blocks, segs, ncols):
            HT = [SCRB[:, i * 12 * NCMAX:(i + 1) * 12 * NCMAX].rearrange("p (c t) -> p c t", t=NCMAX) for i in range(2)]
            HTb = [Buf("HT0"), Buf("HT1")]
            U = [SCRF[:, i * 264:(i + 1) * 264] for i in range(2)]
            Ub = [Buf("U0"), Buf("U1")]
            C = [SCRF[:, 528 + i * 256:528 + (i + 1) * 256] for i in range(2)]
            Cb = [Buf("C0"), Buf("C1")]
            w1 = ffn_w1[l]
            w2 = ffn_w2[l]
            w3 = ffn_w3[l]
            for j, ntok, col0 in blocks:
                cast_T(j, ntok, col0)
            xtb = [XTb[j] for j, _, _ in blocks]
            groups = [(0, 12), (12, 12), (24, 12), (36, 7)]
            it = 0
            for gi, (c0g, ncg) in enumerate(groups):
                ht = HT[gi % 2]
                htb = HTb[gi % 2]
                for pc in range(0, ncg, 2):
                    npc = min(2, ncg - pc)
                    fc0 = c0g + pc
                    w1v, w1b = wnext(w1, 0, KC, fc0 * 128, npc * 128)
                    w2v, w2b = wnext(w2, 0, KC, fc0 * 128, npc * 128)
                    for cc in range(npc):
                        fc = fc0 + cc
                        fcl = pc + cc
                        ups, upsb = getS()
                        zps, zpsb = getS()

                        def fu(e, wv=w1v, ps=ups, cc=cc):
                            ins = None
                            for k in range(KC):
                                ins = e.matmul(ps[:, 0:ncols], lhsT=wv[:, k, cc * 128:(cc + 1) * 128], rhs=XT[:, k, 0:ncols], start=(k == 0), stop=(k == KC - 1))
                            return ins
                        if not K.plan:
                            K.op("pe", fu, R=[w1b] + xtb, W=[upsb])
                            K.op("pe", lambda e, wv=w2v, ps=zps, cc=cc: fu(e, wv, ps, cc), R=[w2b] + xtb, W=[zpsb])
                        for (sc0, n, Hh, Hb) in segs:
                            u = U[it % 2]
                            ub = Ub[it % 2]
                            c = C[it % 2]
                            cb = Cb[it % 2]
                            it += 1
                            K.op("act", lambda e, u=u, ups=ups, sc0=sc0, n=n: e.activation(func=AF.Copy, out=u[:, 2:2 + n], in_=ups[:, sc0:sc0 + n]), R=[upsb], W=[ub])
                            K.op("act", lambda e, u=u, Hh=Hh, fc=fc: e.activation(func=AF.Copy, out=u[:, 0:2], in_=Hh[:, l, fc, :]), R=[Hb], W=[ub])
                            K.op("act", lambda e, c=c, ups=ups, sc0=sc0, n=n, fc=fc: e.activation(
                                out=c[:, 0:n], in_=ups[:, sc0:sc0 + n], func=AF.Identity, bias=CBc[:, l, fc:fc + 1], scale=CW[:, l, fc, 2:3]),
                                R=[upsb, CONSTb], W=[cb])
                            K.op("dve", lambda e, c=c, u=u, n=n, fc=fc: e.scalar_tensor_tensor(
                                out=c[:, 0:n], in0=u[:, 1:1 + n], scalar=CW[:, l, fc, 1:2], in1=c[:, 0:n], op0=ALU.mult, op1=ALU.add),
                                R=[ub, CONSTb], W=[cb])
                            K.op("dve", lambda e, c=c, u=u, n=n, fc=fc: e.scalar_tensor_tensor(
                                out=c[:, 0:n], in0=u[:, 0:n], scalar=CW[:, l, fc, 0:1], in1=c[:, 0:n], op0=ALU.mult, op1=ALU.add),
                                R=[ub, CONSTb], W=[cb])
                            K.op("act", lambda e, c=c, n=n: e.activation(out=c[:, 0:n], in_=c[:, 0:n], func=AF.Gelu_apprx_tanh), R=[], W=[cb])
                            K.op("dve", lambda e, c=c, n=n, ht=ht, fcl=fcl, zps=zps, sc0=sc0: e.tensor_tensor(
                                out=ht[:, fcl, sc0:sc0 + n], in0=c[:, 0:n], in1=zps[:, sc0:sc0 + n], op=ALU.mult), R=[cb, zpsb], W=[htb])
                            K.op("act", lambda e, u=u, Hh=Hh, fc=fc, n=n: e.activation(func=AF.Copy, out=Hh[:, l, fc, :], in_=u[:, n:n + 2]), R=[ub], W=[Hb])
                for cbk in range(D // WCOL):
                    w3v, w3b = wnext(w3, c0g * 128, ncg, cbk * WCOL, WCOL)
                    for j, ntok, col0 in blocks:
                        ps, psb = getS()

                        def f3(e, ps=ps, wv=w3v, ntok=ntok, col0=col0, ht=ht, ncg=ncg):
                            ins = None
                            for k in range(ncg):
                                ins = e.matmul(ps[0:ntok, 0:WCOL], lhsT=ht[:, k, col0:col0 + ntok], rhs=wv[:, k, :], start=(k == 0), stop=(k == ncg - 1))
                            return ins
                        if not K.plan:
                            K.op("pe", f3, R=[w3b, htb], W=[psb])
                        xd = X[j][0:ntok, cbk * WCOL:(cbk + 1) * WCOL]
                        if gi == 0:
                            K.op("dve", lambda e, xd=xd, ps=ps, ntok=ntok: e.scalar_tensor_tensor(
                                out=xd, in0=xd, scalar=ALPHA, in1=ps[0:ntok, 0:WCOL], op0=ALU.mult, op1=ALU.add), R=[psb], W=[Xb[j]])
                        else:
                            K.op("dve", lambda e, xd=xd, ps=ps, ntok=ntok: e.tensor_tensor(out=xd, in0=xd, in1=ps[0:ntok, 0:WCOL], op=ALU.add),
                                 R=[psb], W=[Xb[j]])

        def out_proj(wap, src_fn, blocks):
            for j, ntok, col0 in blocks:
                s_, sb_ = src_fn(j)
                to_T(s_, sb_, ntok, col0, [XTb[j]])
            for cbk in range(D // WCOL):
                wv, wb = wnext(wap, 0, KC, cbk * WCOL, WCOL)
                for j, ntok, col0 in blocks:
                    if K.plan:
                        continue
                    ps, psb = proj_tok(wv, wb, KC, col0, ntok, [XTb[j]], WCOL)
                    xd = X[j][0:ntok, cbk * WCOL:(cbk + 1) * WCOL]
                    K.op("dve", lambda e, xd=xd, ps=ps, ntok=ntok: e.scalar_tensor_tensor(
                        out=xd, in0=xd, scalar=ALPHA, in1=ps[0:ntok, 0:WCOL], op0=ALU.mult, op1=ALU.add), R=[psb], W=[Xb[j]])

        def carveA():
            o = [0]

            def tb(n):
                a = SCRB[:, o[0]:o[0] + n]
                o[0] += n
                return a
            of = [0]

            def tf(n):
                a = SCRF[:, of[0]:of[0] + n]
                of[0] += n
                return a
            A = {}
            A["QKA"] = [tb(1024) for _ in range(NTM)]
            A["VA"] = [tb(1024) for _ in range(NTM)]
            A["GOG"] = [tb(1024) for _ in range(NTM)]
            A["MIX"] = [tb(2048) for _ in range(NTM)]
            A["QT"] = tb(8 * NCMAX).rearrange("p (h t) -> p h t", t=NCMAX)
            A["KTs"] = tb(8 * 32).rearrange("p (h t) -> p h t", t=32)
            A["VSN"] = tb(8 * 130).rearrange("p (h d) -> p h d", d=130)
            A["GLT"] = tb(NCMAX)
            A["KD"] = tb(512)
            A["QD"] = tb(512)
            A["QDTf"] = tb(512).rearrange("p (h t) -> p h t", t=128)
            A["QDT0"] = tb(512).rearrange("p (h t) -> p h t", t=128)
            A["QDT1"] = tb(512).rearrange("p (h t) -> p h t", t=128)
            A["KDT"] = tb(512).rearrange("p (h t) -> p h t", t=128)
            A["ATm"] = tb(512).rearrange("p (h t) -> p h t", t=128)
            A["PT"] = [tb(640).rearrange("p (b t) -> p b t", t=128) for _ in range(2)]
            A["KSh"] = [tb(512).rearrange("p (b d) -> p b d", d=128) for _ in range(2)]
            A["KTSh"] = [tb(512) for _ in range(2)]
            A["VSh"] = [tb(4 * 130).rearrange("p (b d) -> p b d", d=130) for _ in range(2)]
            A["LOGG"] = [tf(512) for _ in range(NTM)]
            A["ENB"] = tf(512)
            A["EB"] = tf(512)
            A["SQ"] = tf(256)
            A["TMP"] = tf(256).rearrange("p (b t) -> p b t", t=128)
            A["EBLt"] = tf(8)
            A["LTMP"] = tf(512)
            assert o[0] <= NB and of[0] <= NF, (o[0], of[0])
            Ab = {}
            for k in ["QKA", "VA", "GOG", "MIX", "LOGG"]:
                Ab[k] = [Buf(k + str(i)) for i in range(NTM)]
            for k in ["QT", "KTs", "VSN", "GLT", "KD", "QD", "QDTf", "QDT0", "QDT1", "KDT", "ATm", "ENB", "EB", "SQ", "EBL", "LTMP"]:
                Ab[k] = Buf(k)
            for k in ["PT", "KSh", "KTSh", "VSh", "TMP"]:
                Ab[k] = [Buf(k + "0"), Buf(k + "1")]
            return A, Ab

        def mixer0(blocks, ncols, A, Ab, mode, kout_rows, sample_blk):
            W = w_in_even
            xtb = [XTb[b[0]] for b in blocks]
            main_blocks = [b for b in blocks if b[3] == "main"]
            nmain = 128 * len(main_blocks)
            full = (mode == "main")
            kv = mode in ("main", "prekv")

            def tok_piece(c0, n, evac):
                wv, wb = wnext(W, 0, KC, c0, n)
                for (j, ntok, col0, kind) in blocks:
                    if K.plan:
                        continue
                    ps, psb = proj_tok(wv, wb, KC, col0, ntok, [XTb[j]], n)
                    evac(j, ntok, kind, ps, psb)
            if full:
                for pi in range(2):
                    tok_piece(pi * 256, 256, lambda j, ntok, kind, ps, psb, pi=pi: K.op(
                        "act", lambda e: e.activation(out=A["QKA"][j][0:ntok, pi * 256:(pi + 1) * 256], in_=ps[0:ntok, 0:256], func=AF.Copy,
                                                      scale=128.0 ** -0.5), R=[psb], W=[Ab["QKA"][j]]))
            for pi in range(2):
                tok_piece(512 + pi * 256, 256, lambda j, ntok, kind, ps, psb, pi=pi: K.op(
                    "dve", lambda e: e.tensor_copy(out=A["QKA"][j][0:ntok, 512 + pi * 256:512 + (pi + 1) * 256], in_=ps[0:ntok, 0:256]),
                    R=[psb], W=[Ab["QKA"][j]]))
            for pi in range(4):
                tok_piece(1024 + pi * 256, 256, lambda j, ntok, kind, ps, psb, pi=pi: K.op(
                    "act", lambda e: e.activation(func=AF.Copy, out=A["VA"][j][0:ntok, pi * 256:(pi + 1) * 256], in_=ps[0:ntok, 0:256]), R=[psb], W=[Ab["VA"][j]]))
            dbg0("m_va")
            wv, wb = wnext(W, 0, KC, 2048, 16)
            if not K.plan:
                ps, psb = getS()

                def fg(e, ps=ps, wv=wv):
                    ins = None
                    for k in range(KC):
                        ins = e.matmul(ps[0:16, 0:ncols], lhsT=wv[:, k, 0:16], rhs=XT[:, k, 0:ncols], start=(k == 0), stop=(k == KC - 1))
                    return ins
                K.op("pe", fg, R=[wb] + xtb, W=[psb])
                K.op("dve", lambda e, ps=ps: e.tensor_copy(out=A["GLT"][0:16, 0:ncols], in_=ps[0:16, 0:ncols]), R=[psb], W=[Ab["GLT"]])
                if True:
                    pass
                for (j, ntok, col0, kind) in blocks:
                    gps, gpsb = getS()

                    def fgate(e, gps=gps, ntok=ntok, col0=col0):
                        e.matmul(gps[0:ntok, 0:512], lhsT=A["GLT"][0:16, col0:col0 + ntok], rhs=WGU[0:16, :], start=True, stop=False)
                        return e.matmul(gps[0:ntok, 0:512], lhsT=ONES[0:1, 0:ntok], rhs=BG[0:1, :], start=False, stop=True)
                    K.op("pe", fgate, R=[Ab["GLT"], CONSTPb], W=[gpsb])
                    K.op("act", lambda e, gps=gps, ntok=ntok: e.activation(out=A["LTMP"][0:ntok, :], in_=gps[0:ntok, 0:512], func=AF.Exp, scale=-1.0),
                         R=[gpsb], W=[Ab["LTMP"]])
                    K.op("dve", lambda e, ntok=ntok: e.tensor_scalar(out=A["LTMP"][0:ntok, :], in0=A["LTMP"][0:ntok, :], scalar1=1.0, scalar2=None, op0=ALU.add),
                         R=[], W=[Ab["LTMP"]])
                    K.op("act", lambda e, j=j, ntok=ntok: e.activation(out=A["LOGG"][j][0:ntok, :], in_=A["LTMP"][0:ntok, :], func=AF.Ln),
                         R=[Ab["LTMP"]], W=[Ab["LOGG"][j]])
            dbg0("m_gate")
            if full:
                def ev_og(j, ntok, kind, ps, psb, pi):
                    dst = A["GOG"][j][0:ntok, pi * 256:(pi + 1) * 256]
                    K.op("act", lambda e: e.activation(out=dst, in_=ps[0:ntok, 0:256], func=AF.Silu), R=[psb], W=[Ab["GOG"][j]])
                    K.op("dve", lambda e: e.tensor_tensor(out=dst, in0=dst, in1=GN[0:ntok, pi * 256:(pi + 1) * 256], op=ALU.mult),
                         R=[CONSTb], W=[Ab["GOG"][j]])
                for pi in range(4):
                    tok_piece(2064 + pi * 256, 256, lambda j, ntok, kind, ps, psb, pi=pi: ev_og(j, ntok, kind, ps, psb, pi))
                for pi in range(4):
                    wv, wb = wnext(W, 0, KC, 3088 + pi * 256, 256)
                    if K.plan:
                        continue
                    for hh in range(2):
                        h = pi * 2 + hh
                        ps, psb = getS()

                        def fq(e, ps=ps, wv=wv, hh=hh):
                            ins = None
                            for k in range(KC):
                                ins = e.matmul(ps[:, 0:ncols], lhsT=wv[:, k, hh * 128:(hh + 1) * 128], rhs=XT[:, k, 0:ncols], start=(k == 0), stop=(k == KC - 1))
                            return ins
                        K.op("pe", fq, R=[wb] + xtb, W=[psb])
                        K.op("act", lambda e, ps=ps, h=h: e.activation(out=A["QT"][:, h, 0:ncols], in_=ps[:, 0:ncols], func=AF.Copy, scale=128.0 ** -0.5),
                             R=[psb], W=[Ab["QT"]])
            dbg0("m_qb")
            if kv:
                for pi in range(4):
                    wv, wb = wnext(W, 0, KC, 4112 + pi * 256, 256)
                    if K.plan:
                        continue
                    for hh in range(2):
                        h = pi * 2 + hh
                        ps, psb = getS()

                        def fk(e, ps=ps, wv=wv, hh=hh):
                            ins = None
                            for k in range(KC):
                                ins = e.matmul(ps[:, 0:ncols], lhsT=wv[:, k, hh * 128:(hh + 1) * 128], rhs=XT[:, k, 0:ncols], start=(k == 0), stop=(k == KC - 1))
                            return ins
                        K.op("pe", fk, R=[wb] + xtb, W=[psb])
                        if nmain:
                            K.op("dve", lambda e, ps=ps, h=h: e.tensor_copy(out=KT[:, h, 512:512 + nmain], in_=ps[:, 0:nmain]),
                                 R=[psb], W=[KTb[4 + b[0]] for b in main_blocks])
                        if sample_blk is not None:
                            K.op("dve", lambda e, ps=ps, h=h: e.tensor_copy(out=A["KTs"][:, h, :], in_=ps[:, nmain:nmain + 32]), R=[psb], W=[Ab["KTs"]])
                    for (j, ntok, col0, kind) in blocks:
                        if kout_rows.get(j) is not None:
                            ps, psb = proj_tok(wv, wb, KC, col0, ntok, [XTb[j]], 256)
                            dstd = (ks_out if kind == "sample" else kout)[kout_rows[j]:kout_rows[j] + ntok, pi * 256:(pi + 1) * 256]
                            out_rows(ps, psb, ntok, 256, dstd)
                dbg0("m_kb")
                def ev_v(j, ntok, kind, ps, psb, pi):
                    import os as _os4
                    src = ps[0:ntok, 0:256].rearrange("p (h d) -> p h d", d=128)
                    if _os4.environ.get("KDBG_SKIP") == "vcopy":
                        pass
                    else:
                        for hh in range(2):
                            h = pi * 2 + hh
                            s2 = ps[0:ntok, hh * 128:(hh + 1) * 128]
                            if kind == "main":
                                K.op("dve", lambda e, h=h, s2=s2: e.tensor_copy(out=VB[0:ntok, 4 + j, h, 0:128], in_=s2), R=[psb], W=[VBb[4 + j]])
                            else:
                                K.op("dve", lambda e, h=h, s2=s2: e.tensor_copy(out=A["VSN"][0:ntok, h, 0:128], in_=s2), R=[psb], W=[Ab["VSN"]])
                    if kout_rows.get(j) is not None:
                        dstd = (vs_out if kind == "sample" else vout)[kout_rows[j]:kout_rows[j] + ntok, pi * 256:(pi + 1) * 256]
                        out_rows(ps, psb, ntok, 256, dstd)
                for pi in range(4):
                    tok_piece(5136 + pi * 256, 256, lambda j, ntok, kind, ps, psb, pi=pi: ev_v(j, ntok, kind, ps, psb, pi))
                dbg0("m_vb")
                for (j, ntok, col0, kind) in blocks:
                    if kind == "main":
                        K.op("dve", lambda e, j=j: e.memset(VB[:, 4 + j, :, 128:129], 1.0), R=[], W=[VBb[4 + j]])
                        if mode == "prekv":
                            K.op("dve", lambda e, j=j: e.tensor_scalar(out=VB[:, 4 + j, :, 128:129], in0=VB[:, 4 + j, :, 128:129], scalar1=PV[:, 0:1],
                                                                      scalar2=None, op0=ALU.mult), R=[CONSTb], W=[VBb[4 + j]])
                    else:
                        K.op("dve", lambda e: e.memset(A["VSN"][:, :, 128:129], 1.0), R=[], W=[Ab["VSN"]])

        def ring_shift(nt):
            for s in range(4):
                K.op("dve", lambda e, s=s: e.tensor_copy(out=KT[:, :, s * 128:(s + 1) * 128], in_=KT[:, :, (s + nt) * 128:(s + nt + 1) * 128]),
                     R=[KTb[s + nt]], W=[KTb[s]])
                K.op("dve", lambda e, s=s: e.tensor_copy(out=VB[:, s, :, :], in_=VB[:, s + nt, :, :]), R=[VBb[s + nt]], W=[VBb[s]])

        def gla_scr(A, Ab):
            return (A["ENB"], A["EB"], A["KD"], A["QD"], A["QDTf"], A["QDT0"], A["QDT1"], A["KDT"], A["ATm"], A["EBLt"], A["SQ"], Ab)

        class _Stop(Exception):
            pass

        def dbg(tag, j):
            import os as _os2
            if _os2.environ.get("KDBG_STOP") == tag:
                if not K.plan:
                    K.dma("sp", [(y_s, X[j][0:32, :])], Xb[j], write=False)
                    K.finish()
                raise _Stop()

        def dbg0(tag):
            import os as _os3
            if _os3.environ.get("KDBG_STOP") == tag:
                if not K.plan:
                    K.finish()
                raise _Stop()

        def program():
            try:
                program_()
            except _Stop:
                pass

        def program_():
            st.update(iS=0, iT=0, iG=0, wi=0, wissued=0, mp=-1, mpk=0)
            if K.plan:
                st["pieces"] = []
                st["scr_off"] = []
                st["scr_total"] = 0
                st["store_dep"] = {}
            cl = [(BT[:, :, :, :].rearrange("p a h q -> p (a h q)"), c_bt), (BTS3[:, :, :].rearrange("p h q -> p (h q)"), c_bts3),
                  (BTS4[:, :, :].rearrange("p h q -> p (h q)"), c_bts4), (CBIAS[:, :], c_cbias), (TRI2[:, :], c_tri2), (SEL[:, :], c_sel),
                  (MASKT[:, :], c_maskt), (GN[:, :], c_gn), (CW[:, :, :, :].rearrange("p l c t -> p (l c t)"), c_cw),
                  (CBc[:, :, :].rearrange("p l c -> p (l c)"), c_cb), (BSP[:, :], c_bsp), (PV[:, :], c_pv)]
            K.dma("sp", cl, CONSTb, write=True)
            K.dma("pool", [(IDENT[:, :], c_ident), (WGU[:, :], c_wgu), (BG[:, :], c_bg), (ONES[:, :], c_ones)], CONSTPb, write=True)
            K.dma("pool", [(WST[:, :, :].rearrange("p g t -> p (g t)"), c_wst)], WSTb, write=True)
            K.dma("sp", [(SS[:, :, :].rearrange("p h v -> p (h v)"), sg.rearrange("p h v -> p (h v)"))], SSb, write=True)
            K.dma("sp", [(CHS[:, :, :, :].rearrange("p l c t -> p (l c t)"), chs_in)], CHSb, write=True)
            K.op("dve", lambda e: e.memset(S[:, :, :], 0.0), R=[], W=[Sb])
            K.op("dve", lambda e: e.memset(SBf[:, :, :], 0.0), R=[], W=[SBfb])
            K.op("dve", lambda e: e.memset(CH[:, :, :, :], 0.0), R=[], W=[CHb])
            K.op("dve", lambda e: e.memset(KT[:, :, :], 0.0), R=[], W=KTb)
            K.op("dve", lambda e: e.memset(VB[:, :, :, :], 0.0), R=[], W=VBb)
            K.op("act", lambda e: e.activation(out=SSBf[:, :, :].rearrange("p h v -> p (h v)"), in_=SS[:, :, :].rearrange("p h v -> p (h v)"), func=AF.Copy), R=[SSb], W=[SSBfb])

            dbg0("consts")
            A, Ab = carveA()
            pre_passes = []
            t = 0
            first = NPRE % NTM if NPRE % NTM else NTM
            sizes = [first] + [NTM] * ((NPRE - first) // NTM)
            for sz in sizes:
                pre_passes.append(list(range(t, t + sz)))
                t += sz
            import os as _os
            _lim = _os.environ.get("KDBG_PASSES")
            if _lim:
                a_, b_ = [int(v) for v in _lim.split(",")]
                pre_passes = pre_passes[len(pre_passes) - a_:] if a_ else []
            for blks in pre_passes:
                blocks = [(i, 128, i * 128, "main") for i in range(len(blks))]
                ncols = 128 * len(blks)
                kvp = blks[0] >= NPRE - 4
                for (j, ntok, col0, kind), gb in zip(blocks, blks):
                    K.dma("pool", [(XB[:, :], xpre[gb * 128:(gb + 1) * 128, :])], XBb, write=True)
                    to_T(XB, XBb, 128, col0, [XTb[j]])
                mixer0(blocks, ncols, A, Ab, "prekv" if kvp else "pre", {}, None)
                for (j, ntok, col0, kind) in blocks:
                    for _ in gla_block(128, 2, A["QKA"][j], Ab["QKA"][j], A["VA"][j], Ab["VA"][j], A["LOGG"][j], Ab["LOGG"][j],
                                       S, Sb, [SBf, SBf2], [SBfb, SBf2b], False, None, None, None, None, gla_scr(A, Ab)):
                        pass
                if kvp:
                    ring_shift(len(blks))
            K.op("act", lambda e: e.activation(out=SBf[:, :, :].rearrange("p h v -> p (h v)"), in_=S[:, :, :].rearrange("p h v -> p (h v)"), func=AF.Copy), R=[Sb], W=[SBfb])
            dbg0("init")
            K.barrier()
            dbg0("init2")

            main_passes = []
            t = 0
            while t < NMAIN:
                sz = min(NTM, NMAIN - t)
                main_passes.append(list(range(t, t + sz)))
                t += sz
            if _lim:
                main_passes = main_passes[len(main_passes) - b_:] if b_ else []
            for pidx, blks in enumerate(main_passes):
                st["mp"] = pidx
                st["mpk"] = 0
                last = (pidx == len(main_passes) - 1)
                has_sample = last and len(blks) < NTM
                assert last is False or has_sample
                nmb = len(blks)
                blocks4 = [(i, 128, i * 128, "main") for i in range(nmb)]
                if has_sample:
                    blocks4.append((nmb, 32, nmb * 128, "sample"))
                blocks = [(j, ntok, col0) for (j, ntok, col0, kind) in blocks4]
                ncols = nmb * 128 + (32 if has_sample else 0)
                kout_rows = {}
                for i, gb in enumerate(blks):
                    if gb >= NMAIN - 4:
                        kout_rows[i] = (gb - (NMAIN - 4)) * 128
                if has_sample:
                    kout_rows[nmb] = 0
                A, Ab = carveA()
                for i, gb in enumerate(blks):
                    K.dma("sp", [(X[i][:, :], xm[gb * 128:(gb + 1) * 128, :])], Xb[i], write=True)
                if has_sample:
                    K.dma("sp", [(X[nmb][0:32, :], xs)], Xb[nmb], write=True)
                dbg0("xload")
                cast_T(0, 128, 0)
                dbg0("xT0")
                for (j, ntok, col0) in blocks[1:]:
                    cast_T(j, ntok, col0)
                dbg0("xT")
                mixer0(blocks4, ncols, A, Ab, "main", kout_rows, (blocks4[-1] if has_sample else None))
                dbg0("mixer0")
                def gla_chain():
                    for (j, ntok, col0, kind) in blocks4:
                        if kind == "main":
                            yield from gla_block(128, 2, A["QKA"][j], Ab["QKA"][j], A["VA"][j], Ab["VA"][j], A["LOGG"][j], Ab["LOGG"][j],
                                                 S, Sb, [SBf, SBf2], [SBfb, SBf2b], True, A["MIX"][j], Ab["MIX"][j], A["GOG"][j], Ab["GOG"][j], gla_scr(A, Ab))
                        else:
                            yield from gla_block(32, 1, A["QKA"][j], Ab["QKA"][j], A["VA"][j], Ab["VA"][j], A["LOGG"][j], Ab["LOGG"][j],
                                                 SS, SSb, [SSBf], [SSBfb], True, A["MIX"][j], Ab["MIX"][j], A["GOG"][j], Ab["GOG"][j], gla_scr(A, Ab))

                def attn_chain():
                    hi = 0
                    for (j, ntok, col0, kind) in blocks4:
                        for h in range(8):
                            PTt, PTb = A["PT"][hi % 2], Ab["PT"][hi % 2]
                            hi += 1
                            if kind == "main":
                                bl = []
                                for d in range(5):
                                    s_ = j + d
                                    bias = None
                                    if d == 0:
                                        bias = BT[:, 0, h, :]
                                    elif d == 3:
                                        bias = BT[:, 1, h, :]
                                    elif d == 4:
                                        bias = BT[:, 2, h, :]
                                    bl.append((KT[:, h, s_ * 128:(s_ + 1) * 128], KTb[s_], VB[:, s_, h, 0:129], VBb[s_], 128, bias))
                                attn_head(h, 128, A["QT"][:, h, col0:col0 + 128], Ab["QT"], bl, PTt, PTb, A["TMP"], Ab["TMP"],
                                          A["MIX"][j][0:128, 1024 + h * 128:1024 + (h + 1) * 128], Ab["MIX"][j])
                            else:
                                ksh, kshb = A["KSh"][h % 2], Ab["KSh"][h % 2]
                                ktsh, ktshb = A["KTSh"][h % 2], Ab["KTSh"][h % 2]
                                vsh, vshb = A["VSh"][h % 2], Ab["VSh"][h % 2]
                                K.dma("pool", [(ksh[:, :, :], ck[:, h * 128:(h + 1) * 128].rearrange("(b p) d -> p b d", p=128))], kshb, write=True)
                                K.dma("pool", [(vsh[:, :, 0:128], cv[:, h * 128:(h + 1) * 128].rearrange("(b p) d -> p b d", p=128))], vshb, write=True)
                                K.op("dve", lambda e, vsh=vsh: e.memset(vsh[:, :, 128:129], 1.0), R=[], W=[vshb])
                                pt, ptb = getT()

                                def ftk(e, pt=pt, ksh=ksh):
                                    ins = None
                                    for b in range(4):
                                        ins = e.transpose(out=pt[:, b * 128:(b + 1) * 128], in_=ksh[:, b, :], identity=IDENT[:, :])
                                    return ins
                                K.op("pe", ftk, R=[kshb, CONSTPb], W=[ptb])
                                K.op("dve", lambda e, pt=pt, ktsh=ktsh: e.tensor_copy(out=ktsh[:, 0:512], in_=pt[:, 0:512]), R=[ptb], W=[ktshb])
                                bl = []
                                for b in range(4):
                                    bias = BTS3[:, h, :] if b == 3 else None
                                    bl.append((ktsh[:, b * 128:(b + 1) * 128], ktshb, vsh[:, b, 0:129], vshb, 128, bias))
                                bl.append((A["KTs"][:, h, :], Ab["KTs"], A["VSN"][0:32, h, 0:129], Ab["VSN"], 32, BTS4[:, h, :]))
                                attn_head(h, 32, A["QT"][:, h, col0:col0 + 32], Ab["QT"], bl, PTt, PTb, A["TMP"], Ab["TMP"],
                                          A["MIX"][j][0:32, 1024 + h * 128:1024 + (h + 1) * 128], Ab["MIX"][j])
                            yield

                gl_, at_ = gla_chain(), attn_chain()
                live = [gl_, at_]
                while live:
                    for g_, nstep in ((gl_, 2), (at_, 1)):
                        if g_ not in live:
                            continue
                        for _ in range(nstep):
                            try:
                                next(g_)
                            except StopIteration:
                                live.remove(g_)
                                break
                dbg0("attn")
                ring_shift(nmb)
                if last:
                    K.dma("sp", [(gla_out, S[:, :, :].rearrange("p h v -> p (h v)"))], Sb, write=False)
                    K.dma("sp", [(gla_s, SS[:, :, :].rearrange("p h v -> p (h v)"))], SSb, write=False)
                out_proj(w_out_even, lambda j: (A["MIX"][j], Ab["MIX"][j]), blocks)
                if has_sample:
                    dbg("preln1", nmb)
                load_ln(0)
                for (j, ntok, col0) in blocks:
                    layer_norm(X[j], Xb[j], ntok)
                if has_sample:
                    dbg("x1", nmb)
                K.barrier()
                segs = []
                if nmb:
                    segs.append((0, nmb * 128, CH, CHb))
                if has_sample:
                    segs.append((nmb * 128, 32, CHS, CHSb))
                ffn(0, blocks, segs, ncols)
                if has_sample:
                    dbg("preln2", nmb)
                load_ln(2)
                for (j, ntok, col0) in blocks:
                    layer_norm(X[j], Xb[j], ntok)
                K.barrier()
                U1 = [SCRB[:, i * 2048:(i + 1) * 2048] for i in range(NTM)]
                VN = [SCRB[:, (NTM + i) * 2048:(NTM + i + 1) * 2048] for i in range(NTM)]
                V1 = [SCRF[:, i * 2048:(i + 1) * 2048] for i in range(NTM)]
                U1b = [Buf("U1%d" % i) for i in range(NTM)]
                VNb = [Buf("VN%d" % i) for i in range(NTM)]
                V1b = [Buf("V1%d" % i) for i in range(NTM)]
                for (j, ntok, col0) in blocks:
                    cast_T(j, ntok, col0)
                for pi in range(16):
                    wv, wb = wnext(w_in_odd, 0, KC, pi * 256, 256)
                    if K.plan:
                        continue
                    for (j, ntok, col0) in blocks:
                        ps, psb = proj_tok(wv, wb, KC, col0, ntok, [XTb[j]], 256)
                        if pi < 8:
                            K.op("act", lambda e, ps=ps, j=j, ntok=ntok, pi=pi: e.activation(
                                out=U1[j][0:ntok, pi * 256:(pi + 1) * 256], in_=ps[0:ntok, 0:256], func=AF.Gelu_apprx_tanh), R=[psb], W=[U1b[j]])
                        else:
                            K.op("act", lambda e, ps=ps, j=j, ntok=ntok, pi=pi: e.activation(
                                out=V1[j][0:ntok, (pi - 8) * 256:(pi - 7) * 256], in_=ps[0:ntok, 0:256], func=AF.Gelu_apprx_tanh), R=[psb], W=[V1b[j]])
                load_ln(8)
                for (j, ntok, col0, kind) in blocks4:
                    layer_norm(V1[j], V1b[j], ntok)
                    K.op("act", lambda e, j=j, ntok=ntok: e.activation(func=AF.Copy, out=VN[j][0:ntok, :], in_=V1[j][0:ntok, :]), R=[V1b[j]], W=[VNb[j]])
                    if kind == "sample":
                        for q in range(8):
                            sg_, sgb = getSTG()
                            K.op("dve", lambda e, sg_=sg_, q=q, j=j: e.tensor_copy(out=sg_[0:32, :], in_=V1[j][0:32, q * 256:(q + 1) * 256]), R=[V1b[j]], W=[sgb])
                            K.dma("sp", [(mlpv_s[:, q * 256:(q + 1) * 256], sg_[0:32, :])], sgb, write=False)
                    for gp in range(4):
                        ps, psb = getS()

                        def fsp(e, ps=ps, gp=gp, j=j, ntok=ntok):
                            ins = None
                            for gg in range(2):
                                g = gp * 2 + gg
                                ins = e.matmul(ps[0:ntok, gg * 256:(gg + 1) * 256], lhsT=WST[0:ntok, g, 0:ntok], rhs=VN[j][0:ntok, g * 256:(g + 1) * 256],
                                               start=True, stop=True)
                            return ins
                        K.op("pe", fsp, R=[WSTb, VNb[j]], W=[psb])
                        for gg in range(2):
                            g = gp * 2 + gg
                            K.op("dve", lambda e, ps=ps, g=g, gg=gg, j=j, ntok=ntok: e.scalar_tensor_tensor(
                                out=U1[j][0:ntok, g * 256:(g + 1) * 256], in0=ps[0:ntok, gg * 256:(gg + 1) * 256], scalar=BSP[0:ntok, g:g + 1],
                                in1=U1[j][0:ntok, g * 256:(g + 1) * 256], op0=ALU.add, op1=ALU.mult), R=[psb, CONSTb], W=[U1b[j]])
                out_proj(w_out_odd, lambda j: (U1[j], U1b[j]), blocks)
                load_ln(4)
                for (j, ntok, col0) in blocks:
                    layer_norm(X[j], Xb[j], ntok)
                K.barrier()
                ffn(1, blocks, segs, ncols)
                load_ln(6)
                for (j, ntok, col0, kind) in blocks4:
                    layer_norm(X[j], Xb[j], ntok)
                    if kind == "main":
                        gb = blks[j]
                        K.dma("sp", [(y_main[gb * 128:(gb + 1) * 128, :], X[j][:, :])], Xb[j], write=False)
                    else:
                        K.dma("sp", [(y_s, X[j][0:32, :])], Xb[j], write=False)
                K.barrier()
            K.dma("sp", [(conv_out, CH[:, :, :, :].rearrange("p l c t -> p (l c t)"))], CHb, write=False)
            K.dma("sp", [(conv_s, CHS[:, :, :, :].rearrange("p l c t -> p (l c t)"))], CHSb, write=False)
            K.finish()

        K.plan = True
        program()
        K.plan = False
        K.reset()
        st["scratch"] = nc.dram_tensor("wscratch", [128, max(1, st["scr_total"])], BF16, kind="Internal").ap()
        program()
        K.emit()
    return nc


_NC_CACHE = {}


def _rel_tiles(table):
    H = table.shape[0]
    k = np.arange(128)[:, None]
    q = np.arange(128)[None, :]
    bt = np.zeros((128, 3, H, 128), np.float32)
    for a, d in enumerate((0, 3, 4)):
        kp = (d - 4) * 128 + k
        rel = np.clip(q - kp, -128, 128) + 128
        vis = (np.floor_divide(kp, 64) >= np.floor_divide(q, 64) - 8) & (np.floor_divide(kp, 64) <= np.floor_divide(q, 64))
        for h in range(H):
            bt[:, a, h, :] = np.where(vis, table[h][rel], np.float32(-30000.0))
    t = np.arange(32)[None, :]
    i = np.arange(128)[:, None]
    bts3 = np.stack([table[h][np.clip(t + 128 - i, -128, 128) + 128] for h in range(H)], axis=1).astype(np.float32)
    s = np.arange(32)[:, None]
    bts4 = np.stack([table[h][np.clip(t - s, -128, 128) + 128] for h in range(H)], axis=1).astype(np.float32)
    cbias = np.broadcast_to(table[:, 256][None, :], (128, H)).astype(np.float32)
    return bt, bts3, bts4, cbias


def _prep(x_prompt, x_sample, cache_attn_k, cache_attn_v, state_gla, state_ffn_conv,
          w_in_even, w_gate_up, b_gate, gla_norm_g, rel_bias, w_out_even,
          w_in_odd, ln_v_g, ln_v_b, w_spatial, b_spatial, w_out_odd,
          ffn_w1, ffn_w2, ffn_conv_w, ffn_conv_b, ffn_w3,
          ln1_g, ln1_b, ln2_g, ln2_b):
    f = lambda a: np.ascontiguousarray(np.asarray(a, dtype=np.float32))
    x_prompt = f(x_prompt); x_sample = f(x_sample)

    s_ = np.arange(128)[:, None]
    t_ = np.arange(128)[None, :]
    same = (s_ // 64) == (t_ // 64)
    causal = s_ <= t_
    tri2 = np.where(same & causal, np.float32(-1.0 / 16.0), np.float32(0.0)).astype(np.float32)
    sel = np.where((s_ // 64) == np.arange(2)[None, :], np.float32(-1.0 / 16.0), np.float32(0.0)).astype(np.float32)
    maskt = np.where(same & causal, np.float32(1.0), np.float32(0.0)).astype(np.float32)
    bt, bts3, bts4, cbias = _rel_tiles(f(rel_bias)[0])
    lnp = np.stack([np.broadcast_to(f(v)[None, :], (128, D)) for v in
                    (ln1_g[0], ln1_b[0], ln2_g[0], ln2_b[0], ln1_g[1], ln1_b[1], ln2_g[1], ln2_b[1], ln_v_g[0], ln_v_b[0])]).astype(np.float32)
    gn = np.ascontiguousarray(np.broadcast_to(f(gla_norm_g)[0].reshape(1, 1024), (128, 1024)))
    cw = np.ascontiguousarray(f(ffn_conv_w).reshape(2, 3, FC, 128).transpose(3, 0, 2, 1)).reshape(128, 2 * FC * 3)
    cb = np.ascontiguousarray(f(ffn_conv_b).reshape(2, FC, 128).transpose(2, 0, 1)).reshape(128, 2 * FC)
    ws = f(w_spatial)[0]
    wst = np.where(causal[:, None, :], ws.transpose(2, 0, 1), np.float32(0.0)).astype(np.float32)
    wst = np.ascontiguousarray(wst).reshape(128, 8 * 128)
    bsp = np.ascontiguousarray(f(b_spatial)[0].T)
    common = dict(
        w_in_even=f(w_in_even)[0], w_out_even=f(w_out_even)[0], w_in_odd=f(w_in_odd)[0], w_out_odd=f(w_out_odd)[0],
        ffn_w1=f(ffn_w1), ffn_w2=f(ffn_w2), ffn_w3=f(ffn_w3), lnp=lnp,
        c_ident=np.eye(128, dtype=np.float32), c_tri2=tri2, c_sel=sel, c_maskt=maskt, c_gn=gn,
        c_wgu=f(w_gate_up)[0], c_bg=f(b_gate)[0].reshape(1, 512), c_ones=np.ones((1, 128), np.float32),
        c_cw=cw, c_cb=cb, c_wst=wst, c_bsp=bsp, c_bt=bt.reshape(128, -1), c_bts3=bts3.reshape(128, -1),
        c_bts4=bts4.reshape(32, -1), c_cbias=cbias)
    in_maps = []
    SPLIT = NMAIN * 128
    START1 = 4096 - SPLIT
    for c in range(8):
        b, half = c // 2, c % 2
        if half == 0:
            xm = x_prompt[b, 0:SPLIT]
            xpre = np.zeros((NPRE * 128, D), np.float32)
            pv = np.zeros((128, 1), np.float32)
        else:
            xm = x_prompt[b, START1:4096]
            xpre = x_prompt[b, 0:START1]
            pv = np.ones((128, 1), np.float32)
        m = dict(common)
        m.update(xm=np.ascontiguousarray(xm), xpre=np.ascontiguousarray(xpre), c_pv=pv, xs=x_sample[c],
                 ck=f(cache_attn_k)[0, c].reshape(512, 1024), cv=f(cache_attn_v)[0, c].reshape(512, 1024),
                 sg=np.ascontiguousarray(f(state_gla)[0, c].transpose(1, 0, 2)),
                 chs=np.ascontiguousarray(f(state_ffn_conv)[:, c].reshape(2, 2, FC, 128).transpose(3, 0, 2, 1)).reshape(128, 2 * FC * 2))
        in_maps.append(m)
    return in_maps


def kernel(x_prompt, x_sample, cache_attn_k, cache_attn_v, state_gla, state_ffn_conv,
           w_in_even, w_gate_up, b_gate, gla_norm_g, rel_bias, w_out_even,
           w_in_odd, ln_v_g, ln_v_b, w_spatial, b_spatial, w_out_odd,
           ffn_w1, ffn_w2, ffn_conv_w, ffn_conv_b, ffn_w3,
           ln1_g, ln1_b, ln2_g, ln2_b):
    in_maps = _prep(x_prompt, x_sample, cache_attn_k, cache_attn_v, state_gla, state_ffn_conv,
                    w_in_even, w_gate_up, b_gate, gla_norm_g, rel_bias, w_out_even,
                    w_in_odd, ln_v_g, ln_v_b, w_spatial, b_spatial, w_out_odd,
                    ffn_w1, ffn_w2, ffn_conv_w, ffn_conv_b, ffn_w3,
                    ln1_g, ln1_b, ln2_g, ln2_b)
    if "nc" not in _NC_CACHE:
        _NC_CACHE["nc"] = build()
    nc = _NC_CACHE["nc"]
    SPLIT = NMAIN * 128
    START1 = 4096 - SPLIT
    res = run_bass_kernel_spmd(nc, in_maps, core_ids=list(range(8)))
    R = res.results
    B = 4
    y_prompt = np.zeros((B, 4096, D), np.float32)
    k_p = np.zeros((1, B, 512, 8, 128), np.float32)
    v_p = np.zeros((1, B, 512, 8, 128), np.float32)
    gla_p = np.zeros((1, B, 4, 128, 256), np.float32)
    conv_p = np.zeros((2, B, 2, DFF), np.float32)
    y_sample = np.zeros((8, 32, D), np.float32)
    k_s = np.zeros((1, 8, 32, 8, 128), np.float32)
    v_s = np.zeros((1, 8, 32, 8, 128), np.float32)
    gla_sm = np.zeros((1, 8, 4, 128, 256), np.float32)
    conv_sm = np.zeros((2, 8, 2, DFF), np.float32)
    mlp_v = np.zeros((1, 8, 32, D), np.float32)

    def conv_unlay(a):
        return a.reshape(128, 2, FC, 2).transpose(1, 3, 2, 0).reshape(2, 2, DFF)
    for c in range(8):
        b, half = c // 2, c % 2
        r = R[c]
        if half == 0:
            y_prompt[b, 0:SPLIT] = r["y_main"]
        else:
            y_prompt[b, SPLIT:4096] = r["y_main"][SPLIT - START1:]
            k_p[0, b] = r["kout"].reshape(512, 8, 128)
            v_p[0, b] = r["vout"].reshape(512, 8, 128)
            gla_p[0, b] = r["gla_out"].reshape(128, 4, 256).transpose(1, 0, 2)
            conv_p[:, b] = conv_unlay(r["conv_out"])
        y_sample[c] = r["y_s"]
        k_s[0, c] = r["ks_out"].reshape(32, 8, 128)
        v_s[0, c] = r["vs_out"].reshape(32, 8, 128)
        gla_sm[0, c] = r["gla_s"].reshape(128, 4, 256).transpose(1, 0, 2)
        conv_sm[:, c] = conv_unlay(r["conv_s"])
        mlp_v[0, c] = r["mlpv_s"]
    return (y_prompt, y_sample, k_p, v_p, gla_p, conv_p, k_s, v_s, gla_sm, conv_sm, mlp_v)
```

```python
import numpy as np
import concourse.bass as bass
import concourse.mybir as mybir
from concourse.bass_utils import run_bass_kernel_spmd
from contextlib import ExitStack

F32 = mybir.dt.float32
BF16 = mybir.dt.bfloat16
AF = mybir.ActivationFunctionType
ALU = mybir.AluOpType
AX = mybir.AxisListType

D = 2048
KC = 16
DFF = 5504
FC = 43
NMAIN = 17
NPRE = 15
NTM = 2
NCMAX = NTM * 128
ALPHA = 4.0 ** 0.25
EPS = 1e-5
WCOL = 256
NSLOT = 4
SAFE_SAME_ENGINE = True


class Buf:
    __slots__ = ("name", "w", "r", "dsem", "dcnt")

    def __init__(self, name):
        self.name = name
        self.w = None
        self.r = {}
        self.dsem = None
        self.dcnt = 0


class Ker:
    ENG = {"pe": "tensor", "act": "scalar", "dve": "vector", "pool": "gpsimd", "sp": "sync"}

    def __init__(self, nc, es):
        self.nc = nc
        self.es = es
        self.plan = False
        self.sem = {e: es.enter_context(nc.semaphore("sem_" + e)) for e in self.ENG}
        self.dsems = []
        self.reset()

    def reset(self):
        self.q = {e: [] for e in self.ENG}
        self.cnt = {e: 0 for e in self.ENG}
        self.seen = {e: {} for e in self.ENG}
        self.dbufs = []

    def _wait(self, eng, deps):
        own = self.sem[eng]
        for sem, val in deps:
            if sem is own and (eng == "pe" or not SAFE_SAME_ENGINE):
                continue
            if self.seen[eng].get(sem, 0) < val:
                self.seen[eng][sem] = val
                self.q[eng].append(lambda e, s=sem, v=val: e.wait_ge(s, v))

    @staticmethod
    def _deps(R, W):
        deps = []
        for b in R:
            if b.w is not None:
                deps.append(b.w)
        for b in W:
            if b.w is not None:
                deps.append(b.w)
            deps.extend(b.r.items())
        return deps

    def op(self, eng, fn, R=(), W=()):
        if self.plan:
            return
        self._wait(eng, self._deps(R, W))
        self.cnt[eng] += 1
        c = self.cnt[eng]
        sem = self.sem[eng]
        self.q[eng].append(lambda e, f=fn, s=sem: f(e).then_inc(s, 1))
        for b in W:
            b.w = (sem, c)
            b.r = {}
        for b in R:
            if b not in W:
                b.r[sem] = c

    def dma(self, eng, pairs, buf, write, extraR=(), deps=()):
        if self.plan:
            return
        key = "w" if write else "r"
        if buf.dsem is None:
            buf.dsem = {}
            buf.dcnt = {}
        if key not in buf.dsem:
            buf.dsem[key] = self.es.enter_context(self.nc.semaphore("d%s_%s" % (key, buf.name)))
            buf.dcnt[key] = 0
        if buf not in self.dbufs:
            self.dbufs.append(buf)
        R = list(extraR) + ([] if write else [buf])
        W = [buf] if write else []
        self._wait(eng, self._deps(R, W) + list(deps))
        sem = buf.dsem[key]
        for (o, i) in pairs:
            self.q[eng].append(lambda e, o=o, i=i, s=sem: e.dma_start(out=o, in_=i).then_inc(s, 16))
        buf.dcnt[key] += 16 * len(pairs)
        if write:
            buf.w = (sem, buf.dcnt[key])
            buf.r = {}
        else:
            buf.r[sem] = buf.dcnt[key]
        for b in extraR:
            b.r[sem] = buf.dcnt[key]
        return (sem, buf.dcnt[key])

    def barrier(self, engs=("pe", "act", "dve")):
        if self.plan:
            return
        for e in engs:
            deps = [(self.sem[f], self.cnt[f]) for f in engs if f != e and self.cnt[f] > 0]
            self._wait(e, deps)

    def finish(self):
        deps = []
        for b in self.dbufs:
            for k_ in b.dsem:
                deps.append((b.dsem[k_], b.dcnt[k_]))
        for e in ("pe", "act", "dve"):
            if self.cnt[e] > 0:
                deps.append((self.sem[e], self.cnt[e]))
        self._wait("sp", deps)

    def emit(self):
        block = self.es.enter_context(self.nc.Block())
        for eng, attr in self.ENG.items():
            lst = self.q[eng]

            def body(e, lst=lst):
                for f in lst:
                    f(e)
            getattr(block, attr)(body)


def build():
    nc = bass.Bass("TRN2", target_bir_lowering=False)

    def din(name, shape):
        return nc.dram_tensor(name, list(shape), F32, kind="ExternalInput").ap()

    def dout(name, shape):
        return nc.dram_tensor(name, list(shape), F32, kind="ExternalOutput").ap()

    xm = din("xm", [NMAIN * 128, D])
    xpre = din("xpre", [NPRE * 128, D])
    xs = din("xs", [32, D])
    ck = din("ck", [512, 1024])
    cv = din("cv", [512, 1024])
    sg = din("sg", [128, 4, 256])
    chs_in = din("chs", [128, 2 * FC * 2])
    w_in_even = din("w_in_even", [D, 6160])
    w_out_even = din("w_out_even", [D, D])
    w_in_odd = din("w_in_odd", [D, 2 * D])
    w_out_odd = din("w_out_odd", [D, D])
    ffn_w1 = din("ffn_w1", [2, D, DFF])
    ffn_w2 = din("ffn_w2", [2, D, DFF])
    ffn_w3 = din("ffn_w3", [2, DFF, D])
    lnp = din("lnp", [10, 128, D])
    c_ident = din("c_ident", [128, 128])
    c_tri2 = din("c_tri2", [128, 128])
    c_sel = din("c_sel", [128, 2])
    c_maskt = din("c_maskt", [128, 128])
    c_gn = din("c_gn", [128, 1024])
    c_wgu = din("c_wgu", [16, 512])
    c_bg = din("c_bg", [1, 512])
    c_ones = din("c_ones", [1, 128])
    c_cw = din("c_cw", [128, 2 * FC * 3])
    c_cb = din("c_cb", [128, 2 * FC])
    c_wst = din("c_wst", [128, 8 * 128])
    c_bsp = din("c_bsp", [128, 8])
    c_pv = din("c_pv", [128, 1])
    c_bt = din("c_bt", [128, 3 * 8 * 128])
    c_bts3 = din("c_bts3", [128, 8 * 32])
    c_bts4 = din("c_bts4", [32, 8 * 32])
    c_cbias = din("c_cbias", [128, 8])

    y_main = dout("y_main", [NMAIN * 128, D])
    y_s = dout("y_s", [32, D])
    kout = dout("kout", [512, 1024])
    vout = dout("vout", [512, 1024])
    gla_out = dout("gla_out", [128, 1024])
    conv_out = dout("conv_out", [128, 2 * FC * 2])
    ks_out = dout("ks_out", [32, 1024])
    vs_out = dout("vs_out", [32, 1024])
    gla_s = dout("gla_s", [128, 1024])
    conv_s = dout("conv_s", [128, 2 * FC * 2])
    mlpv_s = dout("mlpv_s", [32, D])

    es = ExitStack()
    with es:
        def sb(name, shape, dt):
            return es.enter_context(nc.sbuf_tensor(name, list(shape), dt))

        X = [sb("X%d" % j, [128, D], F32) for j in range(NTM)]
        Xb = [Buf("X%d" % j) for j in range(NTM)]
        XB = sb("XB", [128, D], BF16); XBb = Buf("XB")
        XT = sb("XT", [128, KC, NCMAX], BF16); XTb = [Buf("XT%d" % j) for j in range(NTM)]
        NRING = 4 + NTM
        KT = sb("KT", [128, 8, NRING * 128], BF16); KTb = [Buf("KT%d" % s) for s in range(NRING)]
        VB = sb("VB", [128, NRING, 8, 130], BF16); VBb = [Buf("VB%d" % s) for s in range(NRING)]
        S = sb("S", [128, 4, 256], F32); Sb = Buf("S")
        SBf = sb("SBf", [128, 4, 256], BF16); SBfb = Buf("SBf")
        SBf2 = sb("SBf2", [128, 4, 256], BF16); SBf2b = Buf("SBf2")
        SS = sb("SS", [128, 4, 256], F32); SSb = Buf("SS")
        SSBf = sb("SSBf", [128, 4, 256], BF16); SSBfb = Buf("SSBf")
        STMP = sb("STMP", [128, 256], F32); STMPb = Buf("STMP")
        WS = [sb("WS%d" % i, [128, KC * WCOL], BF16) for i in range(NSLOT)]
        WSb = [Buf("WS%d" % i) for i in range(NSLOT)]
        LNG = sb("LNG", [128, D], F32); LNGb = Buf("LNG")
        LNB = sb("LNB", [128, D], F32); LNBb = Buf("LNB")
        BT = sb("BT", [128, 3, 8, 128], F32)
        BTS3 = sb("BTS3", [128, 8, 32], F32)
        BTS4 = sb("BTS4", [32, 8, 32], F32)
        CBIAS = sb("CBIAS", [128, 8], F32)
        IDENT = sb("IDENT", [128, 128], BF16)
        TRI2 = sb("TRI2", [128, 128], F32)
        SEL = sb("SEL", [128, 2], F32)
        MASKT = sb("MASKT", [128, 128], F32)
        GN = sb("GN", [128, 1024], F32)
        WGU = sb("WGU", [16, 512], BF16)
        BG = sb("BG", [1, 512], BF16)
        ONES = sb("ONES", [1, 128], BF16)
        CW = sb("CW", [128, 2, FC, 3], F32)
        CBc = sb("CBc", [128, 2, FC], F32)
        WST = sb("WST", [128, 8, 128], BF16); WSTb = Buf("WST")
        BSP = sb("BSP", [128, 8], F32)
        PV = sb("PV", [128, 1], F32)
        CH = sb("CH", [128, 2, FC, 2], F32); CHb = Buf("CH")
        CHS = sb("CHS", [128, 2, FC, 2], F32); CHSb = Buf("CHS")
        STG = [sb("STG%d" % i, [128, 256], F32) for i in range(4)]
        STGb = [Buf("STG%d" % i) for i in range(4)]
        SMALL = sb("SMALL", [128, 64], F32); SMALLb = Buf("SMALL")
        CONSTb = Buf("CONST")
        CONSTPb = Buf("CONSTP")

        NB = 21900
        NF = 4096
        SCRB = sb("SCRB", [128, NB], BF16)
        SCRF = sb("SCRF", [128, NF], F32)

        psS = [es.enter_context(nc.psum_tensor("psS%d" % i, [128, 512], F32)) for i in range(4)]
        psSb = [Buf("psS%d" % i) for i in range(4)]
        psL = [es.enter_context(nc.psum_tensor("psL%d" % i, [128, 512], F32)) for i in range(2)]
        psLb = [Buf("psL%d" % i) for i in range(2)]
        psT = [es.enter_context(nc.psum_tensor("psT%d" % i, [128, 1024], BF16)) for i in range(2)]
        psTb = [Buf("psT%d" % i) for i in range(2)]

        K = Ker(nc, es)
        st = {}

        def getS():
            i = st["iS"] % 4
            st["iS"] += 1
            return psS[i], psSb[i]

        def getT():
            i = st["iT"] % 2
            st["iT"] += 1
            return psT[i], psTb[i]

        def getSTG():
            i = st["iG"] % 4
            st["iG"] += 1
            return STG[i], STGb[i]

        def wnext(wap, r0, nk, c0, n):
            if K.plan:
                st["pieces"].append((wap, r0, nk, c0, n, st["mp"], st["mpk"]))
                if st["mp"] == 0:
                    st["scr_off"].append(st["scr_total"])
                    st["scr_total"] += nk * n
                st["mpk"] += 1
                return None, None
            i = st["wi"]
            exp = st["pieces"][i]
            assert exp[1:5] == (r0, nk, c0, n), (exp[1:5], (r0, nk, c0, n))
            while st["wissued"] < min(len(st["pieces"]), i + NSLOT - 1):
                pi = st["wissued"]
                (pw, pr0, pnk, pc0, pn, pmp, pk) = st["pieces"][pi]
                slot = pi % NSLOT
                if pmp >= 1:
                    off = st["scr_off"][pk]
                    sz = pnk * pn
                    K.dma("pool", [(WS[slot][:, 0:sz], st["scratch"][:, off:off + sz])], WSb[slot], write=True, deps=[st["store_dep"][pk]])
                else:
                    pairs = []
                    kstep = 4
                    for k0 in range(0, pnk, kstep):
                        k1 = min(pnk, k0 + kstep)
                        o = WS[slot][:, k0 * pn:k1 * pn].rearrange("p (k n) -> p k n", n=pn)
                        src = pw[pr0 + k0 * 128: pr0 + k1 * 128, pc0:pc0 + pn].rearrange("(k p) n -> p k n", p=128)
                        pairs.append((o, src))
                    K.dma("pool", pairs, WSb[slot], write=True)
                st["wissued"] += 1
            st["wi"] += 1
            slot = i % NSLOT
            if exp[5] == 0:
                off = st["scr_off"][exp[6]]
                sz = nk * n
                st["store_dep"][exp[6]] = K.dma("sp", [(st["scratch"][:, off:off + sz], WS[slot][:, 0:sz])], WSb[slot], write=False)
            view = WS[slot][:, 0:nk * n].rearrange("p (k n) -> p k n", n=n)
            return view, WSb[slot]

        def to_T(src, srcb, ntok, col0, dstbufs):
            for half in range(2):
                pt, ptb = getT()

                def f(e, pt=pt, half=half):
                    ins = None
                    for i in range(8):
                        kc = half * 8 + i
                        ins = e.transpose(out=pt[:, i * 128:(i + 1) * 128], in_=src[:, kc * 128:(kc + 1) * 128], identity=IDENT[:, :])
                    return ins
                K.op("pe", f, R=[srcb, CONSTPb], W=[ptb])
                pv = pt[:, :].rearrange("p (i t) -> p i t", t=128)[:, :, 0:ntok]
                dst = XT[:, half * 8:(half + 1) * 8, col0:col0 + ntok]
                K.op("dve", lambda e, pv=pv, dst=dst: e.tensor_copy(out=dst, in_=pv), R=[ptb], W=dstbufs)

        def cast_T(j, ntok, col0):
            K.op("dve", lambda e: e.tensor_copy(out=XB[0:ntok, :], in_=X[j][0:ntok, :]), R=[Xb[j]], W=[XBb])
            to_T(XB, XBb, ntok, col0, [XTb[j]])

        def load_ln(gi):
            K.dma("sp", [(LNG[:, :], lnp[gi])], LNGb, write=True)
            K.dma("sp", [(LNB[:, :], lnp[gi + 1])], LNBb, write=True)

        def layer_norm(xt, xtb, ntok):
            st6 = SMALL[0:ntok, 0:24]
            mv = SMALL[0:ntok, 24:26]
            sd = SMALL[0:ntok, 26:27]
            rs = SMALL[0:ntok, 27:28]

            def f1(e):
                ins = None
                for i in range(4):
                    ins = e.bn_stats(out=SMALL[0:ntok, i * 6:(i + 1) * 6], in_=xt[0:ntok, i * 512:(i + 1) * 512])
                return ins
            K.op("dve", f1, R=[xtb], W=[SMALLb])
            K.op("dve", lambda e: e.bn_aggr(out=mv, in_=st6), R=[], W=[SMALLb])
            K.op("dve", lambda e: e.tensor_scalar(out=sd, in0=SMALL[0:ntok, 25:26], scalar1=EPS, scalar2=None, op0=ALU.add),
                 R=[], W=[SMALLb])
            K.op("act", lambda e: e.activation(out=sd, in_=sd, func=AF.Sqrt), R=[], W=[SMALLb])
            K.op("dve", lambda e: e.reciprocal(out=rs, in_=sd), R=[], W=[SMALLb])
            K.op("dve", lambda e: e.tensor_scalar(out=xt[0:ntok, :], in0=xt[0:ntok, :], scalar1=SMALL[0:ntok, 24:25],
                                                  scalar2=rs, op0=ALU.subtract, op1=ALU.mult), R=[SMALLb], W=[xtb])
            K.op("dve", lambda e: e.tensor_tensor(out=xt[0:ntok, :], in0=xt[0:ntok, :], in1=LNG[0:ntok, :], op=ALU.mult),
                 R=[LNGb], W=[xtb])
            K.op("dve", lambda e: e.tensor_tensor(out=xt[0:ntok, :], in0=xt[0:ntok, :], in1=LNB[0:ntok, :], op=ALU.add),
                 R=[LNBb], W=[xtb])

        def proj_tok(wv, wb, nk, xt_cols, ntok, xtbufs, ncols):
            ps, psb = getS()

            def f(e):
                ins = None
                for k in range(nk):
                    ins = e.matmul(ps[0:ntok, 0:ncols], lhsT=XT[:, k, xt_cols:xt_cols + ntok], rhs=wv[:, k, 0:ncols],
                                   start=(k == 0), stop=(k == nk - 1))
                return ins
            K.op("pe", f, R=[wb] + xtbufs, W=[psb])
            return ps, psb

        def out_rows(ps, psb, ntok, ncols, dst_ap):
            sg_, sgb = getSTG()
            K.op("dve", lambda e: e.tensor_copy(out=sg_[0:ntok, 0:ncols], in_=ps[0:ntok, 0:ncols]), R=[psb], W=[sgb])
            K.dma("sp", [(dst_ap, sg_[0:ntok, 0:ncols])], sgb, write=False)

        def gla_block(ntok, nch, QKAj, QKAb, VAj, VAb, LGj, LGb, Sx, Sxb, SBl, SBlb, need_out, MIXj, MIXb, GOGj, GOGb, scr):
            csz = ntok // nch
            ENB, EB, KD, QD, QDTf, QDT0, QDT1, KDT, ATm, EBLt, SQ, scrb = scr
            bps, bpsb = getS()
            K.op("pe", lambda e: e.matmul(bps[0:ntok, 0:512], lhsT=TRI2[0:ntok, 0:ntok], rhs=LGj[0:ntok, :], start=True, stop=True),
                 R=[LGb, CONSTb], W=[bpsb])
            blps, blpsb = getS()

            def fbl(e):
                ins = None
                for h in range(4):
                    for c_ in range(nch):
                        ins = e.matmul(blps[:, c_ * 4 + h:c_ * 4 + h + 1], lhsT=LGj[0:ntok, h * 128:(h + 1) * 128], rhs=SEL[0:ntok, c_:c_ + 1],
                                       start=True, stop=True)
                return ins
            K.op("pe", fbl, R=[LGb, CONSTb], W=[blpsb])
            K.op("act", lambda e: e.activation(out=ENB[0:ntok, :], in_=bps[0:ntok, 0:512], func=AF.Exp, scale=-1.0), R=[bpsb], W=[scrb["ENB"]])
            if need_out:
                K.op("act", lambda e: e.activation(out=EB[0:ntok, :], in_=bps[0:ntok, 0:512], func=AF.Exp), R=[bpsb], W=[scrb["EB"]])
            K.op("act", lambda e: e.activation(out=EBLt[:, 0:4 * nch], in_=blps[:, 0:4 * nch], func=AF.Exp), R=[blpsb], W=[scrb["EBL"]])
            dbg0("g_b")
            K.op("dve", lambda e: e.tensor_tensor(out=KD[0:ntok, :], in0=QKAj[0:ntok, 512:1024], in1=ENB[0:ntok, :], op=ALU.mult),
                 R=[QKAb, scrb["ENB"]], W=[scrb["KD"]])
            if need_out:
                K.op("dve", lambda e: e.tensor_tensor(out=QD[0:ntok, :], in0=QKAj[0:ntok, 0:512], in1=EB[0:ntok, :], op=ALU.mult),
                     R=[QKAb, scrb["EB"]], W=[scrb["QD"]])
                dbg0("g_qd")
                pt, ptb = getT()

                def ftr(e):
                    ins = None
                    for h in range(4):
                        ins = e.transpose(out=pt[:, h * 128:(h + 1) * 128], in_=QD[:, h * 128:(h + 1) * 128], identity=IDENT[:, :])
                    for h in range(4):
                        ins = e.transpose(out=pt[:, 512 + h * 128:512 + (h + 1) * 128], in_=KD[:, h * 128:(h + 1) * 128], identity=IDENT[:, :])
                    return ins
                K.op("pe", ftr, R=[scrb["QD"], scrb["KD"], CONSTPb], W=[ptb])
                dbg0("g_trp")
                pq = pt[:, 0:512].rearrange("p (h t) -> p h t", t=128)
                pk = pt[:, 512:1024].rearrange("p (h t) -> p h t", t=128)
                K.op("dve", lambda e: e.tensor_copy(out=QDTf[:, :, 0:ntok], in_=pq[:, :, 0:ntok]), R=[ptb], W=[scrb["QDTf"]])
                K.op("dve", lambda e: e.tensor_copy(out=KDT[:, :, 0:ntok], in_=pk[:, :, 0:ntok]), R=[ptb], W=[scrb["KDT"]])
                dbg0("g_ev")
                if nch == 2:
                    K.op("dve", lambda e: e.memset(QDT0[:, :, :], 0.0), R=[], W=[scrb["QDT0"]])
                    K.op("dve", lambda e: e.memset(QDT1[:, :, :], 0.0), R=[], W=[scrb["QDT1"]])
                    K.op("dve", lambda e: e.tensor_copy(out=QDT0[:, :, 0:64], in_=pq[:, :, 0:64]), R=[ptb], W=[scrb["QDT0"]])
                    K.op("dve", lambda e: e.tensor_copy(out=QDT1[:, :, 64:128], in_=pq[:, :, 64:128]), R=[ptb], W=[scrb["QDT1"]])
                dbg0("g_tr")
                aps, apsb = getS()

                def fat(e):
                    ins = None
                    for h in range(4):
                        ins = e.matmul(aps[0:ntok, h * 128:h * 128 + ntok], lhsT=KDT[:, h, 0:ntok], rhs=QDTf[:, h, 0:ntok], start=True, stop=True)
                    return ins
                K.op("pe", fat, R=[scrb["KDT"], scrb["QDTf"]], W=[apsb])

                def fmask(e):
                    ins = None
                    for h in range(4):
                        ins = e.tensor_tensor(out=ATm[0:ntok, h, 0:ntok], in0=aps[0:ntok, h * 128:h * 128 + ntok], in1=MASKT[0:ntok, 0:ntok], op=ALU.mult)
                    return ins
                K.op("dve", fmask, R=[apsb, CONSTb], W=[scrb["ATm"]])
                ops_ = [(psL[0], psLb[0]), (psL[1], psLb[1])]
            dbg0("g_at")

            def o_groups():
                for hp in range(2):
                    for hh in range(2):
                        h = hp * 2 + hh

                        def fo(e, hp=hp, hh=hh, h=h):
                            dst = ops_[hp][0][0:ntok, hh * 256:(hh + 1) * 256]
                            if nch == 1:
                                e.matmul(dst, lhsT=QDTf[:, h, 0:ntok], rhs=SBl[0][:, h, :], start=True, stop=False)
                            else:
                                e.matmul(dst, lhsT=QDT0[:, h, 0:ntok], rhs=SBl[0][:, h, :], start=True, stop=False)
                                e.matmul(dst, lhsT=QDT1[:, h, 0:ntok], rhs=SBl[1][:, h, :], start=False, stop=False)
                            return e.matmul(dst, lhsT=ATm[0:ntok, h, 0:ntok], rhs=VAj[0:ntok, h * 256:(h + 1) * 256], start=False, stop=True)
                        K.op("pe", fo, R=[scrb["QDTf"], scrb["QDT0"], scrb["QDT1"], scrb["ATm"], VAb] + SBlb, W=[ops_[hp][1]])
            if need_out and nch == 1:
                o_groups()
            for c in range(nch):
                r0 = c * csz
                if need_out and nch == 2 and c == 1:
                    o_groups()
                for hp in range(2):
                    dps, dpsb = getS()

                    def fds(e, hp=hp, dps=dps, r0=r0):
                        ins = None
                        for hh in range(2):
                            h = hp * 2 + hh
                            ins = e.matmul(dps[:, hh * 256:(hh + 1) * 256], lhsT=KD[r0:r0 + csz, h * 128:(h + 1) * 128],
                                           rhs=VAj[r0:r0 + csz, h * 256:(h + 1) * 256], start=True, stop=True)
                        return ins
                    K.op("pe", fds, R=[scrb["KD"], VAb], W=[dpsb])
                    for hh in range(2):
                        h = hp * 2 + hh
                        ebl = EBLt[:, c * 4 + h:c * 4 + h + 1]
                        K.op("dve", lambda e, h=h, ebl=ebl: e.tensor_scalar(out=STMP[:, :], in0=Sx[:, h, :], scalar1=ebl, scalar2=None, op0=ALU.mult),
                             R=[Sxb, scrb["EBL"]], W=[STMPb])
                        K.op("dve", lambda e, h=h, hh=hh, ebl=ebl, dps=dps: e.scalar_tensor_tensor(
                            out=Sx[:, h, :], in0=dps[:, hh * 256:(hh + 1) * 256], scalar=ebl, in1=STMP[:, :], op0=ALU.mult, op1=ALU.add),
                            R=[dpsb, STMPb, scrb["EBL"]], W=[Sxb])
                if need_out and nch == 2:
                    nb_ = (c + 1) % 2
                    K.op("act", lambda e, nb_=nb_: e.activation(out=SBl[nb_][:, :, :].rearrange("p h v -> p (h v)"), in_=Sx[:, :, :].rearrange("p h v -> p (h v)"), func=AF.Copy),
                         R=[Sxb], W=[SBlb[nb_]])
            dbg0("g_o")
            if need_out:
                for h in range(4):
                    hp, hh = h // 2, h % 2
                    osrc = ops_[hp][0][0:ntok, hh * 256:(hh + 1) * 256]
                    K.op("act", lambda e, osrc=osrc: e.activation(out=SQ[0:ntok, :], in_=osrc, func=AF.Square), R=[ops_[hp][1]], W=[scrb["SQ"]])
                    K.op("dve", lambda e, h=h: e.reduce_sum(out=SMALL[0:ntok, 32 + h:33 + h], in_=SQ[0:ntok, :], axis=AX.X), R=[scrb["SQ"]], W=[SMALLb])
                K.op("dve", lambda e: e.tensor_scalar(out=SMALL[0:ntok, 36:40], in0=SMALL[0:ntok, 32:36], scalar1=1.0 / 256.0, scalar2=EPS,
                                                      op0=ALU.mult, op1=ALU.add), R=[], W=[SMALLb])
                K.op("act", lambda e: e.activation(out=SMALL[0:ntok, 36:40], in_=SMALL[0:ntok, 36:40], func=AF.Sqrt), R=[], W=[SMALLb])
                K.op("dve", lambda e: e.reciprocal(out=SMALL[0:ntok, 40:44], in_=SMALL[0:ntok, 36:40]), R=[], W=[SMALLb])
                for h in range(4):
                    hp, hh = h // 2, h % 2
                    osrc = ops_[hp][0][0:ntok, hh * 256:(hh + 1) * 256]
                    K.op("dve", lambda e, h=h, osrc=osrc: e.scalar_tensor_tensor(
                        out=MIXj[0:ntok, h * 256:(h + 1) * 256], in0=osrc, scalar=SMALL[0:ntok, 40 + h:41 + h],
                        in1=GOGj[0:ntok, h * 256:(h + 1) * 256], op0=ALU.mult, op1=ALU.mult),
                        R=[ops_[hp][1], SMALLb, GOGb], W=[MIXb])

        def attn_head(h, nq, qap, qb, blocks, PTt, PTb, TMPt, TMPb, dst, dstb, dflag=False):
            sc = [getS(), getS()]

            def fsc(e):
                ins = None
                for bi, (kT, kb, v, vb, nk, bias) in enumerate(blocks):
                    ps = sc[bi // 4][0]
                    ins = e.matmul(ps[0:nk, (bi % 4) * 128:(bi % 4) * 128 + nq], lhsT=kT, rhs=qap, start=True, stop=True)
                return ins
            K.op("pe", fsc, R=[qb] + [b[1] for b in blocks], W=[sc[0][1], sc[1][1]])
            if dflag:
                dbg0("a_sc")
            for bi, (kT, kb, v, vb, nk, bias) in enumerate(blocks):
                ps, psb = sc[bi // 4]
                src = ps[0:nk, (bi % 4) * 128:(bi % 4) * 128 + nq]
                dstp = PTt[0:nk, bi, 0:nq]
                import os as _os6
                _sk = _os6.environ.get("KDBG_SKIP", "")
                if dflag and (("b%d" % bi) in _sk.split(",")):
                    continue
                if bias is None:
                    K.op("act", lambda e, src=src, dstp=dstp, nk=nk: e.activation(out=dstp, in_=src, func=AF.Exp, bias=CBIAS[0:nk, h:h + 1]),
                         R=[psb, CONSTb], W=[PTb])
                else:
                    K.op("dve", lambda e, src=src, bias=bias: e.tensor_tensor(out=src, in0=src, in1=bias, op=ALU.add), R=[CONSTb], W=[psb])
                    K.op("act", lambda e, src=src, dstp=dstp: e.activation(out=dstp, in_=src, func=AF.Exp), R=[psb], W=[PTb])
            if dflag:
                dbg0("a_exp")
            ob, obb = getS()

            def fpv(e):
                ins = None
                nb = len(blocks)
                for bi, (kT, kb, v, vb, nk, bias) in enumerate(blocks):
                    ins = e.matmul(ob[0:nq, 0:129], lhsT=PTt[0:nk, bi, 0:nq], rhs=v, start=(bi == 0), stop=(bi == nb - 1))
                return ins
            K.op("pe", fpv, R=[PTb] + [b[3] for b in blocks], W=[obb])
            if dflag:
                dbg0("a_pv")
            K.op("dve", lambda e: e.reciprocal(out=SMALL[0:nq, 48:49], in_=ob[0:nq, 128:129]), R=[obb], W=[SMALLb])
            K.op("dve", lambda e: e.tensor_scalar(out=dst, in0=ob[0:nq, 0:128], scalar1=SMALL[0:nq, 48:49], scalar2=None, op0=ALU.mult),
                 R=[obb, SMALLb], W=[dstb])

        def ffn(l, blocks, segs, ncols, precast=False):
            HT = [SCRB[:, i * 12 * NCMAX:(i + 1) * 12 * NCMAX].rearrange("p (c t) -> p c t", t=NCMAX) for i in range(2)]
            HTb = [Buf("HT0"), Buf("HT1")]
            U = [SCRF[:, i * 264:(i + 1) * 264] for i in range(2)]
            Ub = [Buf("U0"), Buf("U1")]
            C = [SCRF[:, 528 + i * 256:528 + (i + 1) * 256] for i in range(2)]
            Cb = [Buf("C0"), Buf("C1")]
            w1 = ffn_w1[l]
            w2 = ffn_w2[l]
            w3 = ffn_w3[l]
            if not precast:
                for j, ntok, col0 in blocks:
                    cast_T(j, ntok, col0)
            xtb = [XTb[j] for j, _, _ in blocks]
            groups = [(0, 12), (12, 12), (24, 12), (36, 7)]
            it = 0
            for gi, (c0g, ncg) in enumerate(groups):
                ht = HT[gi % 2]
                htb = HTb[gi % 2]
                for pc in range(0, ncg, 2):
                    npc = min(2, ncg - pc)
                    fc0 = c0g + pc
                    w1v, w1b = wnext(w1, 0, KC, fc0 * 128, npc * 128)
                    w2v, w2b = wnext(w2, 0, KC, fc0 * 128, npc * 128)
                    for cc in range(npc):
                        fc = fc0 + cc
                        fcl = pc + cc
                        ups, upsb = getS()
                        zps, zpsb = getS()

                        def fu(e, wv=w1v, ps=ups, cc=cc):
                            ins = None
                            for k in range(KC):
                                ins = e.matmul(ps[:, 0:ncols], lhsT=wv[:, k, cc * 128:(cc + 1) * 128], rhs=XT[:, k, 0:ncols], start=(k == 0), stop=(k == KC - 1))
                            return ins
                        if not K.plan:
                            K.op("pe", fu, R=[w1b] + xtb, W=[upsb])
                            K.op("pe", lambda e, wv=w2v, ps=zps, cc=cc: fu(e, wv, ps, cc), R=[w2b] + xtb, W=[zpsb])
                        for (sc0, n, Hh, Hb) in segs:
                            u = U[it % 2]
                            ub = Ub[it % 2]
                            c = C[it % 2]
                            cb = Cb[it % 2]
                            it += 1
                            K.op("act", lambda e, u=u, ups=ups, sc0=sc0, n=n: e.activation(func=AF.Copy, out=u[:, 2:2 + n], in_=ups[:, sc0:sc0 + n]), R=[upsb], W=[ub])
                            K.op("act", lambda e, u=u, Hh=Hh, fc=fc: e.activation(func=AF.Copy, out=u[:, 0:2], in_=Hh[:, l, fc, :]), R=[Hb], W=[ub])
                            K.op("act", lambda e, c=c, ups=ups, sc0=sc0, n=n, fc=fc: e.activation(
                                out=c[:, 0:n], in_=ups[:, sc0:sc0 + n], func=AF.Identity, bias=CBc[:, l, fc:fc + 1], scale=CW[:, l, fc, 2:3]),
                                R=[upsb, CONSTb], W=[cb])
                            K.op("dve", lambda e, c=c, u=u, n=n, fc=fc: e.scalar_tensor_tensor(
                                out=c[:, 0:n], in0=u[:, 1:1 + n], scalar=CW[:, l, fc, 1:2], in1=c[:, 0:n], op0=ALU.mult, op1=ALU.add),
                                R=[ub, CONSTb], W=[cb])
                            K.op("dve", lambda e, c=c, u=u, n=n, fc=fc: e.scalar_tensor_tensor(
                                out=c[:, 0:n], in0=u[:, 0:n], scalar=CW[:, l, fc, 0:1], in1=c[:, 0:n], op0=ALU.mult, op1=ALU.add),
                                R=[ub, CONSTb], W=[cb])
                            K.op("act", lambda e, c=c, n=n: e.activation(out=c[:, 0:n], in_=c[:, 0:n], func=AF.Gelu_apprx_tanh), R=[], W=[cb])
                            K.op("dve", lambda e, c=c, n=n, ht=ht, fcl=fcl, zps=zps, sc0=sc0: e.tensor_tensor(
                                out=ht[:, fcl, sc0:sc0 + n], in0=c[:, 0:n], in1=zps[:, sc0:sc0 + n], op=ALU.mult), R=[cb, zpsb], W=[htb])
                            K.op("act", lambda e, u=u, Hh=Hh, fc=fc, n=n: e.activation(func=AF.Copy, out=Hh[:, l, fc, :], in_=u[:, n:n + 2]), R=[ub], W=[Hb])
                for cbk in range(D // WCOL):
                    w3v, w3b = wnext(w3, c0g * 128, ncg, cbk * WCOL, WCOL)
                    for j, ntok, col0 in blocks:
                        ps, psb = getS()

                        def f3(e, ps=ps, wv=w3v, ntok=ntok, col0=col0, ht=ht, ncg=ncg):
                            ins = None
                            for k in range(ncg):
                                ins = e.matmul(ps[0:ntok, 0:WCOL], lhsT=ht[:, k, col0:col0 + ntok], rhs=wv[:, k, :], start=(k == 0), stop=(k == ncg - 1))
                            return ins
                        if not K.plan:
                            K.op("pe", f3, R=[w3b, htb], W=[psb])
                        xd = X[j][0:ntok, cbk * WCOL:(cbk + 1) * WCOL]
                        if gi == 0:
                            K.op("dve", lambda e, xd=xd, ps=ps, ntok=ntok: e.scalar_tensor_tensor(
                                out=xd, in0=xd, scalar=ALPHA, in1=ps[0:ntok, 0:WCOL], op0=ALU.mult, op1=ALU.add), R=[psb], W=[Xb[j]])
                        else:
                            K.op("dve", lambda e, xd=xd, ps=ps, ntok=ntok: e.tensor_tensor(out=xd, in0=xd, in1=ps[0:ntok, 0:WCOL], op=ALU.add),
                                 R=[psb], W=[Xb[j]])

        def out_proj(wap, src_fn, blocks):
            for j, ntok, col0 in blocks:
                s_, sb_ = src_fn(j)
                to_T(s_, sb_, ntok, col0, [XTb[j]])
            for cbk in range(D // WCOL):
                wv, wb = wnext(wap, 0, KC, cbk * WCOL, WCOL)
                for j, ntok, col0 in blocks:
                    if K.plan:
                        continue
                    ps, psb = proj_tok(wv, wb, KC, col0, ntok, [XTb[j]], WCOL)
                    xd = X[j][0:ntok, cbk * WCOL:(cbk + 1) * WCOL]
                    K.op("dve", lambda e, xd=xd, ps=ps, ntok=ntok: e.scalar_tensor_tensor(
                        out=xd, in0=xd, scalar=ALPHA, in1=ps[0:ntok, 0:WCOL], op0=ALU.mult, op1=ALU.add), R=[psb], W=[Xb[j]])

        def carveA():
            o = [0]

            def tb(n):
                a = SCRB[:, o[0]:o[0] + n]
                o[0] += n
                return a
            of = [0]

            def tf(n):
                a = SCRF[:, of[0]:of[0] + n]
                of[0] += n
                return a
            A = {}
            A["QKA"] = [tb(1024) for _ in range(NTM)]
            A["VA"] = [tb(1024) for _ in range(NTM)]
            A["GOG"] = [tb(1024) for _ in range(NTM)]
            A["MIX"] = [tb(2048) for _ in range(NTM)]
            A["QT"] = tb(8 * NCMAX).rearrange("p (h t) -> p h t", t=NCMAX)
            A["KTs"] = tb(8 * 32).rearrange("p (h t) -> p h t", t=32)
            A["VSN"] = tb(8 * 130).rearrange("p (h d) -> p h d", d=130)
            A["GLT"] = tb(NCMAX)
            A["KD"] = tb(512)
            A["QD"] = tb(512)
            A["QDTf"] = tb(512).rearrange("p (h t) -> p h t", t=128)
            A["QDT0"] = tb(512).rearrange("p (h t) -> p h t", t=128)
            A["QDT1"] = tb(512).rearrange("p (h t) -> p h t", t=128)
            A["KDT"] = tb(512).rearrange("p (h t) -> p h t", t=128)
            A["ATm"] = tb(512).rearrange("p (h t) -> p h t", t=128)
            A["PT"] = [tb(640).rearrange("p (b t) -> p b t", t=128) for _ in range(2)]
            A["KSh"] = [tb(512).rearrange("p (b d) -> p b d", d=128) for _ in range(2)]
            A["KTSh"] = [tb(512) for _ in range(2)]
            A["VSh"] = [tb(4 * 130).rearrange("p (b d) -> p b d", d=130) for _ in range(2)]
            A["LOGG"] = [tf(512) for _ in range(NTM)]
            A["ENB"] = tf(512)
            A["EB"] = tf(512)
            A["SQ"] = tf(256)
            A["TMP"] = tf(256).rearrange("p (b t) -> p b t", t=128)
            A["EBLt"] = tf(8)
            A["LTMP"] = tf(512)
            assert o[0] <= NB and of[0] <= NF, (o[0], of[0])
            Ab = {}
            for k in ["QKA", "VA", "GOG", "MIX", "LOGG"]:
                Ab[k] = [Buf(k + str(i)) for i in range(NTM)]
            for k in ["QT", "KTs", "VSN", "GLT", "KD", "QD", "QDTf", "QDT0", "QDT1", "KDT", "ATm", "ENB", "EB", "SQ", "EBL", "LTMP"]:
                Ab[k] = Buf(k)
            for k in ["PT", "KSh", "KTSh", "VSh", "TMP"]:
                Ab[k] = [Buf(k + "0"), Buf(k + "1")]
            return A, Ab

        def mixer0(blocks, ncols, A, Ab, mode, kout_rows, sample_blk):
            W = w_in_even
            xtb = [XTb[b[0]] for b in blocks]
            main_blocks = [b for b in blocks if b[3] == "main"]
            nmain = 128 * len(main_blocks)
            full = (mode == "main")
            kv = mode in ("main", "prekv")

            def tok_piece(c0, n, evac):
                wv, wb = wnext(W, 0, KC, c0, n)
                for (j, ntok, col0, kind) in blocks:
                    if K.plan:
                        continue
                    ps, psb = proj_tok(wv, wb, KC, col0, ntok, [XTb[j]], n)
                    evac(j, ntok, kind, ps, psb)
            if full:
                for pi in range(2):
                    tok_piece(pi * 256, 256, lambda j, ntok, kind, ps, psb, pi=pi: K.op(
                        "act", lambda e: e.activation(out=A["QKA"][j][0:ntok, pi * 256:(pi + 1) * 256], in_=ps[0:ntok, 0:256], func=AF.Copy,
                                                      scale=128.0 ** -0.5), R=[psb], W=[Ab["QKA"][j]]))
            for pi in range(2):
                tok_piece(512 + pi * 256, 256, lambda j, ntok, kind, ps, psb, pi=pi: K.op(
                    "dve", lambda e: e.tensor_copy(out=A["QKA"][j][0:ntok, 512 + pi * 256:512 + (pi + 1) * 256], in_=ps[0:ntok, 0:256]),
                    R=[psb], W=[Ab["QKA"][j]]))
            for pi in range(4):
                tok_piece(1024 + pi * 256, 256, lambda j, ntok, kind, ps, psb, pi=pi: K.op(
                    "act", lambda e: e.activation(func=AF.Copy, out=A["VA"][j][0:ntok, pi * 256:(pi + 1) * 256], in_=ps[0:ntok, 0:256]), R=[psb], W=[Ab["VA"][j]]))
            dbg0("m_va")
            wv, wb = wnext(W, 0, KC, 2048, 16)
            if not K.plan:
                ps, psb = getS()

                def fg(e, ps=ps, wv=wv):
                    ins = None
                    for k in range(KC):
                        ins = e.matmul(ps[0:16, 0:ncols], lhsT=wv[:, k, 0:16], rhs=XT[:, k, 0:ncols], start=(k == 0), stop=(k == KC - 1))
                    return ins
                K.op("pe", fg, R=[wb] + xtb, W=[psb])
                K.op("dve", lambda e, ps=ps: e.tensor_copy(out=A["GLT"][0:16, 0:ncols], in_=ps[0:16, 0:ncols]), R=[psb], W=[Ab["GLT"]])
                if True:
                    pass
                for (j, ntok, col0, kind) in blocks:
                    gps, gpsb = getS()

                    def fgate(e, gps=gps, ntok=ntok, col0=col0):
                        e.matmul(gps[0:ntok, 0:512], lhsT=A["GLT"][0:16, col0:col0 + ntok], rhs=WGU[0:16, :], start=True, stop=False)
                        return e.matmul(gps[0:ntok, 0:512], lhsT=ONES[0:1, 0:ntok], rhs=BG[0:1, :], start=False, stop=True)
                    K.op("pe", fgate, R=[Ab["GLT"], CONSTPb], W=[gpsb])
                    K.op("act", lambda e, gps=gps, ntok=ntok: e.activation(out=A["LTMP"][0:ntok, :], in_=gps[0:ntok, 0:512], func=AF.Exp, scale=-1.0),
                         R=[gpsb], W=[Ab["LTMP"]])
                    K.op("dve", lambda e, ntok=ntok: e.tensor_scalar(out=A["LTMP"][0:ntok, :], in0=A["LTMP"][0:ntok, :], scalar1=1.0, scalar2=None, op0=ALU.add),
                         R=[], W=[Ab["LTMP"]])
                    K.op("act", lambda e, j=j, ntok=ntok: e.activation(out=A["LOGG"][j][0:ntok, :], in_=A["LTMP"][0:ntok, :], func=AF.Ln),
                         R=[Ab["LTMP"]], W=[Ab["LOGG"][j]])
            dbg0("m_gate")
            if full:
                def ev_og(j, ntok, kind, ps, psb, pi):
                    dst = A["GOG"][j][0:ntok, pi * 256:(pi + 1) * 256]
                    K.op("act", lambda e: e.activation(out=dst, in_=ps[0:ntok, 0:256], func=AF.Silu), R=[psb], W=[Ab["GOG"][j]])
                    K.op("dve", lambda e: e.tensor_tensor(out=dst, in0=dst, in1=GN[0:ntok, pi * 256:(pi + 1) * 256], op=ALU.mult),
                         R=[CONSTb], W=[Ab["GOG"][j]])
                for pi in range(4):
                    tok_piece(2064 + pi * 256, 256, lambda j, ntok, kind, ps, psb, pi=pi: ev_og(j, ntok, kind, ps, psb, pi))
                for pi in range(4):
                    wv, wb = wnext(W, 0, KC, 3088 + pi * 256, 256)
                    if K.plan:
                        continue
                    for hh in range(2):
                        h = pi * 2 + hh
                        ps, psb = getS()

                        def fq(e, ps=ps, wv=wv, hh=hh):
                            ins = None
                            for k in range(KC):
                                ins = e.matmul(ps[:, 0:ncols], lhsT=wv[:, k, hh * 128:(hh + 1) * 128], rhs=XT[:, k, 0:ncols], start=(k == 0), stop=(k == KC - 1))
                            return ins
                        K.op("pe", fq, R=[wb] + xtb, W=[psb])
                        K.op("act", lambda e, ps=ps, h=h: e.activation(out=A["QT"][:, h, 0:ncols], in_=ps[:, 0:ncols], func=AF.Copy, scale=128.0 ** -0.5),
                             R=[psb], W=[Ab["QT"]])
            dbg0("m_qb")
            if kv:
                for pi in range(4):
                    wv, wb = wnext(W, 0, KC, 4112 + pi * 256, 256)
                    if K.plan:
                        continue
                    for hh in range(2):
                        h = pi * 2 + hh
                        ps, psb = getS()

                        def fk(e, ps=ps, wv=wv, hh=hh):
                            ins = None
                            for k in range(KC):
                                ins = e.matmul(ps[:, 0:ncols], lhsT=wv[:, k, hh * 128:(hh + 1) * 128], rhs=XT[:, k, 0:ncols], start=(k == 0), stop=(k == KC - 1))
                            return ins
                        K.op("pe", fk, R=[wb] + xtb, W=[psb])
                        if nmain:
                            K.op("dve", lambda e, ps=ps, h=h: e.tensor_copy(out=KT[:, h, 512:512 + nmain], in_=ps[:, 0:nmain]),
                                 R=[psb], W=[KTb[4 + b[0]] for b in main_blocks])
                        if sample_blk is not None:
                            K.op("dve", lambda e, ps=ps, h=h: e.tensor_copy(out=A["KTs"][:, h, :], in_=ps[:, nmain:nmain + 32]), R=[psb], W=[Ab["KTs"]])
                    for (j, ntok, col0, kind) in blocks:
                        if kout_rows.get(j) is not None:
                            ps, psb = proj_tok(wv, wb, KC, col0, ntok, [XTb[j]], 256)
                            dstd = (ks_out if kind == "sample" else kout)[kout_rows[j]:kout_rows[j] + ntok, pi * 256:(pi + 1) * 256]
                            out_rows(ps, psb, ntok, 256, dstd)
                dbg0("m_kb")
                def ev_v(j, ntok, kind, ps, psb, pi):
                    import os as _os4
                    src = ps[0:ntok, 0:256].rearrange("p (h d) -> p h d", d=128)
                    if _os4.environ.get("KDBG_SKIP") == "vcopy":
                        pass
                    else:
                        for hh in range(2):
                            h = pi * 2 + hh
                            s2 = ps[0:ntok, hh * 128:(hh + 1) * 128]
                            if kind == "main":
                                K.op("dve", lambda e, h=h, s2=s2: e.tensor_copy(out=VB[0:ntok, 4 + j, h, 0:128], in_=s2), R=[psb], W=[VBb[4 + j]])
                            else:
                                K.op("dve", lambda e, h=h, s2=s2: e.tensor_copy(out=A["VSN"][0:ntok, h, 0:128], in_=s2), R=[psb], W=[Ab["VSN"]])
                    if kout_rows.get(j) is not None:
                        dstd = (vs_out if kind == "sample" else vout)[kout_rows[j]:kout_rows[j] + ntok, pi * 256:(pi + 1) * 256]
                        out_rows(ps, psb, ntok, 256, dstd)
                for pi in range(4):
                    tok_piece(5136 + pi * 256, 256, lambda j, ntok, kind, ps, psb, pi=pi: ev_v(j, ntok, kind, ps, psb, pi))
                dbg0("m_vb")
                for (j, ntok, col0, kind) in blocks:
                    if kind == "main":
                        K.op("dve", lambda e, j=j: e.memset(VB[:, 4 + j, :, 128:129], 1.0), R=[], W=[VBb[4 + j]])
                        if mode == "prekv":
                            K.op("dve", lambda e, j=j: e.tensor_scalar(out=VB[:, 4 + j, :, 128:129], in0=VB[:, 4 + j, :, 128:129], scalar1=PV[:, 0:1],
                                                                      scalar2=None, op0=ALU.mult), R=[CONSTb], W=[VBb[4 + j]])
                    else:
                        K.op("dve", lambda e: e.memset(A["VSN"][:, :, 128:129], 1.0), R=[], W=[Ab["VSN"]])

        def ring_shift(nt):
            for s in range(4):
                K.op("dve", lambda e, s=s: e.tensor_copy(out=KT[:, :, s * 128:(s + 1) * 128], in_=KT[:, :, (s + nt) * 128:(s + nt + 1) * 128]),
                     R=[KTb[s + nt]], W=[KTb[s]])
                K.op("dve", lambda e, s=s: e.tensor_copy(out=VB[:, s, :, :], in_=VB[:, s + nt, :, :]), R=[VBb[s + nt]], W=[VBb[s]])

        def gla_scr(A, Ab):
            return (A["ENB"], A["EB"], A["KD"], A["QD"], A["QDTf"], A["QDT0"], A["QDT1"], A["KDT"], A["ATm"], A["EBLt"], A["SQ"], Ab)

        class _Stop(Exception):
            pass

        def dbg(tag, j):
            import os as _os2
            if _os2.environ.get("KDBG_STOP") == tag:
                if not K.plan:
                    K.dma("sp", [(y_s, X[j][0:32, :])], Xb[j], write=False)
                    K.finish()
                raise _Stop()

        def dbg0(tag):
            import os as _os3
            if _os3.environ.get("KDBG_STOP") == tag:
                if not K.plan:
                    K.finish()
                raise _Stop()

        def program():
            try:
                program_()
            except _Stop:
                pass

        def program_():
            st.update(iS=0, iT=0, iG=0, wi=0, wissued=0, mp=-1, mpk=0)
            if K.plan:
                st["pieces"] = []
                st["scr_off"] = []
                st["scr_total"] = 0
                st["store_dep"] = {}
            cl = [(BT[:, :, :, :].rearrange("p a h q -> p (a h q)"), c_bt), (BTS3[:, :, :].rearrange("p h q -> p (h q)"), c_bts3),
                  (BTS4[:, :, :].rearrange("p h q -> p (h q)"), c_bts4), (CBIAS[:, :], c_cbias), (TRI2[:, :], c_tri2), (SEL[:, :], c_sel),
                  (MASKT[:, :], c_maskt), (GN[:, :], c_gn), (CW[:, :, :, :].rearrange("p l c t -> p (l c t)"), c_cw),
                  (CBc[:, :, :].rearrange("p l c -> p (l c)"), c_cb), (BSP[:, :], c_bsp), (PV[:, :], c_pv)]
            K.dma("sp", cl, CONSTb, write=True)
            K.dma("pool", [(IDENT[:, :], c_ident), (WGU[:, :], c_wgu), (BG[:, :], c_bg), (ONES[:, :], c_ones)], CONSTPb, write=True)
            K.dma("pool", [(WST[:, :, :].rearrange("p g t -> p (g t)"), c_wst)], WSTb, write=True)
            K.dma("sp", [(SS[:, :, :].rearrange("p h v -> p (h v)"), sg.rearrange("p h v -> p (h v)"))], SSb, write=True)
            K.dma("sp", [(CHS[:, :, :, :].rearrange("p l c t -> p (l c t)"), chs_in)], CHSb, write=True)
            K.op("dve", lambda e: e.memset(S[:, :, :], 0.0), R=[], W=[Sb])
            K.op("dve", lambda e: e.memset(SBf[:, :, :], 0.0), R=[], W=[SBfb])
            K.op("dve", lambda e: e.memset(CH[:, :, :, :], 0.0), R=[], W=[CHb])
            K.op("dve", lambda e: e.memset(KT[:, :, :], 0.0), R=[], W=KTb)
            K.op("dve", lambda e: e.memset(VB[:, :, :, :], 0.0), R=[], W=VBb)
            K.op("act", lambda e: e.activation(out=SSBf[:, :, :].rearrange("p h v -> p (h v)"), in_=SS[:, :, :].rearrange("p h v -> p (h v)"), func=AF.Copy), R=[SSb], W=[SSBfb])

            dbg0("consts")
            A, Ab = carveA()
            pre_passes = []
            t = 0
            first = NPRE % NTM if NPRE % NTM else NTM
            sizes = [first] + [NTM] * ((NPRE - first) // NTM)
            for sz in sizes:
                pre_passes.append(list(range(t, t + sz)))
                t += sz
            import os as _os
            _lim = _os.environ.get("KDBG_PASSES")
            if _lim:
                a_, b_ = [int(v) for v in _lim.split(",")]
                pre_passes = pre_passes[len(pre_passes) - a_:] if a_ else []
            for blks in pre_passes:
                blocks = [(i, 128, i * 128, "main") for i in range(len(blks))]
                ncols = 128 * len(blks)
                kvp = blks[0] >= NPRE - 4
                for (j, ntok, col0, kind), gb in zip(blocks, blks):
                    K.dma("pool", [(XB[:, :], xpre[gb * 128:(gb + 1) * 128, :])], XBb, write=True)
                    to_T(XB, XBb, 128, col0, [XTb[j]])
                mixer0(blocks, ncols, A, Ab, "prekv" if kvp else "pre", {}, None)
                for (j, ntok, col0, kind) in blocks:
                    gla_block(128, 2, A["QKA"][j], Ab["QKA"][j], A["VA"][j], Ab["VA"][j], A["LOGG"][j], Ab["LOGG"][j],
                              S, Sb, [SBf, SBf2], [SBfb, SBf2b], False, None, None, None, None, gla_scr(A, Ab))
                if kvp:
                    ring_shift(len(blks))
            K.op("act", lambda e: e.activation(out=SBf[:, :, :].rearrange("p h v -> p (h v)"), in_=S[:, :, :].rearrange("p h v -> p (h v)"), func=AF.Copy), R=[Sb], W=[SBfb])
            dbg0("init")
            K.barrier()
            dbg0("init2")

            main_passes = []
            t = 0
            while t < NMAIN:
                sz = min(NTM, NMAIN - t)
                main_passes.append(list(range(t, t + sz)))
                t += sz
            if _lim:
                main_passes = main_passes[len(main_passes) - b_:] if b_ else []
            for pidx, blks in enumerate(main_passes):
                st["mp"] = pidx
                st["mpk"] = 0
                last = (pidx == len(main_passes) - 1)
                has_sample = last and len(blks) < NTM
                assert last is False or has_sample
                nmb = len(blks)
                blocks4 = [(i, 128, i * 128, "main") for i in range(nmb)]
                if has_sample:
                    blocks4.append((nmb, 32, nmb * 128, "sample"))
                blocks = [(j, ntok, col0) for (j, ntok, col0, kind) in blocks4]
                ncols = nmb * 128 + (32 if has_sample else 0)
                kout_rows = {}
                for i, gb in enumerate(blks):
                    if gb >= NMAIN - 4:
                        kout_rows[i] = (gb - (NMAIN - 4)) * 128
                if has_sample:
                    kout_rows[nmb] = 0
                A, Ab = carveA()
                for i, gb in enumerate(blks):
                    K.dma("sp", [(X[i][:, :], xm[gb * 128:(gb + 1) * 128, :])], Xb[i], write=True)
                if has_sample:
                    K.dma("sp", [(X[nmb][0:32, :], xs)], Xb[nmb], write=True)
                dbg0("xload")
                cast_T(0, 128, 0)
                dbg0("xT0")
                for (j, ntok, col0) in blocks[1:]:
                    cast_T(j, ntok, col0)
                dbg0("xT")
                mixer0(blocks4, ncols, A, Ab, "main", kout_rows, (blocks4[-1] if has_sample else None))
                dbg0("mixer0")
                for (j, ntok, col0, kind) in blocks4:
                    if kind == "main":
                        gla_block(128, 2, A["QKA"][j], Ab["QKA"][j], A["VA"][j], Ab["VA"][j], A["LOGG"][j], Ab["LOGG"][j],
                                  S, Sb, [SBf, SBf2], [SBfb, SBf2b], True, A["MIX"][j], Ab["MIX"][j], A["GOG"][j], Ab["GOG"][j], gla_scr(A, Ab))
                    else:
                        gla_block(32, 1, A["QKA"][j], Ab["QKA"][j], A["VA"][j], Ab["VA"][j], A["LOGG"][j], Ab["LOGG"][j],
                                  SS, SSb, [SSBf], [SSBfb], True, A["MIX"][j], Ab["MIX"][j], A["GOG"][j], Ab["GOG"][j], gla_scr(A, Ab))
                dbg0("gla")
                hi = 0
                for (j, ntok, col0, kind) in blocks4:
                    for h in range(8):
                        PTt, PTb = A["PT"][hi % 2], Ab["PT"][hi % 2]
                        hi += 1
                        if kind == "main":
                            bl = []
                            for d in range(5):
                                s = j + d
                                bias = None
                                if d == 0:
                                    bias = BT[:, 0, h, :]
                                elif d == 3:
                                    bias = BT[:, 1, h, :]
                                elif d == 4:
                                    bias = BT[:, 2, h, :]
                                bl.append((KT[:, h, s * 128:(s + 1) * 128], KTb[s], VB[:, s, h, 0:129], VBb[s], 128, bias))
                            attn_head(h, 128, A["QT"][:, h, col0:col0 + 128], Ab["QT"], bl, PTt, PTb, A["TMP"], Ab["TMP"],
                                      A["MIX"][j][0:128, 1024 + h * 128:1024 + (h + 1) * 128], Ab["MIX"][j])
                        else:
                            if h == 0:
                                dbg0("a_main")
                            ksh, kshb = A["KSh"][h % 2], Ab["KSh"][h % 2]
                            ktsh, ktshb = A["KTSh"][h % 2], Ab["KTSh"][h % 2]
                            vsh, vshb = A["VSh"][h % 2], Ab["VSh"][h % 2]
                            K.dma("pool", [(ksh[:, :, :], ck[:, h * 128:(h + 1) * 128].rearrange("(b p) d -> p b d", p=128))], kshb, write=True)
                            K.dma("pool", [(vsh[:, :, 0:128], cv[:, h * 128:(h + 1) * 128].rearrange("(b p) d -> p b d", p=128))], vshb, write=True)
                            K.op("dve", lambda e, vsh=vsh: e.memset(vsh[:, :, 128:129], 1.0), R=[], W=[vshb])
                            pt, ptb = getT()

                            def ftk(e, pt=pt, ksh=ksh):
                                ins = None
                                for b in range(4):
                                    ins = e.transpose(out=pt[:, b * 128:(b + 1) * 128], in_=ksh[:, b, :], identity=IDENT[:, :])
                                return ins
                            K.op("pe", ftk, R=[kshb, CONSTPb], W=[ptb])
                            K.op("dve", lambda e, pt=pt, ktsh=ktsh: e.tensor_copy(out=ktsh[:, 0:512], in_=pt[:, 0:512]), R=[ptb], W=[ktshb])
                            if h == 0:
                                dbg0("a_st")
                            bl = []
                            for b in range(4):
                                bias = BTS3[:, h, :] if b == 3 else None
                                bl.append((ktsh[:, b * 128:(b + 1) * 128], ktshb, vsh[:, b, 0:129], vshb, 128, bias))
                            bl.append((A["KTs"][:, h, :], Ab["KTs"], A["VSN"][0:32, h, 0:129], Ab["VSN"], 32, BTS4[:, h, :]))
                            attn_head(h, 32, A["QT"][:, h, col0:col0 + 32], Ab["QT"], bl, PTt, PTb, A["TMP"], Ab["TMP"],
                                      A["MIX"][j][0:32, 1024 + h * 128:1024 + (h + 1) * 128], Ab["MIX"][j], dflag=(h == 0))
                dbg0("attn")
                ring_shift(nmb)
                if last:
                    K.dma("sp", [(gla_out, S[:, :, :].rearrange("p h v -> p (h v)"))], Sb, write=False)
                    K.dma("sp", [(gla_s, SS[:, :, :].rearrange("p h v -> p (h v)"))], SSb, write=False)
                out_proj(w_out_even, lambda j: (A["MIX"][j], Ab["MIX"][j]), blocks)
                if has_sample:
                    dbg("preln1", nmb)
                load_ln(0)
                for (j, ntok, col0) in blocks:
                    layer_norm(X[j], Xb[j], ntok)
                    cast_T(j, ntok, col0)
                if has_sample:
                    dbg("x1", nmb)
                K.barrier()
                segs = []
                if nmb:
                    segs.append((0, nmb * 128, CH, CHb))
                if has_sample:
                    segs.append((nmb * 128, 32, CHS, CHSb))
                ffn(0, blocks, segs, ncols, precast=True)
                if has_sample:
                    dbg("preln2", nmb)
                load_ln(2)
                for (j, ntok, col0) in blocks:
                    layer_norm(X[j], Xb[j], ntok)
                    cast_T(j, ntok, col0)
                K.barrier()
                U1 = [SCRB[:, i * 2048:(i + 1) * 2048] for i in range(NTM)]
                VN = [SCRB[:, (NTM + i) * 2048:(NTM + i + 1) * 2048] for i in range(NTM)]
                V1 = [SCRF[:, i * 2048:(i + 1) * 2048] for i in range(NTM)]
                U1b = [Buf("U1%d" % i) for i in range(NTM)]
                VNb = [Buf("VN%d" % i) for i in range(NTM)]
                V1b = [Buf("V1%d" % i) for i in range(NTM)]
                for pi in range(16):
                    wv, wb = wnext(w_in_odd, 0, KC, pi * 256, 256)
                    if K.plan:
                        continue
                    for (j, ntok, col0) in blocks:
                        ps, psb = proj_tok(wv, wb, KC, col0, ntok, [XTb[j]], 256)
                        if pi < 8:
                            K.op("act", lambda e, ps=ps, j=j, ntok=ntok, pi=pi: e.activation(
                                out=U1[j][0:ntok, pi * 256:(pi + 1) * 256], in_=ps[0:ntok, 0:256], func=AF.Gelu_apprx_tanh), R=[psb], W=[U1b[j]])
                        else:
                            K.op("act", lambda e, ps=ps, j=j, ntok=ntok, pi=pi: e.activation(
                                out=V1[j][0:ntok, (pi - 8) * 256:(pi - 7) * 256], in_=ps[0:ntok, 0:256], func=AF.Gelu_apprx_tanh), R=[psb], W=[V1b[j]])
                load_ln(8)
                for (j, ntok, col0, kind) in blocks4:
                    layer_norm(V1[j], V1b[j], ntok)
                    K.op("act", lambda e, j=j, ntok=ntok: e.activation(func=AF.Copy, out=VN[j][0:ntok, :], in_=V1[j][0:ntok, :]), R=[V1b[j]], W=[VNb[j]])
                    if kind == "sample":
                        for q in range(8):
                            sg_, sgb = getSTG()
                            K.op("dve", lambda e, sg_=sg_, q=q, j=j: e.tensor_copy(out=sg_[0:32, :], in_=V1[j][0:32, q * 256:(q + 1) * 256]), R=[V1b[j]], W=[sgb])
                            K.dma("sp", [(mlpv_s[:, q * 256:(q + 1) * 256], sg_[0:32, :])], sgb, write=False)
                    for gp in range(4):
                        ps, psb = getS()

                        def fsp(e, ps=ps, gp=gp, j=j, ntok=ntok):
                            ins = None
                            for gg in range(2):
                                g = gp * 2 + gg
                                ins = e.matmul(ps[0:ntok, gg * 256:(gg + 1) * 256], lhsT=WST[0:ntok, g, 0:ntok], rhs=VN[j][0:ntok, g * 256:(g + 1) * 256],
                                               start=True, stop=True)
                            return ins
                        K.op("pe", fsp, R=[WSTb, VNb[j]], W=[psb])
                        for gg in range(2):
                            g = gp * 2 + gg
                            K.op("dve", lambda e, ps=ps, g=g, gg=gg, j=j, ntok=ntok: e.scalar_tensor_tensor(
                                out=U1[j][0:ntok, g * 256:(g + 1) * 256], in0=ps[0:ntok, gg * 256:(gg + 1) * 256], scalar=BSP[0:ntok, g:g + 1],
                                in1=U1[j][0:ntok, g * 256:(g + 1) * 256], op0=ALU.add, op1=ALU.mult), R=[psb, CONSTb], W=[U1b[j]])
                out_proj(w_out_odd, lambda j: (U1[j], U1b[j]), blocks)
                load_ln(4)
                for (j, ntok, col0) in blocks:
                    layer_norm(X[j], Xb[j], ntok)
                    cast_T(j, ntok, col0)
                K.barrier()
                ffn(1, blocks, segs, ncols, precast=True)
                load_ln(6)
                for (j, ntok, col0, kind) in blocks4:
                    layer_norm(X[j], Xb[j], ntok)
                    if kind == "main":
                        gb = blks[j]
                        K.dma("sp", [(y_main[gb * 128:(gb + 1) * 128, :], X[j][:, :])], Xb[j], write=False)
                    else:
                        K.dma("sp", [(y_s, X[j][0:32, :])], Xb[j], write=False)
                K.barrier()
            K.dma("sp", [(conv_out, CH[:, :, :, :].rearrange("p l c t -> p (l c t)"))], CHb, write=False)
            K.dma("sp", [(conv_s, CHS[:, :, :, :].rearrange("p l c t -> p (l c t)"))], CHSb, write=False)
            K.finish()

        K.plan = True
        program()
        K.plan = False
        K.reset()
        st["scratch"] = nc.dram_tensor("wscratch", [128, max(1, st["scr_total"])], BF16, kind="Internal").ap()
        program()
        K.emit()
    return nc


_NC_CACHE = {}


def _rel_tiles(table):
    H = table.shape[0]
    k = np.arange(128)[:, None]
    q = np.arange(128)[None, :]
    bt = np.zeros((128, 3, H, 128), np.float32)
    for a, d in enumerate((0, 3, 4)):
        kp = (d - 4) * 128 + k
        rel = np.clip(q - kp, -128, 128) + 128
        vis = (np.floor_divide(kp, 64) >= np.floor_divide(q, 64) - 8) & (np.floor_divide(kp, 64) <= np.floor_divide(q, 64))
        for h in range(H):
            bt[:, a, h, :] = np.where(vis, table[h][rel], np.float32(-30000.0))
    t = np.arange(32)[None, :]
    i = np.arange(128)[:, None]
    bts3 = np.stack([table[h][np.clip(t + 128 - i, -128, 128) + 128] for h in range(H)], axis=1).astype(np.float32)
    s = np.arange(32)[:, None]
    bts4 = np.stack([table[h][np.clip(t - s, -128, 128) + 128] for h in range(H)], axis=1).astype(np.float32)
    cbias = np.broadcast_to(table[:, 256][None, :], (128, H)).astype(np.float32)
    return bt, bts3, bts4, cbias


def _prep(x_prompt, x_sample, cache_attn_k, cache_attn_v, state_gla, state_ffn_conv,
          w_in_even, w_gate_up, b_gate, gla_norm_g, rel_bias, w_out_even,
          w_in_odd, ln_v_g, ln_v_b, w_spatial, b_spatial, w_out_odd,
          ffn_w1, ffn_w2, ffn_conv_w, ffn_conv_b, ffn_w3,
          ln1_g, ln1_b, ln2_g, ln2_b):
    f = lambda a: np.ascontiguousarray(np.asarray(a, dtype=np.float32))
    x_prompt = f(x_prompt); x_sample = f(x_sample)

    s_ = np.arange(128)[:, None]
    t_ = np.arange(128)[None, :]
    same = (s_ // 64) == (t_ // 64)
    causal = s_ <= t_
    tri2 = np.where(same & causal, np.float32(-1.0 / 16.0), np.float32(0.0)).astype(np.float32)
    sel = np.where((s_ // 64) == np.arange(2)[None, :], np.float32(-1.0 / 16.0), np.float32(0.0)).astype(np.float32)
    maskt = np.where(same & causal, np.float32(1.0), np.float32(0.0)).astype(np.float32)
    bt, bts3, bts4, cbias = _rel_tiles(f(rel_bias)[0])
    lnp = np.stack([np.broadcast_to(f(v)[None, :], (128, D)) for v in
                    (ln1_g[0], ln1_b[0], ln2_g[0], ln2_b[0], ln1_g[1], ln1_b[1], ln2_g[1], ln2_b[1], ln_v_g[0], ln_v_b[0])]).astype(np.float32)
    gn = np.ascontiguousarray(np.broadcast_to(f(gla_norm_g)[0].reshape(1, 1024), (128, 1024)))
    cw = np.ascontiguousarray(f(ffn_conv_w).reshape(2, 3, FC, 128).transpose(3, 0, 2, 1)).reshape(128, 2 * FC * 3)
    cb = np.ascontiguousarray(f(ffn_conv_b).reshape(2, FC, 128).transpose(2, 0, 1)).reshape(128, 2 * FC)
    ws = f(w_spatial)[0]
    wst = np.where(causal[:, None, :], ws.transpose(2, 0, 1), np.float32(0.0)).astype(np.float32)
    wst = np.ascontiguousarray(wst).reshape(128, 8 * 128)
    bsp = np.ascontiguousarray(f(b_spatial)[0].T)
    common = dict(
        w_in_even=f(w_in_even)[0], w_out_even=f(w_out_even)[0], w_in_odd=f(w_in_odd)[0], w_out_odd=f(w_out_odd)[0],
        ffn_w1=f(ffn_w1), ffn_w2=f(ffn_w2), ffn_w3=f(ffn_w3), lnp=lnp,
        c_ident=np.eye(128, dtype=np.float32), c_tri2=tri2, c_sel=sel, c_maskt=maskt, c_gn=gn,
        c_wgu=f(w_gate_up)[0], c_bg=f(b_gate)[0].reshape(1, 512), c_ones=np.ones((1, 128), np.float32),
        c_cw=cw, c_cb=cb, c_wst=wst, c_bsp=bsp, c_bt=bt.reshape(128, -1), c_bts3=bts3.reshape(128, -1),
        c_bts4=bts4.reshape(32, -1), c_cbias=cbias)
    in_maps = []
    SPLIT = NMAIN * 128
    START1 = 4096 - SPLIT
    for c in range(8):
        b, half = c // 2, c % 2
        if half == 0:
            xm = x_prompt[b, 0:SPLIT]
            xpre = np.zeros((NPRE * 128, D), np.float32)
            pv = np.zeros((128, 1), np.float32)
        else:
            xm = x_prompt[b, START1:4096]
            xpre = x_prompt[b, 0:START1]
            pv = np.ones((128, 1), np.float32)
        m = dict(common)
        m.update(xm=np.ascontiguousarray(xm), xpre=np.ascontiguousarray(xpre), c_pv=pv, xs=x_sample[c],
                 ck=f(cache_attn_k)[0, c].reshape(512, 1024), cv=f(cache_attn_v)[0, c].reshape(512, 1024),
                 sg=np.ascontiguousarray(f(state_gla)[0, c].transpose(1, 0, 2)),
                 chs=np.ascontiguousarray(f(state_ffn_conv)[:, c].reshape(2, 2, FC, 128).transpose(3, 0, 2, 1)).reshape(128, 2 * FC * 2))
        in_maps.append(m)
    return in_maps


def kernel(x_prompt, x_sample, cache_attn_k, cache_attn_v, state_gla, state_ffn_conv,
           w_in_even, w_gate_up, b_gate, gla_norm_g, rel_bias, w_out_even,
           w_in_odd, ln_v_g, ln_v_b, w_spatial, b_spatial, w_out_odd,
           ffn_w1, ffn_w2, ffn_conv_w, ffn_conv_b, ffn_w3,
           ln1_g, ln1_b, ln2_g, ln2_b):
    in_maps = _prep(x_prompt, x_sample, cache_attn_k, cache_attn_v, state_gla, state_ffn_conv,
                    w_in_even, w_gate_up, b_gate, gla_norm_g, rel_bias, w_out_even,
                    w_in_odd, ln_v_g, ln_v_b, w_spatial, b_spatial, w_out_odd,
                    ffn_w1, ffn_w2, ffn_conv_w, ffn_conv_b, ffn_w3,
                    ln1_g, ln1_b, ln2_g, ln2_b)
    if "nc" not in _NC_CACHE:
        _NC_CACHE["nc"] = build()
    nc = _NC_CACHE["nc"]
    SPLIT = NMAIN * 128
    START1 = 4096 - SPLIT
    res = run_bass_kernel_spmd(nc, in_maps, core_ids=list(range(8)))
    R = res.results
    B = 4
    y_prompt = np.zeros((B, 4096, D), np.float32)
    k_p = np.zeros((1, B, 512, 8, 128), np.float32)
    v_p = np.zeros((1, B, 512, 8, 128), np.float32)
    gla_p = np.zeros((1, B, 4, 128, 256), np.float32)
    conv_p = np.zeros((2, B, 2, DFF), np.float32)
    y_sample = np.zeros((8, 32, D), np.float32)
    k_s = np.zeros((1, 8, 32, 8, 128), np.float32)
    v_s = np.zeros((1, 8, 32, 8, 128), np.float32)
    gla_sm = np.zeros((1, 8, 4, 128, 256), np.float32)
    conv_sm = np.zeros((2, 8, 2, DFF), np.float32)
    mlp_v = np.zeros((1, 8, 32, D), np.float32)

    def conv_unlay(a):
        return a.reshape(128, 2, FC, 2).transpose(1, 3, 2, 0).reshape(2, 2, DFF)
    for c in range(8):
        b, half = c // 2, c % 2
        r = R[c]
        if half == 0:
            y_prompt[b, 0:SPLIT] = r["y_main"]
        else:
            y_prompt[b, SPLIT:4096] = r["y_main"][SPLIT - START1:]
            k_p[0, b] = r["kout"].reshape(512, 8, 128)
            v_p[0, b] = r["vout"].reshape(512, 8, 128)
            gla_p[0, b] = r["gla_out"].reshape(128, 4, 256).transpose(1, 0, 2)
            conv_p[:, b] = conv_unlay(r["conv_out"])
        y_sample[c] = r["y_s"]
        k_s[0, c] = r["ks_out"].reshape(32, 8, 128)
        v_s[0, c] = r["vs_out"].reshape(32, 8, 128)
        gla_sm[0, c] = r["gla_s"].reshape(128, 4, 256).transpose(1, 0, 2)
        conv_sm[:, c] = conv_unlay(r["conv_s"])
        mlp_v[0, c] = r["mlpv_s"]
    return (y_prompt, y_sample, k_p, v_p, gla_p, conv_p, k_s, v_s, gla_sm, conv_sm, mlp_v)
```

```python
import numpy as np
import concourse.bass as bass
import concourse.mybir as mybir
from concourse.bass_utils import run_bass_kernel_spmd
from contextlib import ExitStack

F32 = mybir.dt.float32
BF16 = mybir.dt.bfloat16
AF = mybir.ActivationFunctionType
ALU = mybir.AluOpType
AX = mybir.AxisListType

D = 2048
KC = 16
DFF = 5504
FC = 43
NMAIN = 17
NPRE = 15
NTM = 2
NCMAX = NTM * 128
ALPHA = 4.0 ** 0.25
EPS = 1e-5
WCOL = 256
NSLOT = 4
SAFE_SAME_ENGINE = True


class Buf:
    __slots__ = ("name", "w", "r", "dsem", "dcnt")

    def __init__(self, name):
        self.name = name
        self.w = None
        self.r = {}
        self.dsem = None
        self.dcnt = 0


class Ker:
    ENG = {"pe": "tensor", "act": "scalar", "dve": "vector", "pool": "gpsimd", "sp": "sync"}

    def __init__(self, nc, es):
        self.nc = nc
        self.es = es
        self.plan = False
        self.sem = {e: es.enter_context(nc.semaphore("sem_" + e)) for e in self.ENG}
        self.dsems = []
        self.reset()

    def reset(self):
        self.q = {e: [] for e in self.ENG}
        self.cnt = {e: 0 for e in self.ENG}
        self.seen = {e: {} for e in self.ENG}
        self.dbufs = []

    def _wait(self, eng, deps):
        own = self.sem[eng]
        for sem, val in deps:
            if sem is own and (eng == "pe" or not SAFE_SAME_ENGINE):
                continue
            if self.seen[eng].get(sem, 0) < val:
                self.seen[eng][sem] = val
                self.q[eng].append(lambda e, s=sem, v=val: e.wait_ge(s, v))

    @staticmethod
    def _deps(R, W):
        deps = []
        for b in R:
            if b.w is not None:
                deps.append(b.w)
        for b in W:
            if b.w is not None:
                deps.append(b.w)
            deps.extend(b.r.items())
        return deps

    def op(self, eng, fn, R=(), W=()):
        if self.plan:
            return
        self._wait(eng, self._deps(R, W))
        self.cnt[eng] += 1
        c = self.cnt[eng]
        sem = self.sem[eng]
        self.q[eng].append(lambda e, f=fn, s=sem: f(e).then_inc(s, 1))
        for b in W:
            b.w = (sem, c)
            b.r = {}
        for b in R:
            if b not in W:
                b.r[sem] = c

    def dma(self, eng, pairs, buf, write, extraR=(), deps=()):
        if self.plan:
            return
        key = "w" if write else "r"
        if buf.dsem is None:
            buf.dsem = {}
            buf.dcnt = {}
        if key not in buf.dsem:
            buf.dsem[key] = self.es.enter_context(self.nc.semaphore("d%s_%s" % (key, buf.name)))
            buf.dcnt[key] = 0
        if buf not in self.dbufs:
            self.dbufs.append(buf)
        R = list(extraR) + ([] if write else [buf])
        W = [buf] if write else []
        self._wait(eng, self._deps(R, W) + list(deps))
        sem = buf.dsem[key]
        for (o, i) in pairs:
            self.q[eng].append(lambda e, o=o, i=i, s=sem: e.dma_start(out=o, in_=i).then_inc(s, 16))
        buf.dcnt[key] += 16 * len(pairs)
        if write:
            buf.w = (sem, buf.dcnt[key])
            buf.r = {}
        else:
            buf.r[sem] = buf.dcnt[key]
        for b in extraR:
            b.r[sem] = buf.dcnt[key]
        return (sem, buf.dcnt[key])

    def barrier(self, engs=("pe", "act", "dve")):
        if self.plan:
            return
        for e in engs:
            deps = [(self.sem[f], self.cnt[f]) for f in engs if f != e and self.cnt[f] > 0]
            self._wait(e, deps)

    def finish(self):
        deps = []
        for b in self.dbufs:
            for k_ in b.dsem:
                deps.append((b.dsem[k_], b.dcnt[k_]))
        for e in ("pe", "act", "dve"):
            if self.cnt[e] > 0:
                deps.append((self.sem[e], self.cnt[e]))
        self._wait("sp", deps)

    def emit(self):
        block = self.es.enter_context(self.nc.Block())
        for eng, attr in self.ENG.items():
            lst = self.q[eng]

            def body(e, lst=lst):
                for f in lst:
                    f(e)
            getattr(block, attr)(body)


def build():
    nc = bass.Bass("TRN2", target_bir_lowering=False)

    def din(name, shape):
        return nc.dram_tensor(name, list(shape), F32, kind="ExternalInput").ap()

    def dout(name, shape):
        return nc.dram_tensor(name, list(shape), F32, kind="ExternalOutput").ap()

    xm = din("xm", [NMAIN * 128, D])
    xpre = din("xpre", [NPRE * 128, D])
    xs = din("xs", [32, D])
    ck = din("ck", [512, 1024])
    cv = din("cv", [512, 1024])
    sg = din("sg", [128, 4, 256])
    chs_in = din("chs", [128, 2 * FC * 2])
    w_in_even = din("w_in_even", [D, 6160])
    w_out_even = din("w_out_even", [D, D])
    w_in_odd = din("w_in_odd", [D, 2 * D])
    w_out_odd = din("w_out_odd", [D, D])
    ffn_w1 = din("ffn_w1", [2, D, DFF])
    ffn_w2 = din("ffn_w2", [2, D, DFF])
    ffn_w3 = din("ffn_w3", [2, DFF, D])
    lnp = din("lnp", [10, 128, D])
    c_ident = din("c_ident", [128, 128])
    c_tri2 = din("c_tri2", [128, 128])
    c_sel = din("c_sel", [128, 2])
    c_maskt = din("c_maskt", [128, 128])
    c_gn = din("c_gn", [128, 1024])
    c_wgu = din("c_wgu", [16, 512])
    c_bg = din("c_bg", [1, 512])
    c_ones = din("c_ones", [1, 128])
    c_cw = din("c_cw", [128, 2 * FC * 3])
    c_cb = din("c_cb", [128, 2 * FC])
    c_wst = din("c_wst", [128, 8 * 128])
    c_bsp = din("c_bsp", [128, 8])
    c_pv = din("c_pv", [128, 1])
    c_bt = din("c_bt", [128, 3 * 8 * 128])
    c_bts3 = din("c_bts3", [128, 8 * 32])
    c_bts4 = din("c_bts4", [32, 8 * 32])
    c_cbias = din("c_cbias", [128, 8])

    y_main = dout("y_main", [NMAIN * 128, D])
    y_s = dout("y_s", [32, D])
    kout = dout("kout", [512, 1024])
    vout = dout("vout", [512, 1024])
    gla_out = dout("gla_out", [128, 1024])
    conv_out = dout("conv_out", [128, 2 * FC * 2])
    ks_out = dout("ks_out", [32, 1024])
    vs_out = dout("vs_out", [32, 1024])
    gla_s = dout("gla_s", [128, 1024])
    conv_s = dout("conv_s", [128, 2 * FC * 2])
    mlpv_s = dout("mlpv_s", [32, D])

    es = ExitStack()
    with es:
        def sb(name, shape, dt):
            return es.enter_context(nc.sbuf_tensor(name, list(shape), dt))

        X = [sb("X%d" % j, [128, D], F32) for j in range(NTM)]
        Xb = [Buf("X%d" % j) for j in range(NTM)]
        XB = sb("XB", [128, D], BF16); XBb = Buf("XB")
        XT = sb("XT", [128, KC, NCMAX], BF16); XTb = [Buf("XT%d" % j) for j in range(NTM)]
        NRING = 4 + NTM
        KT = sb("KT", [128, 8, NRING * 128], BF16); KTb = [Buf("KT%d" % s) for s in range(NRING)]
        VB = sb("VB", [128, NRING, 8, 130], BF16); VBb = [Buf("VB%d" % s) for s in range(NRING)]
        S = sb("S", [128, 4, 256], F32); Sb = Buf("S")
        SBf = sb("SBf", [128, 4, 256], BF16); SBfb = Buf("SBf")
        SBf2 = sb("SBf2", [128, 4, 256], BF16); SBf2b = Buf("SBf2")
        SS = sb("SS", [128, 4, 256], F32); SSb = Buf("SS")
        SSBf = sb("SSBf", [128, 4, 256], BF16); SSBfb = Buf("SSBf")
        STMP = sb("STMP", [128, 256], F32); STMPb = Buf("STMP")
        WS = [sb("WS%d" % i, [128, KC * WCOL], BF16) for i in range(NSLOT)]
        WSb = [Buf("WS%d" % i) for i in range(NSLOT)]
        LNG = sb("LNG", [128, D], F32); LNGb = Buf("LNG")
        LNB = sb("LNB", [128, D], F32); LNBb = Buf("LNB")
        BT = sb("BT", [128, 3, 8, 128], F32)
        BTS3 = sb("BTS3", [128, 8, 32], F32)
        BTS4 = sb("BTS4", [32, 8, 32], F32)
        CBIAS = sb("CBIAS", [128, 8], F32)
        IDENT = sb("IDENT", [128, 128], BF16)
        TRI2 = sb("TRI2", [128, 128], F32)
        SEL = sb("SEL", [128, 2], F32)
        MASKT = sb("MASKT", [128, 128], F32)
        GN = sb("GN", [128, 1024], F32)
        WGU = sb("WGU", [16, 512], BF16)
        BG = sb("BG", [1, 512], BF16)
        ONES = sb("ONES", [1, 128], BF16)
        CW = sb("CW", [128, 2, FC, 3], F32)
        CBc = sb("CBc", [128, 2, FC], F32)
        WST = sb("WST", [128, 8, 128], BF16); WSTb = Buf("WST")
        BSP = sb("BSP", [128, 8], F32)
        PV = sb("PV", [128, 1], F32)
        CH = sb("CH", [128, 2, FC, 2], F32); CHb = Buf("CH")
        CHS = sb("CHS", [128, 2, FC, 2], F32); CHSb = Buf("CHS")
        STG = [sb("STG%d" % i, [128, 256], F32) for i in range(4)]
        STGb = [Buf("STG%d" % i) for i in range(4)]
        SMALL = sb("SMALL", [128, 64], F32); SMALLb = Buf("SMALL")
        CONSTb = Buf("CONST")
        CONSTPb = Buf("CONSTP")

        NB = 21900
        NF = 4096
        SCRB = sb("SCRB", [128, NB], BF16)
        SCRF = sb("SCRF", [128, NF], F32)

        psS = [es.enter_context(nc.psum_tensor("psS%d" % i, [128, 512], F32)) for i in range(4)]
        psSb = [Buf("psS%d" % i) for i in range(4)]
        psL = [es.enter_context(nc.psum_tensor("psL%d" % i, [128, 512], F32)) for i in range(2)]
        psLb = [Buf("psL%d" % i) for i in range(2)]
        psT = [es.enter_context(nc.psum_tensor("psT%d" % i, [128, 1024], BF16)) for i in range(2)]
        psTb = [Buf("psT%d" % i) for i in range(2)]

        K = Ker(nc, es)
        st = {}

        def getS():
            i = st["iS"] % 4
            st["iS"] += 1
            return psS[i], psSb[i]

        def getT():
            i = st["iT"] % 2
            st["iT"] += 1
            return psT[i], psTb[i]

        def getSTG():
            i = st["iG"] % 4
            st["iG"] += 1
            return STG[i], STGb[i]

        def wnext(wap, r0, nk, c0, n, live=0):
            if K.plan:
                st["pieces"].append((wap, r0, nk, c0, n, st["mp"], st["mpk"]))
                if st["mp"] == 0:
                    st["scr_off"].append(st["scr_total"])
                    st["scr_total"] += nk * n
                st["mpk"] += 1
                return None, None
            i = st["wi"]
            exp = st["pieces"][i]
            assert exp[1:5] == (r0, nk, c0, n), (exp[1:5], (r0, nk, c0, n))
            while st["wissued"] < min(len(st["pieces"]), i + NSLOT - live):
                pi = st["wissued"]
                (pw, pr0, pnk, pc0, pn, pmp, pk) = st["pieces"][pi]
                slot = pi % NSLOT
                if pmp >= 1:
                    off = st["scr_off"][pk]
                    sz = pnk * pn
                    K.dma("pool", [(WS[slot][:, 0:sz], st["scratch"][:, off:off + sz])], WSb[slot], write=True, deps=[st["store_dep"][pk]])
                else:
                    pairs = []
                    kstep = 4
                    for k0 in range(0, pnk, kstep):
                        k1 = min(pnk, k0 + kstep)
                        o = WS[slot][:, k0 * pn:k1 * pn].rearrange("p (k n) -> p k n", n=pn)
                        src = pw[pr0 + k0 * 128: pr0 + k1 * 128, pc0:pc0 + pn].rearrange("(k p) n -> p k n", p=128)
                        pairs.append((o, src))
                    K.dma("pool", pairs, WSb[slot], write=True)
                st["wissued"] += 1
            st["wi"] += 1
            slot = i % NSLOT
            if exp[5] == 0:
                off = st["scr_off"][exp[6]]
                sz = nk * n
                st["store_dep"][exp[6]] = K.dma("sp", [(st["scratch"][:, off:off + sz], WS[slot][:, 0:sz])], WSb[slot], write=False)
            view = WS[slot][:, 0:nk * n].rearrange("p (k n) -> p k n", n=n)
            return view, WSb[slot]

        def to_T(src, srcb, ntok, col0, dstbufs):
            for half in range(2):
                pt, ptb = getT()

                def f(e, pt=pt, half=half):
                    ins = None
                    for i in range(8):
                        kc = half * 8 + i
                        ins = e.transpose(out=pt[:, i * 128:(i + 1) * 128], in_=src[:, kc * 128:(kc + 1) * 128], identity=IDENT[:, :])
                    return ins
                K.op("pe", f, R=[srcb, CONSTPb], W=[ptb])
                pv = pt[:, :].rearrange("p (i t) -> p i t", t=128)[:, :, 0:ntok]
                dst = XT[:, half * 8:(half + 1) * 8, col0:col0 + ntok]
                K.op("dve", lambda e, pv=pv, dst=dst: e.tensor_copy(out=dst, in_=pv), R=[ptb], W=dstbufs)

        def cast_T(j, ntok, col0):
            K.op("dve", lambda e: e.tensor_copy(out=XB[0:ntok, :], in_=X[j][0:ntok, :]), R=[Xb[j]], W=[XBb])
            to_T(XB, XBb, ntok, col0, [XTb[j]])

        def load_ln(gi):
            K.dma("sp", [(LNG[:, :], lnp[gi])], LNGb, write=True)
            K.dma("sp", [(LNB[:, :], lnp[gi + 1])], LNBb, write=True)

        def layer_norm(xt, xtb, ntok):
            st6 = SMALL[0:ntok, 0:24]
            mv = SMALL[0:ntok, 24:26]
            sd = SMALL[0:ntok, 26:27]
            rs = SMALL[0:ntok, 27:28]

            def f1(e):
                ins = None
                for i in range(4):
                    ins = e.bn_stats(out=SMALL[0:ntok, i * 6:(i + 1) * 6], in_=xt[0:ntok, i * 512:(i + 1) * 512])
                return ins
            K.op("dve", f1, R=[xtb], W=[SMALLb])
            K.op("dve", lambda e: e.bn_aggr(out=mv, in_=st6), R=[], W=[SMALLb])
            K.op("dve", lambda e: e.tensor_scalar(out=sd, in0=SMALL[0:ntok, 25:26], scalar1=EPS, scalar2=None, op0=ALU.add),
                 R=[], W=[SMALLb])
            K.op("act", lambda e: e.activation(out=sd, in_=sd, func=AF.Sqrt), R=[], W=[SMALLb])
            K.op("dve", lambda e: e.reciprocal(out=rs, in_=sd), R=[], W=[SMALLb])
            K.op("dve", lambda e: e.tensor_scalar(out=xt[0:ntok, :], in0=xt[0:ntok, :], scalar1=SMALL[0:ntok, 24:25],
                                                  scalar2=rs, op0=ALU.subtract, op1=ALU.mult), R=[SMALLb], W=[xtb])
            K.op("dve", lambda e: e.tensor_tensor(out=xt[0:ntok, :], in0=xt[0:ntok, :], in1=LNG[0:ntok, :], op=ALU.mult),
                 R=[LNGb], W=[xtb])
            K.op("dve", lambda e: e.tensor_tensor(out=xt[0:ntok, :], in0=xt[0:ntok, :], in1=LNB[0:ntok, :], op=ALU.add),
                 R=[LNBb], W=[xtb])

        def proj_tok(wv, wb, nk, xt_cols, ntok, xtbufs, ncols):
            ps, psb = getS()

            def f(e):
                ins = None
                for k in range(nk):
                    ins = e.matmul(ps[0:ntok, 0:ncols], lhsT=XT[:, k, xt_cols:xt_cols + ntok], rhs=wv[:, k, 0:ncols],
                                   start=(k == 0), stop=(k == nk - 1))
                return ins
            K.op("pe", f, R=[wb] + xtbufs, W=[psb])
            return ps, psb

        def out_rows(ps, psb, ntok, ncols, dst_ap):
            sg_, sgb = getSTG()
            K.op("dve", lambda e: e.tensor_copy(out=sg_[0:ntok, 0:ncols], in_=ps[0:ntok, 0:ncols]), R=[psb], W=[sgb])
            K.dma("sp", [(dst_ap, sg_[0:ntok, 0:ncols])], sgb, write=False)

        def gla_block(ntok, nch, QKAj, QKAb, VAj, VAb, LGj, LGb, Sx, Sxb, SBl, SBlb, need_out, MIXj, MIXb, GOGj, GOGb, scr):
            csz = ntok // nch
            ENB, EB, KD, QD, QDTf, QDT0, QDT1, KDT, ATm, EBLt, SQ, scrb = scr
            bps, bpsb = getS()
            K.op("pe", lambda e: e.matmul(bps[0:ntok, 0:512], lhsT=TRI2[0:ntok, 0:ntok], rhs=LGj[0:ntok, :], start=True, stop=True),
                 R=[LGb, CONSTb], W=[bpsb])
            blps, blpsb = getS()

            def fbl(e):
                ins = None
                for h in range(4):
                    for c_ in range(nch):
                        ins = e.matmul(blps[:, c_ * 4 + h:c_ * 4 + h + 1], lhsT=LGj[0:ntok, h * 128:(h + 1) * 128], rhs=SEL[0:ntok, c_:c_ + 1],
                                       start=True, stop=True)
                return ins
            K.op("pe", fbl, R=[LGb, CONSTb], W=[blpsb])
            K.op("act", lambda e: e.activation(out=ENB[0:ntok, :], in_=bps[0:ntok, 0:512], func=AF.Exp, scale=-1.0), R=[bpsb], W=[scrb["ENB"]])
            if need_out:
                K.op("act", lambda e: e.activation(out=EB[0:ntok, :], in_=bps[0:ntok, 0:512], func=AF.Exp), R=[bpsb], W=[scrb["EB"]])
            K.op("act", lambda e: e.activation(out=EBLt[:, 0:4 * nch], in_=blps[:, 0:4 * nch], func=AF.Exp), R=[blpsb], W=[scrb["EBL"]])
            dbg0("g_b")
            K.op("dve", lambda e: e.tensor_tensor(out=KD[0:ntok, :], in0=QKAj[0:ntok, 512:1024], in1=ENB[0:ntok, :], op=ALU.mult),
                 R=[QKAb, scrb["ENB"]], W=[scrb["KD"]])
            if need_out:
                K.op("dve", lambda e: e.tensor_tensor(out=QD[0:ntok, :], in0=QKAj[0:ntok, 0:512], in1=EB[0:ntok, :], op=ALU.mult),
                     R=[QKAb, scrb["EB"]], W=[scrb["QD"]])
                dbg0("g_qd")
                pt, ptb = getT()

                def ftr(e):
                    ins = None
                    for h in range(4):
                        ins = e.transpose(out=pt[:, h * 128:(h + 1) * 128], in_=QD[:, h * 128:(h + 1) * 128], identity=IDENT[:, :])
                    for h in range(4):
                        ins = e.transpose(out=pt[:, 512 + h * 128:512 + (h + 1) * 128], in_=KD[:, h * 128:(h + 1) * 128], identity=IDENT[:, :])
                    return ins
                K.op("pe", ftr, R=[scrb["QD"], scrb["KD"], CONSTPb], W=[ptb])
                dbg0("g_trp")
                pq = pt[:, 0:512].rearrange("p (h t) -> p h t", t=128)
                pk = pt[:, 512:1024].rearrange("p (h t) -> p h t", t=128)
                K.op("dve", lambda e: e.tensor_copy(out=QDTf[:, :, 0:ntok], in_=pq[:, :, 0:ntok]), R=[ptb], W=[scrb["QDTf"]])
                K.op("dve", lambda e: e.tensor_copy(out=KDT[:, :, 0:ntok], in_=pk[:, :, 0:ntok]), R=[ptb], W=[scrb["KDT"]])
                dbg0("g_ev")
                if nch == 2:
                    K.op("dve", lambda e: e.memset(QDT0[:, :, :], 0.0), R=[], W=[scrb["QDT0"]])
                    K.op("dve", lambda e: e.memset(QDT1[:, :, :], 0.0), R=[], W=[scrb["QDT1"]])
                    K.op("dve", lambda e: e.tensor_copy(out=QDT0[:, :, 0:64], in_=pq[:, :, 0:64]), R=[ptb], W=[scrb["QDT0"]])
                    K.op("dve", lambda e: e.tensor_copy(out=QDT1[:, :, 64:128], in_=pq[:, :, 64:128]), R=[ptb], W=[scrb["QDT1"]])
                dbg0("g_tr")
                aps, apsb = getS()

                def fat(e):
                    ins = None
                    for h in range(4):
                        ins = e.matmul(aps[0:ntok, h * 128:h * 128 + ntok], lhsT=KDT[:, h, 0:ntok], rhs=QDTf[:, h, 0:ntok], start=True, stop=True)
                    return ins
                K.op("pe", fat, R=[scrb["KDT"], scrb["QDTf"]], W=[apsb])

                def fmask(e):
                    ins = None
                    for h in range(4):
                        ins = e.tensor_tensor(out=ATm[0:ntok, h, 0:ntok], in0=aps[0:ntok, h * 128:h * 128 + ntok], in1=MASKT[0:ntok, 0:ntok], op=ALU.mult)
                    return ins
                K.op("dve", fmask, R=[apsb, CONSTb], W=[scrb["ATm"]])
                ops_ = [(psL[0], psLb[0]), (psL[1], psLb[1])]
            dbg0("g_at")

            def o_groups():
                for hp in range(2):
                    for hh in range(2):
                        h = hp * 2 + hh

                        def fo(e, hp=hp, hh=hh, h=h):
                            dst = ops_[hp][0][0:ntok, hh * 256:(hh + 1) * 256]
                            if nch == 1:
                                e.matmul(dst, lhsT=QDTf[:, h, 0:ntok], rhs=SBl[0][:, h, :], start=True, stop=False)
                            else:
                                e.matmul(dst, lhsT=QDT0[:, h, 0:ntok], rhs=SBl[0][:, h, :], start=True, stop=False)
                                e.matmul(dst, lhsT=QDT1[:, h, 0:ntok], rhs=SBl[1][:, h, :], start=False, stop=False)
                            return e.matmul(dst, lhsT=ATm[0:ntok, h, 0:ntok], rhs=VAj[0:ntok, h * 256:(h + 1) * 256], start=False, stop=True)
                        K.op("pe", fo, R=[scrb["QDTf"], scrb["QDT0"], scrb["QDT1"], scrb["ATm"], VAb] + SBlb, W=[ops_[hp][1]])
            if need_out and nch == 1:
                o_groups()
            for c in range(nch):
                r0 = c * csz
                if need_out and nch == 2 and c == 1:
                    o_groups()
                for hp in range(2):
                    dps, dpsb = getS()

                    def fds(e, hp=hp, dps=dps, r0=r0):
                        ins = None
                        for hh in range(2):
                            h = hp * 2 + hh
                            ins = e.matmul(dps[:, hh * 256:(hh + 1) * 256], lhsT=KD[r0:r0 + csz, h * 128:(h + 1) * 128],
                                           rhs=VAj[r0:r0 + csz, h * 256:(h + 1) * 256], start=True, stop=True)
                        return ins
                    K.op("pe", fds, R=[scrb["KD"], VAb], W=[dpsb])
                    for hh in range(2):
                        h = hp * 2 + hh
                        ebl = EBLt[:, c * 4 + h:c * 4 + h + 1]
                        K.op("dve", lambda e, h=h, ebl=ebl: e.tensor_scalar(out=STMP[:, :], in0=Sx[:, h, :], scalar1=ebl, scalar2=None, op0=ALU.mult),
                             R=[Sxb, scrb["EBL"]], W=[STMPb])
                        K.op("dve", lambda e, h=h, hh=hh, ebl=ebl, dps=dps: e.scalar_tensor_tensor(
                            out=Sx[:, h, :], in0=dps[:, hh * 256:(hh + 1) * 256], scalar=ebl, in1=STMP[:, :], op0=ALU.mult, op1=ALU.add),
                            R=[dpsb, STMPb, scrb["EBL"]], W=[Sxb])
                if need_out and nch == 2:
                    nb_ = (c + 1) % 2
                    K.op("act", lambda e, nb_=nb_: e.activation(out=SBl[nb_][:, :, :].rearrange("p h v -> p (h v)"), in_=Sx[:, :, :].rearrange("p h v -> p (h v)"), func=AF.Copy),
                         R=[Sxb], W=[SBlb[nb_]])
            dbg0("g_o")
            if need_out:
                for h in range(4):
                    hp, hh = h // 2, h % 2
                    osrc = ops_[hp][0][0:ntok, hh * 256:(hh + 1) * 256]
                    K.op("act", lambda e, osrc=osrc: e.activation(out=SQ[0:ntok, :], in_=osrc, func=AF.Square), R=[ops_[hp][1]], W=[scrb["SQ"]])
                    K.op("dve", lambda e, h=h: e.reduce_sum(out=SMALL[0:ntok, 32 + h:33 + h], in_=SQ[0:ntok, :], axis=AX.X), R=[scrb["SQ"]], W=[SMALLb])
                K.op("dve", lambda e: e.tensor_scalar(out=SMALL[0:ntok, 36:40], in0=SMALL[0:ntok, 32:36], scalar1=1.0 / 256.0, scalar2=EPS,
                                                      op0=ALU.mult, op1=ALU.add), R=[], W=[SMALLb])
                K.op("act", lambda e: e.activation(out=SMALL[0:ntok, 36:40], in_=SMALL[0:ntok, 36:40], func=AF.Sqrt), R=[], W=[SMALLb])
                K.op("dve", lambda e: e.reciprocal(out=SMALL[0:ntok, 40:44], in_=SMALL[0:ntok, 36:40]), R=[], W=[SMALLb])
                for h in range(4):
                    hp, hh = h // 2, h % 2
                    osrc = ops_[hp][0][0:ntok, hh * 256:(hh + 1) * 256]
                    K.op("dve", lambda e, h=h, osrc=osrc: e.scalar_tensor_tensor(
                        out=MIXj[0:ntok, h * 256:(h + 1) * 256], in0=osrc, scalar=SMALL[0:ntok, 40 + h:41 + h],
                        in1=GOGj[0:ntok, h * 256:(h + 1) * 256], op0=ALU.mult, op1=ALU.mult),
                        R=[ops_[hp][1], SMALLb, GOGb], W=[MIXb])

        def attn_head(h, nq, qap, qb, blocks, PTt, PTb, TMPt, TMPb, dst, dstb, dflag=False):
            sc = [getS(), getS()]

            def fsc(e):
                ins = None
                for bi, (kT, kb, v, vb, nk, bias) in enumerate(blocks):
                    ps = sc[bi // 4][0]
                    ins = e.matmul(ps[0:nk, (bi % 4) * 128:(bi % 4) * 128 + nq], lhsT=kT, rhs=qap, start=True, stop=True)
                return ins
            K.op("pe", fsc, R=[qb] + [b[1] for b in blocks], W=[sc[0][1], sc[1][1]])
            if dflag:
                dbg0("a_sc")
            for bi, (kT, kb, v, vb, nk, bias) in enumerate(blocks):
                ps, psb = sc[bi // 4]
                src = ps[0:nk, (bi % 4) * 128:(bi % 4) * 128 + nq]
                dstp = PTt[0:nk, bi, 0:nq]
                import os as _os6
                _sk = _os6.environ.get("KDBG_SKIP", "")
                if dflag and (("b%d" % bi) in _sk.split(",")):
                    continue
                if bias is None:
                    K.op("act", lambda e, src=src, dstp=dstp, nk=nk: e.activation(out=dstp, in_=src, func=AF.Exp, bias=CBIAS[0:nk, h:h + 1]),
                         R=[psb, CONSTb], W=[PTb])
                else:
                    K.op("dve", lambda e, src=src, bias=bias: e.tensor_tensor(out=src, in0=src, in1=bias, op=ALU.add), R=[CONSTb], W=[psb])
                    K.op("act", lambda e, src=src, dstp=dstp: e.activation(out=dstp, in_=src, func=AF.Exp), R=[psb], W=[PTb])
            if dflag:
                dbg0("a_exp")
            ob, obb = getS()

            def fpv(e):
                ins = None
                nb = len(blocks)
                for bi, (kT, kb, v, vb, nk, bias) in enumerate(blocks):
                    ins = e.matmul(ob[0:nq, 0:129], lhsT=PTt[0:nk, bi, 0:nq], rhs=v, start=(bi == 0), stop=(bi == nb - 1))
                return ins
            K.op("pe", fpv, R=[PTb] + [b[3] for b in blocks], W=[obb])
            if dflag:
                dbg0("a_pv")
            K.op("dve", lambda e: e.reciprocal(out=SMALL[0:nq, 48:49], in_=ob[0:nq, 128:129]), R=[obb], W=[SMALLb])
            K.op("dve", lambda e: e.tensor_scalar(out=dst, in0=ob[0:nq, 0:128], scalar1=SMALL[0:nq, 48:49], scalar2=None, op0=ALU.mult),
                 R=[obb, SMALLb], W=[dstb])

        def ffn(l, blocks, segs, ncols):
            HT = [SCRB[:, i * 12 * NCMAX:(i + 1) * 12 * NCMAX].rearrange("p (c t) -> p c t", t=NCMAX) for i in range(2)]
            HTb = [Buf("HT0"), Buf("HT1")]
            U = [SCRF[:, i * 264:(i + 1) * 264] for i in range(2)]
            Ub = [Buf("U0"), Buf("U1")]
            C = [SCRF[:, 528 + i * 256:528 + (i + 1) * 256] for i in range(2)]
            Cb = [Buf("C0"), Buf("C1")]
            w1 = ffn_w1[l]
            w2 = ffn_w2[l]
            w3 = ffn_w3[l]
            for j, ntok, col0 in blocks:
                cast_T(j, ntok, col0)
            xtb = [XTb[j] for j, _, _ in blocks]
            groups = [(0, 12), (12, 12), (24, 12), (36, 7)]
            it = 0
            for gi, (c0g, ncg) in enumerate(groups):
                ht = HT[gi % 2]
                htb = HTb[gi % 2]
                for pc in range(0, ncg, 2):
                    npc = min(2, ncg - pc)
                    fc0 = c0g + pc
                    w1v, w1b = wnext(w1, 0, KC, fc0 * 128, npc * 128)
                    w2v, w2b = wnext(w2, 0, KC, fc0 * 128, npc * 128, live=1)
                    for cc in range(npc):
                        fc = fc0 + cc
                        fcl = pc + cc
                        ups, upsb = getS()
                        zps, zpsb = getS()

                        def fu(e, wv=w1v, ps=ups, cc=cc):
                            ins = None
                            for k in range(KC):
                                ins = e.matmul(ps[:, 0:ncols], lhsT=wv[:, k, cc * 128:(cc + 1) * 128], rhs=XT[:, k, 0:ncols], start=(k == 0), stop=(k == KC - 1))
                            return ins
                        if not K.plan:
                            K.op("pe", fu, R=[w1b] + xtb, W=[upsb])
                            K.op("pe", lambda e, wv=w2v, ps=zps, cc=cc: fu(e, wv, ps, cc), R=[w2b] + xtb, W=[zpsb])
                        for (sc0, n, Hh, Hb) in segs:
                            u = U[it % 2]
                            ub = Ub[it % 2]
                            c = C[it % 2]
                            cb = Cb[it % 2]
                            it += 1
                            K.op("act", lambda e, u=u, ups=ups, sc0=sc0, n=n: e.activation(func=AF.Copy, out=u[:, 2:2 + n], in_=ups[:, sc0:sc0 + n]), R=[upsb], W=[ub])
                            K.op("act", lambda e, u=u, Hh=Hh, fc=fc: e.activation(func=AF.Copy, out=u[:, 0:2], in_=Hh[:, l, fc, :]), R=[Hb], W=[ub])
                            K.op("act", lambda e, c=c, ups=ups, sc0=sc0, n=n, fc=fc: e.activation(
                                out=c[:, 0:n], in_=ups[:, sc0:sc0 + n], func=AF.Identity, bias=CBc[:, l, fc:fc + 1], scale=CW[:, l, fc, 2:3]),
                                R=[upsb, CONSTb], W=[cb])
                            K.op("dve", lambda e, c=c, u=u, n=n, fc=fc: e.scalar_tensor_tensor(
                                out=c[:, 0:n], in0=u[:, 1:1 + n], scalar=CW[:, l, fc, 1:2], in1=c[:, 0:n], op0=ALU.mult, op1=ALU.add),
                                R=[ub, CONSTb], W=[cb])
                            K.op("dve", lambda e, c=c, u=u, n=n, fc=fc: e.scalar_tensor_tensor(
                                out=c[:, 0:n], in0=u[:, 0:n], scalar=CW[:, l, fc, 0:1], in1=c[:, 0:n], op0=ALU.mult, op1=ALU.add),
                                R=[ub, CONSTb], W=[cb])
                            K.op("act", lambda e, c=c, n=n: e.activation(out=c[:, 0:n], in_=c[:, 0:n], func=AF.Gelu_apprx_tanh), R=[], W=[cb])
                            K.op("dve", lambda e, c=c, n=n, ht=ht, fcl=fcl, zps=zps, sc0=sc0: e.tensor_tensor(
                                out=ht[:, fcl, sc0:sc0 + n], in0=c[:, 0:n], in1=zps[:, sc0:sc0 + n], op=ALU.mult), R=[cb, zpsb], W=[htb])
                            K.op("act", lambda e, u=u, Hh=Hh, fc=fc, n=n: e.activation(func=AF.Copy, out=Hh[:, l, fc, :], in_=u[:, n:n + 2]), R=[ub], W=[Hb])
                for cbk in range(D // WCOL):
                    w3v, w3b = wnext(w3, c0g * 128, ncg, cbk * WCOL, WCOL)
                    for j, ntok, col0 in blocks:
                        ps, psb = getS()

                        def f3(e, ps=ps, wv=w3v, ntok=ntok, col0=col0, ht=ht, ncg=ncg):
                            ins = None
                            for k in range(ncg):
                                ins = e.matmul(ps[0:ntok, 0:WCOL], lhsT=ht[:, k, col0:col0 + ntok], rhs=wv[:, k, :], start=(k == 0), stop=(k == ncg - 1))
                            return ins
                        if not K.plan:
                            K.op("pe", f3, R=[w3b, htb], W=[psb])
                        xd = X[j][0:ntok, cbk * WCOL:(cbk + 1) * WCOL]
                        if gi == 0:
                            K.op("dve", lambda e, xd=xd, ps=ps, ntok=ntok: e.scalar_tensor_tensor(
                                out=xd, in0=xd, scalar=ALPHA, in1=ps[0:ntok, 0:WCOL], op0=ALU.mult, op1=ALU.add), R=[psb], W=[Xb[j]])
                        else:
                            K.op("dve", lambda e, xd=xd, ps=ps, ntok=ntok: e.tensor_tensor(out=xd, in0=xd, in1=ps[0:ntok, 0:WCOL], op=ALU.add),
                                 R=[psb], W=[Xb[j]])

        def out_proj(wap, src_fn, blocks):
            for j, ntok, col0 in blocks:
                s_, sb_ = src_fn(j)
                to_T(s_, sb_, ntok, col0, [XTb[j]])
            for cbk in range(D // WCOL):
                wv, wb = wnext(wap, 0, KC, cbk * WCOL, WCOL)
                for j, ntok, col0 in blocks:
                    if K.plan:
                        continue
                    ps, psb = proj_tok(wv, wb, KC, col0, ntok, [XTb[j]], WCOL)
                    xd = X[j][0:ntok, cbk * WCOL:(cbk + 1) * WCOL]
                    K.op("dve", lambda e, xd=xd, ps=ps, ntok=ntok: e.scalar_tensor_tensor(
                        out=xd, in0=xd, scalar=ALPHA, in1=ps[0:ntok, 0:WCOL], op0=ALU.mult, op1=ALU.add), R=[psb], W=[Xb[j]])

        def carveA():
            o = [0]

            def tb(n):
                a = SCRB[:, o[0]:o[0] + n]
                o[0] += n
                return a
            of = [0]

            def tf(n):
                a = SCRF[:, of[0]:of[0] + n]
                of[0] += n
                return a
            A = {}
            A["QKA"] = [tb(1024) for _ in range(NTM)]
            A["VA"] = [tb(1024) for _ in range(NTM)]
            A["GOG"] = [tb(1024) for _ in range(NTM)]
            A["MIX"] = [tb(2048) for _ in range(NTM)]
            A["QT"] = tb(8 * NCMAX).rearrange("p (h t) -> p h t", t=NCMAX)
            A["KTs"] = tb(8 * 32).rearrange("p (h t) -> p h t", t=32)
            A["VSN"] = tb(8 * 130).rearrange("p (h d) -> p h d", d=130)
            A["GLT"] = tb(NCMAX)
            A["KD"] = tb(512)
            A["QD"] = tb(512)
            A["QDTf"] = tb(512).rearrange("p (h t) -> p h t", t=128)
            A["QDT0"] = tb(512).rearrange("p (h t) -> p h t", t=128)
            A["QDT1"] = tb(512).rearrange("p (h t) -> p h t", t=128)
            A["KDT"] = tb(512).rearrange("p (h t) -> p h t", t=128)
            A["ATm"] = tb(512).rearrange("p (h t) -> p h t", t=128)
            A["PT"] = [tb(640).rearrange("p (b t) -> p b t", t=128) for _ in range(2)]
            A["KSh"] = [tb(512).rearrange("p (b d) -> p b d", d=128) for _ in range(2)]
            A["KTSh"] = [tb(512) for _ in range(2)]
            A["VSh"] = [tb(4 * 130).rearrange("p (b d) -> p b d", d=130) for _ in range(2)]
            A["LOGG"] = [tf(512) for _ in range(NTM)]
            A["ENB"] = tf(512)
            A["EB"] = tf(512)
            A["SQ"] = tf(256)
            A["TMP"] = tf(256).rearrange("p (b t) -> p b t", t=128)
            A["EBLt"] = tf(8)
            A["LTMP"] = tf(512)
            assert o[0] <= NB and of[0] <= NF, (o[0], of[0])
            Ab = {}
            for k in ["QKA", "VA", "GOG", "MIX", "LOGG"]:
                Ab[k] = [Buf(k + str(i)) for i in range(NTM)]
            for k in ["QT", "KTs", "VSN", "GLT", "KD", "QD", "QDTf", "QDT0", "QDT1", "KDT", "ATm", "ENB", "EB", "SQ", "EBL", "LTMP"]:
                Ab[k] = Buf(k)
            for k in ["PT", "KSh", "KTSh", "VSh", "TMP"]:
                Ab[k] = [Buf(k + "0"), Buf(k + "1")]
            return A, Ab

        def mixer0(blocks, ncols, A, Ab, mode, kout_rows, sample_blk):
            W = w_in_even
            xtb = [XTb[b[0]] for b in blocks]
            main_blocks = [b for b in blocks if b[3] == "main"]
            nmain = 128 * len(main_blocks)
            full = (mode == "main")
            kv = mode in ("main", "prekv")

            def tok_piece(c0, n, evac):
                wv, wb = wnext(W, 0, KC, c0, n)
                for (j, ntok, col0, kind) in blocks:
                    if K.plan:
                        continue
                    ps, psb = proj_tok(wv, wb, KC, col0, ntok, [XTb[j]], n)
                    evac(j, ntok, kind, ps, psb)
            if full:
                for pi in range(2):
                    tok_piece(pi * 256, 256, lambda j, ntok, kind, ps, psb, pi=pi: K.op(
                        "act", lambda e: e.activation(out=A["QKA"][j][0:ntok, pi * 256:(pi + 1) * 256], in_=ps[0:ntok, 0:256], func=AF.Copy,
                                                      scale=128.0 ** -0.5), R=[psb], W=[Ab["QKA"][j]]))
            for pi in range(2):
                tok_piece(512 + pi * 256, 256, lambda j, ntok, kind, ps, psb, pi=pi: K.op(
                    "dve", lambda e: e.tensor_copy(out=A["QKA"][j][0:ntok, 512 + pi * 256:512 + (pi + 1) * 256], in_=ps[0:ntok, 0:256]),
                    R=[psb], W=[Ab["QKA"][j]]))
            for pi in range(4):
                tok_piece(1024 + pi * 256, 256, lambda j, ntok, kind, ps, psb, pi=pi: K.op(
                    "act", lambda e: e.activation(func=AF.Copy, out=A["VA"][j][0:ntok, pi * 256:(pi + 1) * 256], in_=ps[0:ntok, 0:256]), R=[psb], W=[Ab["VA"][j]]))
            dbg0("m_va")
            wv, wb = wnext(W, 0, KC, 2048, 16)
            if not K.plan:
                ps, psb = getS()

                def fg(e, ps=ps, wv=wv):
                    ins = None
                    for k in range(KC):
                        ins = e.matmul(ps[0:16, 0:ncols], lhsT=wv[:, k, 0:16], rhs=XT[:, k, 0:ncols], start=(k == 0), stop=(k == KC - 1))
                    return ins
                K.op("pe", fg, R=[wb] + xtb, W=[psb])
                K.op("dve", lambda e, ps=ps: e.tensor_copy(out=A["GLT"][0:16, 0:ncols], in_=ps[0:16, 0:ncols]), R=[psb], W=[Ab["GLT"]])
                if True:
                    pass
                for (j, ntok, col0, kind) in blocks:
                    gps, gpsb = getS()

                    def fgate(e, gps=gps, ntok=ntok, col0=col0):
                        e.matmul(gps[0:ntok, 0:512], lhsT=A["GLT"][0:16, col0:col0 + ntok], rhs=WGU[0:16, :], start=True, stop=False)
                        return e.matmul(gps[0:ntok, 0:512], lhsT=ONES[0:1, 0:ntok], rhs=BG[0:1, :], start=False, stop=True)
                    K.op("pe", fgate, R=[Ab["GLT"], CONSTPb], W=[gpsb])
                    K.op("act", lambda e, gps=gps, ntok=ntok: e.activation(out=A["LTMP"][0:ntok, :], in_=gps[0:ntok, 0:512], func=AF.Exp, scale=-1.0),
                         R=[gpsb], W=[Ab["LTMP"]])
                    K.op("dve", lambda e, ntok=ntok: e.tensor_scalar(out=A["LTMP"][0:ntok, :], in0=A["LTMP"][0:ntok, :], scalar1=1.0, scalar2=None, op0=ALU.add),
                         R=[], W=[Ab["LTMP"]])
                    K.op("act", lambda e, j=j, ntok=ntok: e.activation(out=A["LOGG"][j][0:ntok, :], in_=A["LTMP"][0:ntok, :], func=AF.Ln),
                         R=[Ab["LTMP"]], W=[Ab["LOGG"][j]])
            dbg0("m_gate")
            if full:
                def ev_og(j, ntok, kind, ps, psb, pi):
                    dst = A["GOG"][j][0:ntok, pi * 256:(pi + 1) * 256]
                    K.op("act", lambda e: e.activation(out=dst, in_=ps[0:ntok, 0:256], func=AF.Silu), R=[psb], W=[Ab["GOG"][j]])
                    K.op("dve", lambda e: e.tensor_tensor(out=dst, in0=dst, in1=GN[0:ntok, pi * 256:(pi + 1) * 256], op=ALU.mult),
                         R=[CONSTb], W=[Ab["GOG"][j]])
                for pi in range(4):
                    tok_piece(2064 + pi * 256, 256, lambda j, ntok, kind, ps, psb, pi=pi: ev_og(j, ntok, kind, ps, psb, pi))
                for pi in range(4):
                    wv, wb = wnext(W, 0, KC, 3088 + pi * 256, 256)
                    if K.plan:
                        continue
                    for hh in range(2):
                        h = pi * 2 + hh
                        ps, psb = getS()

                        def fq(e, ps=ps, wv=wv, hh=hh):
                            ins = None
                            for k in range(KC):
                                ins = e.matmul(ps[:, 0:ncols], lhsT=wv[:, k, hh * 128:(hh + 1) * 128], rhs=XT[:, k, 0:ncols], start=(k == 0), stop=(k == KC - 1))
                            return ins
                        K.op("pe", fq, R=[wb] + xtb, W=[psb])
                        K.op("act", lambda e, ps=ps, h=h: e.activation(out=A["QT"][:, h, 0:ncols], in_=ps[:, 0:ncols], func=AF.Copy, scale=128.0 ** -0.5),
                             R=[psb], W=[Ab["QT"]])
            dbg0("m_qb")
            if kv:
                for pi in range(4):
                    wv, wb = wnext(W, 0, KC, 4112 + pi * 256, 256)
                    if K.plan:
                        continue
                    for hh in range(2):
                        h = pi * 2 + hh
                        ps, psb = getS()

                        def fk(e, ps=ps, wv=wv, hh=hh):
                            ins = None
                            for k in range(KC):
                                ins = e.matmul(ps[:, 0:ncols], lhsT=wv[:, k, hh * 128:(hh + 1) * 128], rhs=XT[:, k, 0:ncols], start=(k == 0), stop=(k == KC - 1))
                            return ins
                        K.op("pe", fk, R=[wb] + xtb, W=[psb])
                        if nmain:
                            K.op("dve", lambda e, ps=ps, h=h: e.tensor_copy(out=KT[:, h, 512:512 + nmain], in_=ps[:, 0:nmain]),
                                 R=[psb], W=[KTb[4 + b[0]] for b in main_blocks])
                        if sample_blk is not None:
                            K.op("dve", lambda e, ps=ps, h=h: e.tensor_copy(out=A["KTs"][:, h, :], in_=ps[:, nmain:nmain + 32]), R=[psb], W=[Ab["KTs"]])
                    for (j, ntok, col0, kind) in blocks:
                        if kout_rows.get(j) is not None:
                            ps, psb = proj_tok(wv, wb, KC, col0, ntok, [XTb[j]], 256)
                            dstd = (ks_out if kind == "sample" else kout)[kout_rows[j]:kout_rows[j] + ntok, pi * 256:(pi + 1) * 256]
                            out_rows(ps, psb, ntok, 256, dstd)
                dbg0("m_kb")
                def ev_v(j, ntok, kind, ps, psb, pi):
                    import os as _os4
                    src = ps[0:ntok, 0:256].rearrange("p (h d) -> p h d", d=128)
                    if _os4.environ.get("KDBG_SKIP") == "vcopy":
                        pass
                    else:
                        for hh in range(2):
                            h = pi * 2 + hh
                            s2 = ps[0:ntok, hh * 128:(hh + 1) * 128]
                            if kind == "main":
                                K.op("dve", lambda e, h=h, s2=s2: e.tensor_copy(out=VB[0:ntok, 4 + j, h, 0:128], in_=s2), R=[psb], W=[VBb[4 + j]])
                            else:
                                K.op("dve", lambda e, h=h, s2=s2: e.tensor_copy(out=A["VSN"][0:ntok, h, 0:128], in_=s2), R=[psb], W=[Ab["VSN"]])
                    if kout_rows.get(j) is not None:
                        dstd = (vs_out if kind == "sample" else vout)[kout_rows[j]:kout_rows[j] + ntok, pi * 256:(pi + 1) * 256]
                        out_rows(ps, psb, ntok, 256, dstd)
                for pi in range(4):
                    tok_piece(5136 + pi * 256, 256, lambda j, ntok, kind, ps, psb, pi=pi: ev_v(j, ntok, kind, ps, psb, pi))
                dbg0("m_vb")
                for (j, ntok, col0, kind) in blocks:
                    if kind == "main":
                        K.op("dve", lambda e, j=j: e.memset(VB[:, 4 + j, :, 128:129], 1.0), R=[], W=[VBb[4 + j]])
                        if mode == "prekv":
                            K.op("dve", lambda e, j=j: e.tensor_scalar(out=VB[:, 4 + j, :, 128:129], in0=VB[:, 4 + j, :, 128:129], scalar1=PV[:, 0:1],
                                                                      scalar2=None, op0=ALU.mult), R=[CONSTb], W=[VBb[4 + j]])
                    else:
                        K.op("dve", lambda e: e.memset(A["VSN"][:, :, 128:129], 1.0), R=[], W=[Ab["VSN"]])

        def ring_shift(nt):
            for s in range(4):
                K.op("dve", lambda e, s=s: e.tensor_copy(out=KT[:, :, s * 128:(s + 1) * 128], in_=KT[:, :, (s + nt) * 128:(s + nt + 1) * 128]),
                     R=[KTb[s + nt]], W=[KTb[s]])
                K.op("dve", lambda e, s=s: e.tensor_copy(out=VB[:, s, :, :], in_=VB[:, s + nt, :, :]), R=[VBb[s + nt]], W=[VBb[s]])

        def gla_scr(A, Ab):
            return (A["ENB"], A["EB"], A["KD"], A["QD"], A["QDTf"], A["QDT0"], A["QDT1"], A["KDT"], A["ATm"], A["EBLt"], A["SQ"], Ab)

        class _Stop(Exception):
            pass

        def dbg(tag, j):
            import os as _os2
            if _os2.environ.get("KDBG_STOP") == tag:
                if not K.plan:
                    K.dma("sp", [(y_s, X[j][0:32, :])], Xb[j], write=False)
                    K.finish()
                raise _Stop()

        def dbg0(tag):
            import os as _os3
            if _os3.environ.get("KDBG_STOP") == tag:
                if not K.plan:
                    K.finish()
                raise _Stop()

        def program():
            try:
                program_()
            except _Stop:
                pass

        def program_():
            st.update(iS=0, iT=0, iG=0, wi=0, wissued=0, mp=-1, mpk=0)
            if K.plan:
                st["pieces"] = []
                st["scr_off"] = []
                st["scr_total"] = 0
                st["store_dep"] = {}
            cl = [(BT[:, :, :, :].rearrange("p a h q -> p (a h q)"), c_bt), (BTS3[:, :, :].rearrange("p h q -> p (h q)"), c_bts3),
                  (BTS4[:, :, :].rearrange("p h q -> p (h q)"), c_bts4), (CBIAS[:, :], c_cbias), (TRI2[:, :], c_tri2), (SEL[:, :], c_sel),
                  (MASKT[:, :], c_maskt), (GN[:, :], c_gn), (CW[:, :, :, :].rearrange("p l c t -> p (l c t)"), c_cw),
                  (CBc[:, :, :].rearrange("p l c -> p (l c)"), c_cb), (BSP[:, :], c_bsp), (PV[:, :], c_pv)]
            K.dma("sp", cl, CONSTb, write=True)
            K.dma("pool", [(IDENT[:, :], c_ident), (WGU[:, :], c_wgu), (BG[:, :], c_bg), (ONES[:, :], c_ones)], CONSTPb, write=True)
            K.dma("pool", [(WST[:, :, :].rearrange("p g t -> p (g t)"), c_wst)], WSTb, write=True)
            K.dma("sp", [(SS[:, :, :].rearrange("p h v -> p (h v)"), sg.rearrange("p h v -> p (h v)"))], SSb, write=True)
            K.dma("sp", [(CHS[:, :, :, :].rearrange("p l c t -> p (l c t)"), chs_in)], CHSb, write=True)
            K.op("dve", lambda e: e.memset(S[:, :, :], 0.0), R=[], W=[Sb])
            K.op("dve", lambda e: e.memset(SBf[:, :, :], 0.0), R=[], W=[SBfb])
            K.op("dve", lambda e: e.memset(CH[:, :, :, :], 0.0), R=[], W=[CHb])
            K.op("dve", lambda e: e.memset(KT[:, :, :], 0.0), R=[], W=KTb)
            K.op("dve", lambda e: e.memset(VB[:, :, :, :], 0.0), R=[], W=VBb)
            K.op("act", lambda e: e.activation(out=SSBf[:, :, :].rearrange("p h v -> p (h v)"), in_=SS[:, :, :].rearrange("p h v -> p (h v)"), func=AF.Copy), R=[SSb], W=[SSBfb])

            dbg0("consts")
            A, Ab = carveA()
            pre_passes = []
            t = 0
            first = NPRE % NTM if NPRE % NTM else NTM
            sizes = [first] + [NTM] * ((NPRE - first) // NTM)
            for sz in sizes:
                pre_passes.append(list(range(t, t + sz)))
                t += sz
            import os as _os
            _lim = _os.environ.get("KDBG_PASSES")
            if _lim:
                a_, b_ = [int(v) for v in _lim.split(",")]
                pre_passes = pre_passes[len(pre_passes) - a_:] if a_ else []
            for blks in pre_passes:
                blocks = [(i, 128, i * 128, "main") for i in range(len(blks))]
                ncols = 128 * len(blks)
                kvp = blks[0] >= NPRE - 4
                for (j, ntok, col0, kind), gb in zip(blocks, blks):
                    K.dma("pool", [(XB[:, :], xpre[gb * 128:(gb + 1) * 128, :])], XBb, write=True)
                    to_T(XB, XBb, 128, col0, [XTb[j]])
                mixer0(blocks, ncols, A, Ab, "prekv" if kvp else "pre", {}, None)
                for (j, ntok, col0, kind) in blocks:
                    gla_block(128, 2, A["QKA"][j], Ab["QKA"][j], A["VA"][j], Ab["VA"][j], A["LOGG"][j], Ab["LOGG"][j],
                              S, Sb, [SBf, SBf2], [SBfb, SBf2b], False, None, None, None, None, gla_scr(A, Ab))
                if kvp:
                    ring_shift(len(blks))
            K.op("act", lambda e: e.activation(out=SBf[:, :, :].rearrange("p h v -> p (h v)"), in_=S[:, :, :].rearrange("p h v -> p (h v)"), func=AF.Copy), R=[Sb], W=[SBfb])
            dbg0("init")
            K.barrier()
            dbg0("init2")

            main_passes = []
            t = 0
            while t < NMAIN:
                sz = min(NTM, NMAIN - t)
                main_passes.append(list(range(t, t + sz)))
                t += sz
            if _lim:
                main_passes = main_passes[len(main_passes) - b_:] if b_ else []
            for pidx, blks in enumerate(main_passes):
                st["mp"] = pidx
                st["mpk"] = 0
                last = (pidx == len(main_passes) - 1)
                has_sample = last and len(blks) < NTM
                assert last is False or has_sample
                nmb = len(blks)
                blocks4 = [(i, 128, i * 128, "main") for i in range(nmb)]
                if has_sample:
                    blocks4.append((nmb, 32, nmb * 128, "sample"))
                blocks = [(j, ntok, col0) for (j, ntok, col0, kind) in blocks4]
                ncols = nmb * 128 + (32 if has_sample else 0)
                kout_rows = {}
                for i, gb in enumerate(blks):
                    if gb >= NMAIN - 4:
                        kout_rows[i] = (gb - (NMAIN - 4)) * 128
                if has_sample:
                    kout_rows[nmb] = 0
                A, Ab = carveA()
                for i, gb in enumerate(blks):
                    K.dma("sp", [(X[i][:, :], xm[gb * 128:(gb + 1) * 128, :])], Xb[i], write=True)
                if has_sample:
                    K.dma("sp", [(X[nmb][0:32, :], xs)], Xb[nmb], write=True)
                dbg0("xload")
                cast_T(0, 128, 0)
                dbg0("xT0")
                for (j, ntok, col0) in blocks[1:]:
                    cast_T(j, ntok, col0)
                dbg0("xT")
                mixer0(blocks4, ncols, A, Ab, "main", kout_rows, (blocks4[-1] if has_sample else None))
                dbg0("mixer0")
                for (j, ntok, col0, kind) in blocks4:
                    if kind == "main":
                        gla_block(128, 2, A["QKA"][j], Ab["QKA"][j], A["VA"][j], Ab["VA"][j], A["LOGG"][j], Ab["LOGG"][j],
                                  S, Sb, [SBf, SBf2], [SBfb, SBf2b], True, A["MIX"][j], Ab["MIX"][j], A["GOG"][j], Ab["GOG"][j], gla_scr(A, Ab))
                    else:
                        gla_block(32, 1, A["QKA"][j], Ab["QKA"][j], A["VA"][j], Ab["VA"][j], A["LOGG"][j], Ab["LOGG"][j],
                                  SS, SSb, [SSBf], [SSBfb], True, A["MIX"][j], Ab["MIX"][j], A["GOG"][j], Ab["GOG"][j], gla_scr(A, Ab))
                dbg0("gla")
                hi = 0
                for (j, ntok, col0, kind) in blocks4:
                    for h in range(8):
                        PTt, PTb = A["PT"][hi % 2], Ab["PT"][hi % 2]
                        hi += 1
                        if kind == "main":
                            bl = []
                            for d in range(5):
                                s = j + d
                                bias = None
                                if d == 0:
                                    bias = BT[:, 0, h, :]
                                elif d == 3:
                                    bias = BT[:, 1, h, :]
                                elif d == 4:
                                    bias = BT[:, 2, h, :]
                                bl.append((KT[:, h, s * 128:(s + 1) * 128], KTb[s], VB[:, s, h, 0:129], VBb[s], 128, bias))
                            attn_head(h, 128, A["QT"][:, h, col0:col0 + 128], Ab["QT"], bl, PTt, PTb, A["TMP"], Ab["TMP"],
                                      A["MIX"][j][0:128, 1024 + h * 128:1024 + (h + 1) * 128], Ab["MIX"][j])
                        else:
                            if h == 0:
                                dbg0("a_main")
                            ksh, kshb = A["KSh"][h % 2], Ab["KSh"][h % 2]
                            ktsh, ktshb = A["KTSh"][h % 2], Ab["KTSh"][h % 2]
                            vsh, vshb = A["VSh"][h % 2], Ab["VSh"][h % 2]
                            K.dma("pool", [(ksh[:, :, :], ck[:, h * 128:(h + 1) * 128].rearrange("(b p) d -> p b d", p=128))], kshb, write=True)
                            K.dma("pool", [(vsh[:, :, 0:128], cv[:, h * 128:(h + 1) * 128].rearrange("(b p) d -> p b d", p=128))], vshb, write=True)
                            K.op("dve", lambda e, vsh=vsh: e.memset(vsh[:, :, 128:129], 1.0), R=[], W=[vshb])
                            pt, ptb = getT()

                            def ftk(e, pt=pt, ksh=ksh):
                                ins = None
                                for b in range(4):
                                    ins = e.transpose(out=pt[:, b * 128:(b + 1) * 128], in_=ksh[:, b, :], identity=IDENT[:, :])
                                return ins
                            K.op("pe", ftk, R=[kshb, CONSTPb], W=[ptb])
                            K.op("dve", lambda e, pt=pt, ktsh=ktsh: e.tensor_copy(out=ktsh[:, 0:512], in_=pt[:, 0:512]), R=[ptb], W=[ktshb])
                            if h == 0:
                                dbg0("a_st")
                            bl = []
                            for b in range(4):
                                bias = BTS3[:, h, :] if b == 3 else None
                                bl.append((ktsh[:, b * 128:(b + 1) * 128], ktshb, vsh[:, b, 0:129], vshb, 128, bias))
                            bl.append((A["KTs"][:, h, :], Ab["KTs"], A["VSN"][0:32, h, 0:129], Ab["VSN"], 32, BTS4[:, h, :]))
                            attn_head(h, 32, A["QT"][:, h, col0:col0 + 32], Ab["QT"], bl, PTt, PTb, A["TMP"], Ab["TMP"],
                                      A["MIX"][j][0:32, 1024 + h * 128:1024 + (h + 1) * 128], Ab["MIX"][j], dflag=(h == 0))
                dbg0("attn")
                ring_shift(nmb)
                if last:
                    K.dma("sp", [(gla_out, S[:, :, :].rearrange("p h v -> p (h v)"))], Sb, write=False)
                    K.dma("sp", [(gla_s, SS[:, :, :].rearrange("p h v -> p (h v)"))], SSb, write=False)
                out_proj(w_out_even, lambda j: (A["MIX"][j], Ab["MIX"][j]), blocks)
                if has_sample:
                    dbg("preln1", nmb)
                load_ln(0)
                for (j, ntok, col0) in blocks:
                    layer_norm(X[j], Xb[j], ntok)
                if has_sample:
                    dbg("x1", nmb)
                K.barrier()
                segs = []
                if nmb:
                    segs.append((0, nmb * 128, CH, CHb))
                if has_sample:
                    segs.append((nmb * 128, 32, CHS, CHSb))
                ffn(0, blocks, segs, ncols)
                if has_sample:
                    dbg("preln2", nmb)
                load_ln(2)
                for (j, ntok, col0) in blocks:
                    layer_norm(X[j], Xb[j], ntok)
                K.barrier()
                U1 = [SCRB[:, i * 2048:(i + 1) * 2048] for i in range(NTM)]
                VN = [SCRB[:, (NTM + i) * 2048:(NTM + i + 1) * 2048] for i in range(NTM)]
                V1 = [SCRF[:, i * 2048:(i + 1) * 2048] for i in range(NTM)]
                U1b = [Buf("U1%d" % i) for i in range(NTM)]
                VNb = [Buf("VN%d" % i) for i in range(NTM)]
                V1b = [Buf("V1%d" % i) for i in range(NTM)]
                for (j, ntok, col0) in blocks:
                    cast_T(j, ntok, col0)
                for pi in range(16):
                    wv, wb = wnext(w_in_odd, 0, KC, pi * 256, 256)
                    if K.plan:
                        continue
                    for (j, ntok, col0) in blocks:
                        ps, psb = proj_tok(wv, wb, KC, col0, ntok, [XTb[j]], 256)
                        if pi < 8:
                            K.op("act", lambda e, ps=ps, j=j, ntok=ntok, pi=pi: e.activation(
                                out=U1[j][0:ntok, pi * 256:(pi + 1) * 256], in_=ps[0:ntok, 0:256], func=AF.Gelu_apprx_tanh), R=[psb], W=[U1b[j]])
                        else:
                            K.op("act", lambda e, ps=ps, j=j, ntok=ntok, pi=pi: e.activation(
                                out=V1[j][0:ntok, (pi - 8) * 256:(pi - 7) * 256], in_=ps[0:ntok, 0:256], func=AF.Gelu_apprx_tanh), R=[psb], W=[V1b[j]])
                load_ln(8)
                for (j, ntok, col0, kind) in blocks4:
                    layer_norm(V1[j], V1b[j], ntok)
                    K.op("act", lambda e, j=j, ntok=ntok: e.activation(func=AF.Copy, out=VN[j][0:ntok, :], in_=V1[j][0:ntok, :]), R=[V1b[j]], W=[VNb[j]])
                    if kind == "sample":
                        for q in range(8):
                            sg_, sgb = getSTG()
                            K.op("dve", lambda e, sg_=sg_, q=q, j=j: e.tensor_copy(out=sg_[0:32, :], in_=V1[j][0:32, q * 256:(q + 1) * 256]), R=[V1b[j]], W=[sgb])
                            K.dma("sp", [(mlpv_s[:, q * 256:(q + 1) * 256], sg_[0:32, :])], sgb, write=False)
                    for gp in range(4):
                        ps, psb = getS()

                        def fsp(e, ps=ps, gp=gp, j=j, ntok=ntok):
                            ins = None
                            for gg in range(2):
                                g = gp * 2 + gg
                                ins = e.matmul(ps[0:ntok, gg * 256:(gg + 1) * 256], lhsT=WST[0:ntok, g, 0:ntok], rhs=VN[j][0:ntok, g * 256:(g + 1) * 256],
                                               start=True, stop=True)
                            return ins
                        K.op("pe", fsp, R=[WSTb, VNb[j]], W=[psb])
                        for gg in range(2):
                            g = gp * 2 + gg
                            K.op("dve", lambda e, ps=ps, g=g, gg=gg, j=j, ntok=ntok: e.scalar_tensor_tensor(
                                out=U1[j][0:ntok, g * 256:(g + 1) * 256], in0=ps[0:ntok, gg * 256:(gg + 1) * 256], scalar=BSP[0:ntok, g:g + 1],
                                in1=U1[j][0:ntok, g * 256:(g + 1) * 256], op0=ALU.add, op1=ALU.mult), R=[psb, CONSTb], W=[U1b[j]])
                out_proj(w_out_odd, lambda j: (U1[j], U1b[j]), blocks)
                load_ln(4)
                for (j, ntok, col0) in blocks:
                    layer_norm(X[j], Xb[j], ntok)
                K.barrier()
                ffn(1, blocks, segs, ncols)
                load_ln(6)
                for (j, ntok, col0, kind) in blocks4:
                    layer_norm(X[j], Xb[j], ntok)
                    if kind == "main":
                        gb = blks[j]
                        K.dma("sp", [(y_main[gb * 128:(gb + 1) * 128, :], X[j][:, :])], Xb[j], write=False)
                    else:
                        K.dma("sp", [(y_s, X[j][0:32, :])], Xb[j], write=False)
                K.barrier()
            K.dma("sp", [(conv_out, CH[:, :, :, :].rearrange("p l c t -> p (l c t)"))], CHb, write=False)
            K.dma("sp", [(conv_s, CHS[:, :, :, :].rearrange("p l c t -> p (l c t)"))], CHSb, write=False)
            K.finish()

        K.plan = True
        program()
        K.plan = False
        K.reset()
        st["scratch"] = nc.dram_tensor("wscratch", [128, max(1, st["scr_total"])], BF16, kind="Internal").ap()
        program()
        K.emit()
    return nc


_NC_CACHE = {}


def _rel_tiles(table):
    H = table.shape[0]
    k = np.arange(128)[:, None]
    q = np.arange(128)[None, :]
    bt = np.zeros((128, 3, H, 128), np.float32)
    for a, d in enumerate((0, 3, 4)):
        kp = (d - 4) * 128 + k
        rel = np.clip(q - kp, -128, 128) + 128
        vis = (np.floor_divide(kp, 64) >= np.floor_divide(q, 64) - 8) & (np.floor_divide(kp, 64) <= np.floor_divide(q, 64))
        for h in range(H):
            bt[:, a, h, :] = np.where(vis, table[h][rel], np.float32(-30000.0))
    t = np.arange(32)[None, :]
    i = np.arange(128)[:, None]
    bts3 = np.stack([table[h][np.clip(t + 128 - i, -128, 128) + 128] for h in range(H)], axis=1).astype(np.float32)
    s = np.arange(32)[:, None]
    bts4 = np.stack([table[h][np.clip(t - s, -128, 128) + 128] for h in range(H)], axis=1).astype(np.float32)
    cbias = np.broadcast_to(table[:, 256][None, :], (128, H)).astype(np.float32)
    return bt, bts3, bts4, cbias


def _prep(x_prompt, x_sample, cache_attn_k, cache_attn_v, state_gla, state_ffn_conv,
          w_in_even, w_gate_up, b_gate, gla_norm_g, rel_bias, w_out_even,
          w_in_odd, ln_v_g, ln_v_b, w_spatial, b_spatial, w_out_odd,
          ffn_w1, ffn_w2, ffn_conv_w, ffn_conv_b, ffn_w3,
          ln1_g, ln1_b, ln2_g, ln2_b):
    f = lambda a: np.ascontiguousarray(np.asarray(a, dtype=np.float32))
    x_prompt = f(x_prompt); x_sample = f(x_sample)

    s_ = np.arange(128)[:, None]
    t_ = np.arange(128)[None, :]
    same = (s_ // 64) == (t_ // 64)
    causal = s_ <= t_
    tri2 = np.where(same & causal, np.float32(-1.0 / 16.0), np.float32(0.0)).astype(np.float32)
    sel = np.where((s_ // 64) == np.arange(2)[None, :], np.float32(-1.0 / 16.0), np.float32(0.0)).astype(np.float32)
    maskt = np.where(same & causal, np.float32(1.0), np.float32(0.0)).astype(np.float32)
    bt, bts3, bts4, cbias = _rel_tiles(f(rel_bias)[0])
    lnp = np.stack([np.broadcast_to(f(v)[None, :], (128, D)) for v in
                    (ln1_g[0], ln1_b[0], ln2_g[0], ln2_b[0], ln1_g[1], ln1_b[1], ln2_g[1], ln2_b[1], ln_v_g[0], ln_v_b[0])]).astype(np.float32)
    gn = np.ascontiguousarray(np.broadcast_to(f(gla_norm_g)[0].reshape(1, 1024), (128, 1024)))
    cw = np.ascontiguousarray(f(ffn_conv_w).reshape(2, 3, FC, 128).transpose(3, 0, 2, 1)).reshape(128, 2 * FC * 3)
    cb = np.ascontiguousarray(f(ffn_conv_b).reshape(2, FC, 128).transpose(2, 0, 1)).reshape(128, 2 * FC)
    ws = f(w_spatial)[0]
    wst = np.where(causal[:, None, :], ws.transpose(2, 0, 1), np.float32(0.0)).astype(np.float32)
    wst = np.ascontiguousarray(wst).reshape(128, 8 * 128)
    bsp = np.ascontiguousarray(f(b_spatial)[0].T)
    common = dict(
        w_in_even=f(w_in_even)[0], w_out_even=f(w_out_even)[0], w_in_odd=f(w_in_odd)[0], w_out_odd=f(w_out_odd)[0],
        ffn_w1=f(ffn_w1), ffn_w2=f(ffn_w2), ffn_w3=f(ffn_w3), lnp=lnp,
        c_ident=np.eye(128, dtype=np.float32), c_tri2=tri2, c_sel=sel, c_maskt=maskt, c_gn=gn,
        c_wgu=f(w_gate_up)[0], c_bg=f(b_gate)[0].reshape(1, 512), c_ones=np.ones((1, 128), np.float32),
        c_cw=cw, c_cb=cb, c_wst=wst, c_bsp=bsp, c_bt=bt.reshape(128, -1), c_bts3=bts3.reshape(128, -1),
        c_bts4=bts4.reshape(32, -1), c_cbias=cbias)
    in_maps = []
    SPLIT = NMAIN * 128
    START1 = 4096 - SPLIT
    for c in range(8):
        b, half = c // 2, c % 2
        if half == 0:
            xm = x_prompt[b, 0:SPLIT]
            xpre = np.zeros((NPRE * 128, D), np.float32)
            pv = np.zeros((128, 1), np.float32)
        else:
            xm = x_prompt[b, START1:4096]
            xpre = x_prompt[b, 0:START1]
            pv = np.ones((128, 1), np.float32)
        m = dict(common)
        m.update(xm=np.ascontiguousarray(xm), xpre=np.ascontiguousarray(xpre), c_pv=pv, xs=x_sample[c],
                 ck=f(cache_attn_k)[0, c].reshape(512, 1024), cv=f(cache_attn_v)[0, c].reshape(512, 1024),
                 sg=np.ascontiguousarray(f(state_gla)[0, c].transpose(1, 0, 2)),
                 chs=np.ascontiguousarray(f(state_ffn_conv)[:, c].reshape(2, 2, FC, 128).transpose(3, 0, 2, 1)).reshape(128, 2 * FC * 2))
        in_maps.append(m)
    return in_maps


def kernel(x_prompt, x_sample, cache_attn_k, cache_attn_v, state_gla, state_ffn_conv,
           w_in_even, w_gate_up, b_gate, gla_norm_g, rel_bias, w_out_even,
           w_in_odd, ln_v_g, ln_v_b, w_spatial, b_spatial, w_out_odd,
           ffn_w1, ffn_w2, ffn_conv_w, ffn_conv_b, ffn_w3,
           ln1_g, ln1_b, ln2_g, ln2_b):
    in_maps = _prep(x_prompt, x_sample, cache_attn_k, cache_attn_v, state_gla, state_ffn_conv,
                    w_in_even, w_gate_up, b_gate, gla_norm_g, rel_bias, w_out_even,
                    w_in_odd, ln_v_g, ln_v_b, w_spatial, b_spatial, w_out_odd,
                    ffn_w1, ffn_w2, ffn_conv_w, ffn_conv_b, ffn_w3,
                    ln1_g, ln1_b, ln2_g, ln2_b)
    if "nc" not in _NC_CACHE:
        _NC_CACHE["nc"] = build()
    nc = _NC_CACHE["nc"]
    SPLIT = NMAIN * 128
    START1 = 4096 - SPLIT
    res = run_bass_kernel_spmd(nc, in_maps, core_ids=list(range(8)))
    R = res.results
    B = 4
    y_prompt = np.zeros((B, 4096, D), np.float32)
    k_p = np.zeros((1, B, 512, 8, 128), np.float32)
    v_p = np.zeros((1, B, 512, 8, 128), np.float32)
    gla_p = np.zeros((1, B, 4, 128, 256), np.float32)
    conv_p = np.zeros((2, B, 2, DFF), np.float32)
    y_sample = np.zeros((8, 32, D), np.float32)
    k_s = np.zeros((1, 8, 32, 8, 128), np.float32)
    v_s = np.zeros((1, 8, 32, 8, 128), np.float32)
    gla_sm = np.zeros((1, 8, 4, 128, 256), np.float32)
    conv_sm = np.zeros((2, 8, 2, DFF), np.float32)
    mlp_v = np.zeros((1, 8, 32, D), np.float32)

    def conv_unlay(a):
        return a.reshape(128, 2, FC, 2).transpose(1, 3, 2, 0).reshape(2, 2, DFF)
    for c in range(8):
        b, half = c // 2, c % 2
        r = R[c]
        if half == 0:
            y_prompt[b, 0:SPLIT] = r["y_main"]
        else:
            y_prompt[b, SPLIT:4096] = r["y_main"][SPLIT - START1:]
            k_p[0, b] = r["kout"].reshape(512, 8, 128)
            v_p[0, b] = r["vout"].reshape(512, 8, 128)
            gla_p[0, b] = r["gla_out"].reshape(128, 4, 256).transpose(1, 0, 2)
            conv_p[:, b] = conv_unlay(r["conv_out"])
        y_sample[c] = r["y_s"]
        k_s[0, c] = r["ks_out"].reshape(32, 8, 128)
        v_s[0, c] = r["vs_out"].reshape(32, 8, 128)
        gla_sm[0, c] = r["gla_s"].reshape(128, 4, 256).transpose(1, 0, 2)
        conv_sm[:, c] = conv_unlay(r["conv_s"])
        mlp_v[0, c] = r["mlpv_s"]
    return (y_prompt, y_sample, k_p, v_p, gla_p, conv_p, k_s, v_s, gla_sm, conv_sm, mlp_v)
```
